# Optimizing a Trainium2 kernel written in Bass

```python
import math
import jax, jax.numpy as jnp
from jax import lax
import numpy as np

D_MODEL = 1024
BATCH = 32
SEQ = 256
DEPTH = 4
DEC_BATCH = 2
DEC_SEQ = 4096
PAST_LEN = 256

GRID_W = 64
D_MIX = D_MODEL
D_A = D_MIX // 2
D_B = D_MIX - D_A
H_A = 4
DK = D_A // H_A
DV = D_A // H_A
G_B = 4
C_B = D_B // G_B
CHUNK_MLP = 128
ROWS_PER_CHUNK = CHUNK_MLP // GRID_W
DN_CHUNK = 64
CONV_W = 3
D_FF = 4 * D_MODEL
N_MOD = 6
PROJ = 4 * D_A + 4 * H_A + 2 * D_B
EPS = 1e-6

kernel_name = 'hybrid_deltanet_gmlp_dit_step'


def rmsnorm(x, w):
    xf = x.astype(jnp.float32)
    y = xf * lax.rsqrt(jnp.mean(xf * xf, axis=-1, keepdims=True) + EPS)
    return (y * w).astype(x.dtype)


def layernorm_noaffine(x):
    xf = x.astype(jnp.float32)
    mu = jnp.mean(xf, axis=-1, keepdims=True)
    var = jnp.mean(jnp.square(xf - mu), axis=-1, keepdims=True)
    return (xf - mu) * lax.rsqrt(var + EPS)


def l2norm(x):
    xf = x.astype(jnp.float32)
    return xf * lax.rsqrt(jnp.sum(xf * xf, axis=-1, keepdims=True) + EPS)


def modulation(cond, w_mod_l, b_mod_l):
    m = jax.nn.silu(cond) @ w_mod_l + b_mod_l
    return jnp.split(m[:, None, :], N_MOD, axis=-1)


def centred_conv(x, w):
    pad = CONV_W // 2
    return lax.conv_general_dilated(x, w[:, None, :], window_strides=(1,), padding=[(pad, pad)],
                                    dimension_numbers=('NWC', 'WIO', 'NWC'),
                                    feature_group_count=x.shape[-1])


def gated_delta_chunked(q, k, v, g, beta, s0):
    out_dtype = v.dtype
    B, T, H, _ = q.shape
    n = T // DN_CHUNK
    f32 = jnp.float32

    def to_chunks(a):
        a = a.astype(f32).reshape((B, n, DN_CHUNK, H) + a.shape[3:])
        return jnp.swapaxes(a, 2, 3)

    q = to_chunks(q) * (DK ** -0.5)
    k = to_chunks(k)
    v = to_chunks(v)
    g = jnp.cumsum(to_chunks(g), axis=-1)
    beta = to_chunks(beta)[..., None]
    k_beta = k * beta
    incl = jnp.tril(jnp.ones((DN_CHUNK, DN_CHUNK), dtype=bool))
    strict = jnp.tril(jnp.ones((DN_CHUNK, DN_CHUNK), dtype=bool), -1)
    diff = g[..., :, None] - g[..., None, :]
    decay = jnp.where(incl, jnp.exp(jnp.where(incl, diff, 0.0)), 0.0)
    a_mat = jnp.where(strict, jnp.einsum('bnhcd,bnhsd->bnhcs', k_beta, k) * decay, 0.0)
    rhs = jnp.concatenate([v * beta, k_beta * jnp.exp(g)[..., None]], axis=-1)
    sol = lax.linalg.triangular_solve(a_mat + jnp.eye(DN_CHUNK, dtype=f32), rhs,
                                      left_side=True, lower=True, unit_diagonal=True)
    u, w = sol[..., :DV], sol[..., DV:]
    qk = jnp.einsum('bnhcd,bnhsd->bnhcs', q, k) * decay
    q_dec = q * jnp.exp(g)[..., None]
    g_last = g[..., -1]
    k_dec = k * jnp.exp(g_last[..., None] - g)[..., None]

    def step(S, xs):
        qd, kd, u_c, w_c, qk_c, gl = xs
        v_new = u_c - jnp.einsum('bhck,bhkv->bhcv', w_c, S)
        o = jnp.einsum('bhck,bhkv->bhcv', qd, S) + jnp.einsum('bhcs,bhsv->bhcv', qk_c, v_new)
        S = S * jnp.exp(gl)[..., None, None] + jnp.einsum('bhck,bhcv->bhkv', kd, v_new)
        return S, o

    xs = tuple(jnp.moveaxis(a, 1, 0) for a in (q_dec, k_dec, u, w, qk, g_last))
    s_fin, o = lax.scan(step, s0.astype(f32), xs)
    o = jnp.transpose(o, (1, 0, 3, 2, 4)).reshape(B, T, H, DV)
    return o.astype(out_dtype), s_fin.astype(s0.dtype)


def delta_bidir(q, k, v, g2, beta2, s0):
    o_f, s_f = gated_delta_chunked(q, k, v, g2[:, :, 0], beta2[:, :, 0], s0[:, 0])
    flip = lambda a: a[:, ::-1]
    o_b, s_b = gated_delta_chunked(flip(q), flip(k), flip(v), flip(g2[:, :, 1]),
                                   flip(beta2[:, :, 1]), s0[:, 1])
    return o_f + flip(o_b), jnp.stack([s_f, s_b], axis=1)


def mixer(h, s0, n_chunks, w_in_l, conv_l, a_log_l, dt_bias_l, o_norm_l, ws_l, bs_l, w_out_l):
    B, T, _ = h.shape
    proj = h @ w_in_l
    cuts = [3 * D_A, 4 * D_A, 4 * D_A + 4 * H_A, 4 * D_A + 4 * H_A + D_B]
    qkv, gate, ab, u, vg = jnp.split(proj, cuts, axis=-1)
    qkv = jax.nn.silu(centred_conv(qkv, conv_l))
    q, k, v = jnp.split(qkv, 3, axis=-1)
    q = l2norm(q.reshape(B, T, H_A, DK))
    k = l2norm(k.reshape(B, T, H_A, DK))
    v = v.reshape(B, T, H_A, DV)
    ab = ab.reshape(B, T, 2, 2, H_A)
    beta2 = jax.nn.sigmoid(ab[:, :, 0].astype(jnp.float32))
    g2 = -jnp.exp(a_log_l.astype(jnp.float32)) * jax.nn.softplus(
        ab[:, :, 1].astype(jnp.float32) + dt_bias_l.astype(jnp.float32))
    o_a, s_new = delta_bidir(q, k, v, g2, beta2, s0)
    o_a = rmsnorm(o_a, o_norm_l) * jax.nn.silu(gate.reshape(B, T, H_A, DV))
    u = jax.nn.gelu(u)
    vg = layernorm_noaffine(jax.nn.gelu(vg).reshape(B, n_chunks, CHUNK_MLP, G_B, C_B))
    mixed = jnp.einsum('gts,bnsgc->bntgc', ws_l, vg) + bs_l.T[None, None, :, :, None]
    o_b = u * mixed.reshape(B, T, D_B).astype(u.dtype)
    out = jnp.concatenate([o_a.reshape(B, T, D_A).astype(h.dtype), o_b.astype(h.dtype)], axis=-1)
    return out @ w_out_l, s_new


def block(x, s0, mod, n_chunks, norm_mix_l, w_in_l, conv_l, a_log_l, dt_bias_l, o_norm_l,
          ws_l, bs_l, w_out_l, norm_ffn_l, w_ff1_l, w_ff2_l):
    sh1, sc1, ga1, sh2, sc2, ga2 = mod
    h = rmsnorm(x, norm_mix_l) * (1.0 + sc1) + sh1
    o, s_new = mixer(h, s0, n_chunks, w_in_l, conv_l, a_log_l, dt_bias_l, o_norm_l, ws_l, bs_l, w_out_l)
    x = x + ga1 * o
    h = rmsnorm(x, norm_ffn_l) * (1.0 + sc2) + sh2
    x = x + ga2 * (jnp.square(jax.nn.relu(h @ w_ff1_l)) @ w_ff2_l)
    return x, s_new


def setup_inputs(seed: int = 0) -> dict:
    key = jax.random.key(seed)
    ks = jax.random.split(key, 24)
    f32 = jnp.float32
    nrm = lambda k, shape, s: jax.random.normal(k, shape, f32) * s
    x_prompt = nrm(ks[0], (BATCH, SEQ, D_MODEL), 1.0)
    x_sample = nrm(ks[1], (DEC_BATCH, DEC_SEQ, D_MODEL), 1.0)
    state_delta = nrm(ks[2], (DEC_BATCH, DEPTH, 2, H_A, DK, DV), 0.2)
    c = nrm(ks[3], (DEC_BATCH, D_MODEL), 1.0)
    c_ctx = nrm(ks[4], (D_MODEL,), 1.0)
    w_mod = nrm(ks[5], (DEPTH, D_MODEL, N_MOD * D_MODEL), 0.5 * D_MODEL ** -0.5)
    b_mod = nrm(ks[6], (DEPTH, N_MOD * D_MODEL), 0.02)
    norm_mix = 1.0 + nrm(ks[7], (DEPTH, D_MODEL), 0.02)
    w_in = nrm(ks[8], (DEPTH, D_MODEL, PROJ), D_MODEL ** -0.5)
    conv_qkv = nrm(ks[9], (DEPTH, CONV_W, 3 * D_A), CONV_W ** -0.5)
    a_log = jnp.log(jax.random.uniform(ks[10], (DEPTH, 2, H_A), f32, 1.0, 16.0))
    dt = jnp.exp(jax.random.uniform(ks[11], (DEPTH, 2, H_A), f32, math.log(1e-3), math.log(1e-1)))
    dt_bias = dt + jnp.log(-jnp.expm1(-dt))
    o_norm = 1.0 + nrm(ks[12], (DEPTH, DV), 0.02)
    w_spatial = nrm(ks[13], (DEPTH, G_B, CHUNK_MLP, CHUNK_MLP), CHUNK_MLP ** -0.5)
    b_spatial = 1.0 + nrm(ks[14], (DEPTH, G_B, CHUNK_MLP), 0.02)
    w_out = nrm(ks[15], (DEPTH, D_MIX, D_MODEL), D_MIX ** -0.5)
    norm_ffn = 1.0 + nrm(ks[16], (DEPTH, D_MODEL), 0.02)
    w_ff1 = nrm(ks[17], (DEPTH, D_MODEL, D_FF), D_MODEL ** -0.5)
    w_ff2 = nrm(ks[18], (DEPTH, D_FF, D_MODEL), D_FF ** -0.5)
    norm_final = 1.0 + nrm(ks[19], (D_MODEL,), 0.02)
    return {'x_prompt': x_prompt, 'x_sample': x_sample, 'state_delta': state_delta,
            'c': c, 'c_ctx': c_ctx, 'w_mod': w_mod, 'b_mod': b_mod, 'norm_mix': norm_mix,
            'w_in': w_in, 'conv_qkv': conv_qkv, 'a_log': a_log, 'dt_bias': dt_bias,
            'o_norm': o_norm, 'w_spatial': w_spatial, 'b_spatial': b_spatial, 'w_out': w_out,
            'norm_ffn': norm_ffn, 'w_ff1': w_ff1, 'w_ff2': w_ff2, 'norm_final': norm_final}


def reference(x_prompt, x_sample, state_delta, c, c_ctx, w_mod, b_mod, norm_mix, w_in, conv_qkv,
              a_log, dt_bias, o_norm, w_spatial, b_spatial, w_out, norm_ffn, w_ff1, w_ff2,
              norm_final):
    x = x_prompt
    n_ctx_chunks = x.shape[1] // CHUNK_MLP
    zero_state = jnp.zeros((x.shape[0], 2, H_A, DK, DV), x.dtype)
    y = x_sample
    rows = y.shape[1] // GRID_W
    n_lat_chunks = rows // ROWS_PER_CHUNK
    new_states = []
    for l in range(DEPTH):
        p = (norm_mix[l], w_in[l], conv_qkv[l], a_log[l], dt_bias[l], o_norm[l],
             w_spatial[l], b_spatial[l], w_out[l], norm_ffn[l], w_ff1[l], w_ff2[l])
        mod_ctx = modulation(c_ctx[None, :], w_mod[l], b_mod[l])
        x, s_ctx = block(x, zero_state, mod_ctx, n_ctx_chunks, *p)
        new_states.append(s_ctx)
        mod_lat = modulation(c, w_mod[l], b_mod[l])
        y, _ = block(y, state_delta[:, l], mod_lat, n_lat_chunks, *p)
    y_prompt = rmsnorm(x, norm_final)
    y_sample = rmsnorm(y, norm_final)
    new_state_delta = jnp.stack(new_states, axis=1)
    return (y_prompt, y_sample, new_state_delta)
```

```python
import contextlib
import numpy as np
import concourse.bass as bass
import concourse.mybir as mybir
from concourse.bass_utils import run_bass_kernel_spmd

F32 = mybir.dt.float32
BF16 = mybir.dt.bfloat16
AF = mybir.ActivationFunctionType
ALU = mybir.AluOpType
AX = mybir.AxisListType

NT = 5120
NTL = NT // 128
EPS = 1e-6
LNSCALE = -0.5 * 4.852030263919617
SEQS = [(0, 256, 0), (256, 256, 0), (512, 256, 0), (768, 256, 0), (1024, 4096, 1)]
C_ID, C_ONES, C_UF, C_UB, C_SAME, C_CH0, C_CH1, C_NSTRF, C_NSTRB, C_INTF, C_INTB = range(11)
C_MB = 11
C_NEGONES = 17
NCONST = 18


class Buf:
    __slots__ = ('name', 'w', 'r', 'excl')

    def __init__(s, name=''):
        s.name = name
        s.w = None
        s.r = {}
        s.excl = False


class T:
    def __init__(s, t, name=''):
        s.t = t
        s.b = Buf(name)

    def __getitem__(s, k):
        return s.t[k]


def _b(x):
    return x.b if isinstance(x, T) else x


class FW:
    EPOCH = 8000
    NSW = 8
    NHW = 16
    MAXEP = 10
    SAME = ('act', 'dve', 'pool')

    def __init__(s, nc):
        s.nc = nc
        s.eng = {'pe': nc.tensor, 'act': nc.scalar, 'dve': nc.vector, 'pool': nc.gpsimd, 'sp': nc.sync}
        s.ops = {e: [] for e in s.eng}
        s.cnt = {e: 0 for e in s.eng}
        s.epoch = {e: 0 for e in s.eng}
        s.seen = {e: {} for e in s.eng}
        s.sems = {}
        for e in s.eng:
            for ep in range(s.MAXEP):
                k = (e, ep)
                s.sems[k] = nc.alloc_semaphore(name='s_%s_%d' % k)
        for cls, n in (('sw', s.NSW), ('hw', s.NHW)):
            for i in range(n):
                k = ('dma' + cls, i)
                s.sems[k] = nc.alloc_semaphore(name='s_%s_%d' % k)
        s.dma_i = {'sw': 0, 'hw': 0}
        s.dma_use = {'sw': [0] * s.NSW, 'hw': [0] * s.NHW}
        s.nops = 0

    def _deps(s, eng, reads, writes):
        need = {}

        def add(ev):
            if ev is None:
                return
            k, v = ev
            if need.get(k, 0) < v:
                need[k] = v
        for b in reads:
            b = _b(b)
            add(b.w)
            if b.excl:
                for k, v in b.r.items():
                    if k[0] != eng:
                        add((k, v))
        for b in writes:
            b = _b(b)
            add(b.w)
            for k, v in b.r.items():
                add((k, v))
        waits = []
        for k, v in need.items():
            if k[0] == eng and eng not in s.SAME:
                continue
            if s.seen[eng].get(k, 0) < v:
                s.seen[eng][k] = v
                waits.append((k, v))
        return waits

    def _commit(s, ev, reads, writes):
        k, v = ev
        for b in writes:
            b = _b(b)
            b.w = ev
            b.r = {}
        wset = set(id(_b(b)) for b in writes)
        for b in reads:
            b = _b(b)
            if id(b) in wset:
                continue
            if b.r.get(k, 0) < v:
                b.r[k] = v

    def op(s, eng, fn, reads=(), writes=()):
        waits = s._deps(eng, reads, writes)
        if s.cnt[eng] >= s.EPOCH:
            s.epoch[eng] += 1
            s.cnt[eng] = 0
        s.cnt[eng] += 1
        key = (eng, s.epoch[eng])
        val = s.cnt[eng]
        s.ops[eng].append((waits, fn, key, 1))
        s.nops += 1
        s._commit((key, val), reads, writes)

    def dma(s, q, out, in_, r=(), w=(), **kw):
        cls = 'sw' if q == 'pool' else 'hw'
        n = s.NSW if cls == 'sw' else s.NHW
        slot = s.dma_i[cls] % n
        s.dma_i[cls] += 1
        prev = s.dma_use[cls][slot]
        key = ('dma' + cls, slot)
        waits = s._deps(q, r, w)
        if prev > 0 and s.seen[q].get(key, 0) < 16 * prev:
            s.seen[q][key] = 16 * prev
            waits.append((key, 16 * prev))
        s.dma_use[cls][slot] = prev + 1
        s.ops[q].append((waits, lambda e: e.dma_start(out=out, in_=in_, **kw), key, 16))
        s.nops += 1
        s._commit((key, 16 * (prev + 1)), r, w)

    def barrier(s):
        evs = []
        for e in s.eng:
            for ep in range(s.epoch[e] + 1):
                c = s.cnt[e] if ep == s.epoch[e] else s.EPOCH
                if c > 0:
                    evs.append(((e, ep), c))
        for cls in ('sw', 'hw'):
            for slot in range(len(s.dma_use[cls])):
                if s.dma_use[cls][slot] > 0:
                    evs.append((('dma' + cls, slot), 16 * s.dma_use[cls][slot]))
        for e in s.eng:
            waits = []
            for k, v in evs:
                if s.seen[e].get(k, 0) < v:
                    s.seen[e][k] = v
                    waits.append((k, v))
            if waits:
                s.ops[e].append((waits, None, None, 0))

    def emit(s):
        s.barrier()
        nc = s.nc
        keys = set()
        for e in s.ops:
            for (w, f, k, i) in s.ops[e]:
                if k is not None:
                    keys.add(k)
                for (kk, v) in w:
                    keys.add(kk)
        for k in sorted(keys, key=str):
            if k not in s.sems:
                s.sems[k] = nc.alloc_semaphore(name='s_' + '_'.join(str(x) for x in k))
        ops = s.ops
        s.ops = {e: [] for e in s.eng}
        with nc.Block() as block:
            def mk(ename):
                def body(e):
                    for waits, fn, key, inc in ops[ename]:
                        for k, v in waits:
                            e.wait_ge(s.sems[k], v)
                        if fn is not None:
                            fn(e).then_inc(s.sems[key], inc)
                return body
            block.tensor(mk('pe'))
            block.scalar(mk('act'))
            block.vector(mk('dve'))
            block.gpsimd(mk('pool'))
            block.sync(mk('sp'))

    def mm(s, out, lhsT, rhs, start=True, stop=True, r=(), w=()):
        s.op('pe', lambda e: e.matmul(out, lhsT, rhs, start=start, stop=stop), r, w)

    def tr(s, out, in_, ident, r=(), w=()):
        s.op('pe', lambda e: e.transpose(out, in_, ident), r, w)

    def act(s, out, in_, func, bias=0.0, scale=1.0, r=(), w=()):
        s.op('act', lambda e: e.activation(out, in_, func, bias=bias, scale=scale), r, w)

    def tt(s, eng, out, in0, in1, op, r=(), w=()):
        s.op(eng, lambda e: e.tensor_tensor(out, in0, in1, op), r, w)

    def ts(s, eng, out, in0, s1, s2, op0, op1=None, r=(), w=()):
        if op1 is None:
            s.op(eng, lambda e: e.tensor_scalar(out, in0, s1, None, op0), r, w)
        else:
            s.op(eng, lambda e: e.tensor_scalar(out, in0, s1, s2, op0, op1), r, w)

    def stt(s, eng, out, in0, scalar, in1, op0, op1, r=(), w=()):
        s.op(eng, lambda e: e.scalar_tensor_tensor(out, in0, scalar, in1, op0, op1), r, w)

    def cp(s, eng, out, in_, r=(), w=()):
        if eng == 'act':
            s.op(eng, lambda e: e.copy(out, in_), r, w)
        else:
            s.op(eng, lambda e: e.tensor_copy(out, in_), r, w)

    def memset(s, eng, ap, val, w=()):
        s.op(eng, lambda e: e.memset(ap, val), (), w)

    def recip(s, out, in_, r=(), w=()):
        s.op('dve', lambda e: e.reciprocal(out, in_), r, w)


def build(L=4, dbg=False, stop=None):
    nc = bass.Bass("TRN2", target_bir_lowering=False)
    fw = FW(nc)

    def din(name, shape, dt=F32):
        return nc.dram_tensor(name, list(shape), dt, kind="ExternalInput")

    def dscr(name, shape, dt=F32):
        return nc.dram_tensor(name, list(shape), dt, kind=("ExternalOutput" if dbg else "Internal"))

    xin = din("xin", [NT, 1024]); cond = din("cond", [2, 1024]); s0 = din("s0", [4, 2, 4, 128, 128])
    w_mod = din("w_mod", [4, 1024, 6144]); b_mod = din("b_mod", [4, 6144]); norm_mix = din("norm_mix", [4, 1024])
    w_in = din("w_in", [4, 1024, 3088]); conv_qkv = din("conv_qkv", [4, 3, 1536]); a_log = din("a_log", [4, 8])
    dt_bias = din("dt_bias", [4, 8]); o_norm = din("o_norm", [4, 128]); wsT = din("wsT", [4, 4, 128, 128])
    b_sp = din("b_sp", [4, 4, 128]); w_out = din("w_out", [4, 1024, 1024]); norm_ffn = din("norm_ffn", [4, 1024])
    w_ff1b = din("w_ff1b", [4, 32, 128, 1024]); w_ff2b = din("w_ff2b", [4, 8, 128, 4096])
    norm_final = din("norm_final", [1024]); consts = din("consts", [NCONST, 128, 128])
    y = nc.dram_tensor("y", [NT, 1024], F32, kind="ExternalOutput")
    ns = nc.dram_tensor("ns", [4, 4, 2, 4, 128, 128], F32, kind="ExternalOutput")
    xs = dscr("xs", [1024, NT]); qkvs = dscr("qkvs", [1536, NT]); gates = dscr("gates", [512, NT])
    obs = dscr("obs", [512, NT], BF16); bgs = dscr("bgs", [NT, 16]); blobs = dscr("blobs", [NTL, 128, 2048], BF16)
    ofs = dscr("ofs", [NTL, 128, 512]); ofs2 = dscr("ofs2", [NTL, 128, 512])
    w1c = nc.dram_tensor("w1c", [32, 128, 1024], BF16, kind="Internal"); w2c = nc.dram_tensor("w2c", [8, 128, 4096], BF16, kind="Internal")

    dbufs = {}

    def DB(name, i):
        k = (name, i)
        if k not in dbufs:
            dbufs[k] = Buf(name + str(i))
        return dbufs[k]

    def DBs(name, t0, t1):
        return [DB(name, t) for t in range(max(t0, 0), min(t1, NTL))]

    xs_v = xs.ap().rearrange("(kc p) t -> p kc t", p=128)
    qkvs_v = qkvs.ap().rearrange("(n p) t -> p n t", p=128)
    gates_v = gates.ap().rearrange("(n p) t -> p n t", p=128)
    obs_v = obs.ap().rearrange("(n p) t -> p n t", p=128)

    stack0 = contextlib.ExitStack()

    uid = [0]

    def alloc(stack, name, shape, dt=F32):
        uid[0] += 1
        name = "%s_u%d" % (name, uid[0])
        return T(stack.enter_context(nc.sbuf_tensor(name, list(shape), dt)), name)

    ps = [T(stack0.enter_context(nc.psum_tensor("psb%d" % i, [128, 512], F32)), "ps%d" % i) for i in range(8)]
    for p_ in ps:
        p_.b.excl = True

    def ps3(i, h=4):
        return ps[i].t[:].rearrange("p (h c) -> p h c", h=h)

    def psbf(i):
        return ps[i].t[:].bitcast(BF16)

    c32 = alloc(stack0, "c32", [128, NCONST, 128], F32)
    c16 = alloc(stack0, "c16", [128, NCONST, 128], BF16)
    fw.dma('sp', c32[:], consts.ap().rearrange("n p c -> p n c"), w=[c32])
    fw.dma('pool', c16[:], consts.ap().rearrange("n p c -> p n c"), w=[c16])
    epsT = alloc(stack0, "epsT", [128, 1], F32)
    fw.memset('pool', epsT[:], EPS, w=[epsT])
    condT = alloc(stack0, "condT", [128, 8, 2], F32)
    scT = alloc(stack0, "scT", [128, 8, 2], BF16)
    for c in range(2):
        fw.dma('sp', condT[:, :, c], cond.ap()[c].rearrange("(kc p) -> p kc", p=128), w=[condT],
               allow_slow_non_contiguous=True)
    fw.act(scT[:], condT[:], AF.Silu, r=[condT], w=[scT])
    nfT = alloc(stack0, "nfT", [128, 8], F32)
    fw.dma('sp', nfT[:], norm_final.ap().rearrange("(kc p) -> p kc", p=128), w=[nfT], allow_slow_non_contiguous=True)
    nmT = alloc(stack0, "nmT", [128, 4, 8], F32); nffT = alloc(stack0, "nffT", [128, 4, 8], F32)
    bmT = alloc(stack0, "bmT", [128, 4, 48], F32); cwT = alloc(stack0, "cwT", [128, 4, 12, 3], F32)
    onT = alloc(stack0, "onT", [128, 4], F32)
    dtb = alloc(stack0, "dtb", [128, 4, 8], F32); nA = alloc(stack0, "nA", [128, 4, 8], F32)
    ws16 = alloc(stack0, "ws16", [128, 4, 4, 128], BF16); bs16 = alloc(stack0, "bs16", [1, 4, 4, 128], BF16)
    for l in range(L):
        fw.dma('sp', nmT[:, l, :], norm_mix.ap()[l].rearrange("(kc p) -> p kc", p=128), w=[nmT], allow_slow_non_contiguous=True)
        fw.dma('sp', nffT[:, l, :], norm_ffn.ap()[l].rearrange("(kc p) -> p kc", p=128), w=[nffT], allow_slow_non_contiguous=True)
        fw.dma('sp', bmT[:, l, :], b_mod.ap()[l].rearrange("(n p) -> p n", p=128), w=[bmT], allow_slow_non_contiguous=True)
        for wi in range(3):
            fw.dma('sp', cwT[:, l, :, wi], conv_qkv.ap()[l, wi].rearrange("(n p) -> p n", p=128), w=[cwT], allow_slow_non_contiguous=True)
        fw.dma('sp', onT[:, l:l + 1], o_norm.ap()[l].rearrange("(p o) -> p o", o=1), w=[onT], allow_slow_non_contiguous=True)
        fw.dma('sp', dtb[:, l, :], dt_bias.ap()[l].partition_broadcast(128), w=[dtb])
        fw.dma('sp', nA[:, l, :], a_log.ap()[l].partition_broadcast(128), w=[nA])
        fw.dma('pool', ws16[:, l, :, :], wsT.ap()[l].rearrange("g s t -> s g t"), w=[ws16])
        fw.dma('pool', bs16[:, l, :, :], b_sp.ap()[l:l + 1], w=[bs16])
    fw.act(nA[:, 0:L, :], nA[:, 0:L, :], AF.Exp, r=[nA], w=[nA])
    fw.ts('dve', nA[:, 0:L, :], nA[:, 0:L, :], -1.0, None, ALU.mult, r=[nA], w=[nA])

    ident32 = c32[:, C_ID, :]
    ident16 = c16[:, C_ID, :]
    ones16 = c16[:, C_ONES, :]
    ones32 = c32[:, C_ONES, :]

    def rstd(out, in_, scale, r, w):
        fw.act(out, in_, AF.Sqrt, bias=epsT[:, 0:1], scale=scale, r=list(r) + [epsT], w=w)
        fw.recip(out, out, r=w, w=w)

    import os
    SKIPA = os.environ.get('SKIPA') == '1'
    def run_jobs(factories, NJ):
        pending = list(factories)
        active = [None] * NJ
        while pending or any(a is not None for a in active):
            for sl in range(NJ):
                if active[sl] is None and pending:
                    active[sl] = pending.pop(0)(sl)
                if active[sl] is not None:
                    try:
                        next(active[sl])
                    except StopIteration:
                        active[sl] = None

    with contextlib.ExitStack() as st:
        xt = [alloc(st, "x0t%d" % i, [128, 1024]) for i in range(2)]
        xo = [alloc(st, "x0o%d" % i, [128, 8, 128]) for i in range(2)]
        for t in range(0 if SKIPA else NTL):
            a = xt[t % 2]; o = xo[t % 2]
            fw.dma('sp', a[:], xin.ap()[t * 128:(t + 1) * 128, :], w=[a])
            for half in range(2):
                pb = ps[(t % 2) * 2 + half]
                for j in range(4):
                    kc = half * 4 + j
                    fw.tr(pb[:, j * 128:(j + 1) * 128], a[:, kc * 128:(kc + 1) * 128], ident32, r=[a, c32], w=[pb])
                fw.cp('dve' if half == 0 else 'act', o[:, half * 4:(half + 1) * 4, :], ps3((t % 2) * 2 + half), r=[pb], w=[o])
            fw.dma('act', xs_v[:, :, t * 128:(t + 1) * 128], o[:], r=[o], w=[DB('xs', t)])
        fw.emit()

    modT = alloc(stack0, "modT", [128, 48, 2], F32)
    w1T = alloc(stack0, "w1T", [128, 8, 2], F32)
    w2T = alloc(stack0, "w2T", [128, 8, 2], F32)

    for l in range(L):
        if stop == 'x0':
            break
        with contextlib.ExitStack() as st:
            wm = [alloc(st, "wm%d" % i, [128, 8, 512], BF16) for i in range(2)]
            pm = ps[0]
            for nb in range(0 if SKIPA else 12):
                wb = wm[nb % 2]
                fw.dma('pool', wb[:], w_mod.ap()[l].rearrange("(kc p) n -> p kc n", p=128)[:, :, nb * 512:(nb + 1) * 512], w=[wb])
                for cc in range(4):
                    n = nb * 4 + cc
                    for kc in range(8):
                        fw.mm(pm[:, n * 2:n * 2 + 2], wb[:, kc, cc * 128:(cc + 1) * 128], scT[:, kc, :],
                              start=(kc == 0), stop=(kc == 7), r=[wb, scT], w=[pm])
            pmv = pm.t[:, 0:96].rearrange("p (n c) -> p n c", c=2)
            for c in range(0 if SKIPA else 2):
                fw.tt('dve', modT[:, :, c], pmv[:, :, c], bmT[:, l, :], ALU.add, r=[pm, bmT], w=[modT])
                fw.stt('dve', w1T[:, :, c], modT[:, 8:16, c], 1.0, nmT[:, l, :], ALU.add, ALU.mult, r=[modT, nmT], w=[w1T])
                fw.stt('dve', w2T[:, :, c], modT[:, 32:40, c], 1.0, nffT[:, l, :], ALU.add, ALU.mult, r=[modT, nffT], w=[w2T])
            fw.emit()

        if stop == 'm':
            break
        with contextlib.ExitStack() as st:
            win = alloc(st, "win", [128, 8, 3088], BF16)
            w_in_v = w_in.ap()[l].rearrange("(kc p) n -> p kc n", p=128)
            for kc in range(8):
                fw.dma('pool', win[:, kc, :], w_in_v[:, kc, :], w=[win])
            NA = 2
            def perA(name, shape, dt=F32, k=1):
                return [[alloc(st, "%s%d_%d" % (name, i, j), shape, dt) for j in range(k)] for i in range(NA)]
            xT = perA("a_x", [128, 8, 512]); sqA = perA("a_sq", [128, 8, 512], BF16); rsA = perA("a_rs", [128, 512])
            tmpA = perA("a_tmp", [128, 512], F32, 2); hA = perA("a_h", [128, 8, 512], BF16); stgA = perA("a_stg", [128, 512], F32, 4)
            uA = perA("a_u", [128, 4, 512]); bgA = perA("a_bg", [128, 16], F32, 2); vgfA = perA("a_vgf", [128, 4, 128], F32, 2)
            vgnA = perA("a_vgn", [128, 4, 128], BF16, 2); st4A = perA("a_st4", [128, 4], F32, 2); obA = perA("a_ob", [128, 4, 128], BF16, 2)
            COLS = [n * 128 for n in range(12)] + [1536 + n * 128 for n in range(4)] + [2064 + n * 128 for n in range(4)]

            def ajob(gi):
                def gen(sl):
                    base = 4 * sl
                    g0 = gi * 512
                    c = 0 if g0 < 1024 else 1
                    x = xT[sl][0]; sq = sqA[sl][0]; rs = rsA[sl][0]; h16 = hA[sl][0]; uT = uA[sl][0]
                    tls = DBs('xs', gi * 4, gi * 4 + 4)
                    fw.dma('sp', x[:], xs_v[:, :, g0:g0 + 512], r=tls, w=[x])
                    yield
                    fw.act(sq[:], x[:], AF.Square, r=[x], w=[sq])
                    yield
                    pst = ps[base + 2]
                    for kc in range(8):
                        fw.mm(pst[:, :], ones16, sq[:, kc, :], start=(kc == 0), stop=(kc == 7), r=[c16, sq], w=[pst])
                    yield
                    fw.act(rs[:], pst[:, :], AF.Sqrt, bias=epsT[:, 0:1], scale=1.0 / 1024, r=[pst, epsT], w=[rs])
                    yield
                    fw.recip(rs[:], rs[:], r=[rs], w=[rs])
                    yield
                    for kc in range(8):
                        tm = tmpA[sl][kc % 2]
                        fw.stt('dve', tm[:], x[:, kc, :], w1T[:, kc, c:c + 1], rs[:], ALU.mult, ALU.mult, r=[x, w1T, rs], w=[tm])
                        fw.act(h16[:, kc, :], tm[:], AF.Identity, bias=modT[:, kc, c:c + 1], r=[tm, modT], w=[h16])
                        if kc % 2 == 1:
                            yield
                    si = 0
                    for n in range(20):
                        pb = ps[base + (n % 2)]
                        col = COLS[n]
                        for kc in range(8):
                            fw.mm(pb[:, :], win[:, kc, col:col + 128], h16[:, kc, :], start=(kc == 0), stop=(kc == 7),
                                  r=[win, h16], w=[pb])
                        if n < 12:
                            sg = stgA[sl][si % 4]; si += 1
                            fw.cp('dve' if n % 2 == 0 else 'act', sg[:], pb[:, :], r=[pb], w=[sg])
                            fw.dma('sp', qkvs_v[:, n, g0:g0 + 512], sg[:], r=[sg], w=DBs('qkvs', gi * 4, gi * 4 + 4))
                        elif n < 16:
                            sg = stgA[sl][si % 4]; si += 1
                            fw.act(sg[:], pb[:, :], AF.Silu, r=[pb], w=[sg])
                            fw.dma('sp', gates_v[:, n - 12, g0:g0 + 512], sg[:], r=[sg], w=DBs('gates', gi * 4, gi * 4 + 4))
                        else:
                            fw.act(uT[:, n - 16, :], pb[:, :], AF.Gelu, r=[pb], w=[uT])
                        yield
                    for tt_ in range(4):
                        t = gi * 4 + tt_
                        tok = slice(tt_ * 128, (tt_ + 1) * 128)
                        pab = ps[base + 2]; pvg = ps[base + 3]
                        b = bgA[sl][tt_ % 2]; vf = vgfA[sl][tt_ % 2]; vn = vgnA[sl][tt_ % 2]; s4 = st4A[sl][tt_ % 2]; ob = obA[sl][tt_ % 2]
                        for kc in range(8):
                            fw.mm(pab[:, 0:16], h16[:, kc, tok], win[:, kc, 2048:2064], start=(kc == 0), stop=(kc == 7),
                                  r=[h16, win], w=[pab])
                        for kc in range(8):
                            fw.mm(pvg[:, :], h16[:, kc, tok], win[:, kc, 2576:3088], start=(kc == 0), stop=(kc == 7),
                                  r=[h16, win], w=[pvg])
                        yield
                        fw.act(b[:, 0:8], pab[:, 0:8], AF.Sigmoid, r=[pab], w=[b])
                        fw.act(vf[:], ps3(base + 3), AF.Gelu, r=[pvg], w=[vf])
                        yield
                        fw.tt('dve', b[:, 8:16], pab[:, 8:16], dtb[:, l, :], ALU.add, r=[pab, dtb], w=[b])
                        fw.op('dve', lambda e, s4=s4, vf=vf: e.reduce_sum(s4[:], vf[:], AX.X), [vf], [s4])
                        yield
                        fw.act(b[:, 8:16], b[:, 8:16], AF.Exp, r=[b], w=[b])
                        fw.ts('dve', s4[:], s4[:], -1.0 / 128, None, ALU.mult, r=[s4], w=[s4])
                        yield
                        fw.act(b[:, 8:16], b[:, 8:16], AF.Ln, bias=1.0, r=[b], w=[b])
                        fw.tt('dve', vf[:], vf[:], s4[:].unsqueeze(2).to_broadcast([128, 4, 128]), ALU.add, r=[vf, s4], w=[vf])
                        yield
                        fw.tt('dve', b[:, 8:16], b[:, 8:16], nA[:, l, :], ALU.mult, r=[b, nA], w=[b])
                        sg = stgA[sl][si % 4]; si += 1
                        sg3 = sg.t[:].rearrange("p (h c) -> p h c", h=4)
                        fw.tt('pool', sg3, vf[:], vf[:], ALU.mult, r=[vf], w=[sg])
                        yield
                        fw.dma('act', bgs.ap()[t * 128:(t + 1) * 128, :], b[:], r=[b], w=[DB('bgs', t)])
                        fw.op('dve', lambda e, s4=s4, sg3=sg3: e.reduce_sum(s4[:], sg3, AX.X), [sg], [s4])
                        yield
                        fw.act(s4[:], s4[:], AF.Sqrt, bias=epsT[:, 0:1], scale=1.0 / 128, r=[s4, epsT], w=[s4])
                        yield
                        fw.recip(s4[:], s4[:], r=[s4], w=[s4])
                        yield
                        fw.tt('dve', vn[:], vf[:], s4[:].unsqueeze(2).to_broadcast([128, 4, 128]), ALU.mult, r=[vf, s4], w=[vn])
                        yield
                        pmx = ps[base + 2]
                        for g in range(4):
                            fw.mm(pmx[:, g * 128:(g + 1) * 128], vn[:, g, :], ws16[:, l, g, :], start=True, stop=False, r=[vn, ws16], w=[pmx])
                            fw.mm(pmx[:, g * 128:(g + 1) * 128], c16[0:1, C_ONES, :], bs16[0:1, l, g, :], start=False, stop=True,
                                  r=[c16, bs16], w=[pmx])
                        yield
                        fw.tt('dve', ob[:], uT[:, :, tok], ps3(base + 2), ALU.mult, r=[uT, pmx], w=[ob])
                        yield
                        fw.dma('act', obs_v[:, :, t * 128:(t + 1) * 128], ob[:], r=[ob], w=[DB('obs', t)])
                return gen
            run_jobs([ajob(gi) for gi in range(0 if SKIPA else NT // 512)], NA)
            fw.emit()

        if stop == 'a':
            break
        import os
        with contextlib.ExitStack() as st:
            NB0 = 3
            def per0(name, shape, dt=F32):
                return [alloc(st, "%s%d" % (name, i), shape, dt) for i in range(NB0)]
            pre = per0("b_pre", [128, 12, 130]); cv = per0("b_cv", [128, 12, 128]); cv2 = per0("b_cv2", [128, 12, 128])
            cv3 = per0("b_cv3", [128, 12, 128]); sq16 = per0("b_sq16", [128, 8, 128], BF16); rn = per0("b_rn", [128, 8, 128])
            vT16 = per0("b_vT16", [128, 4, 128], BF16); blob0 = per0("b_blob", [128, 4, 4, 128], BF16)
            cw = cwT.t[:, l, :, :]

            def b0job(t, sstart, slen):
                def gen(sl):
                    p = pre[sl]; c_ = cv[sl]; c2 = cv2[sl]; c3 = cv3[sl]; bl = blob0[sl]; sq_ = sq16[sl]; rn_ = rn[sl]; vt_ = vT16[sl]
                    ia = 2 * sl; ib = 2 * sl + 1
                    t0 = t * 128
                    lo = max(t0 - 1, sstart); hi = min(t0 + 129, sstart + slen)
                    if lo > t0 - 1:
                        fw.memset('pool', p[:, :, 0:1], 0.0, w=[p])
                    if hi < t0 + 129:
                        fw.memset('pool', p[:, :, 129:130], 0.0, w=[p])
                    for n3 in range(3):
                        fw.dma('sp', p[:, n3 * 4:(n3 + 1) * 4, lo - (t0 - 1):hi - (t0 - 1)], qkvs_v[:, n3 * 4:(n3 + 1) * 4, lo:hi],
                               r=DBs('qkvs', t - 1, t + 2), w=[p])
                    yield
                    fw.tt('dve', c_[:], p[:, :, 0:128], cw[:, :, 0:1].to_broadcast([128, 12, 128]), ALU.mult, r=[p, cwT], w=[c_])
                    fw.tt('pool', c2[:], p[:, :, 1:129], cw[:, :, 1:2].to_broadcast([128, 12, 128]), ALU.mult, r=[p, cwT], w=[c2])
                    fw.tt('pool', c3[:], p[:, :, 2:130], cw[:, :, 2:3].to_broadcast([128, 12, 128]), ALU.mult, r=[p, cwT], w=[c3])
                    yield
                    fw.tt('dve', c_[:], c_[:], c2[:], ALU.add, r=[c_, c2], w=[c_])
                    yield
                    fw.tt('dve', c_[:], c_[:], c3[:], ALU.add, r=[c_, c3], w=[c_])
                    yield
                    fw.act(c_[:], c_[:], AF.Silu, r=[c_], w=[c_])
                    yield
                    fw.act(sq_[:], c_[:, 0:8, :], AF.Square, r=[c_], w=[sq_])
                    fw.cp('act', vt_[:], c_[:, 8:12, :], r=[c_], w=[vt_])
                    yield
                    fw.mm(ps[ia][:, :], ones16, sq_[:, 0:4, :], r=[c16, sq_], w=[ps[ia]])
                    fw.mm(ps[ib][:, :], ones16, sq_[:, 4:8, :], r=[c16, sq_], w=[ps[ib]])
                    yield
                    fw.act(rn_[:, 0:4, :], ps3(ia), AF.Sqrt, bias=epsT[:, 0:1], scale=1.0, r=[ps[ia], epsT], w=[rn_])
                    fw.act(rn_[:, 4:8, :], ps3(ib), AF.Sqrt, bias=epsT[:, 0:1], scale=1.0, r=[ps[ib], epsT], w=[rn_])
                    yield
                    fw.recip(rn_[:], rn_[:], r=[rn_], w=[rn_])
                    yield
                    fw.tt('dve', bl[:, 0:2, :, :], c_[:, 0:8, :].rearrange("p (a h) c -> p a h c", a=2),
                          rn_[:].rearrange("p (a h) c -> p a h c", a=2), ALU.mult, r=[c_, rn_], w=[bl])
                    yield
                    pv = psbf(ia)
                    for h in range(4):
                        fw.tr(pv[:, h * 128:(h + 1) * 128], bl[:, 1, h, :], ident16, r=[bl, c16], w=[ps[ia]])
                        fw.tr(pv[:, 512 + h * 128:512 + (h + 1) * 128], vt_[:, h, :], ident16, r=[vt_, c16], w=[ps[ia]])
                    yield
                    fw.cp('dve', bl[:, 2:4, :, :], pv.rearrange("p (a h c) -> p a h c", a=2, h=4), r=[ps[ia]], w=[bl])
                    yield
                    fw.dma('act', blobs.ap()[t], bl[:].rearrange("p a h c -> p (a h c)"), r=[bl], w=[DB('blobs', t)])
                return gen
            jobs0 = []
            for (sstart, slen, cidx) in SEQS:
                for t in range(sstart // 128, (sstart + slen) // 128):
                    jobs0.append(b0job(t, sstart, slen))
            run_jobs(jobs0, NB0)
            fw.emit()

        if stop == 'b0':
            break
        with contextlib.ExitStack() as st:
            NJ = 4
            def per(name, shape, dt=F32):
                return [alloc(st, "%s%d" % (name, i), shape, dt) for i in range(NJ)]
            blobj = per("j_blob", [128, 4, 4, 128], BF16)
            bgt = per("j_bg", [128, 16]); Gs = per("j_Gs", [128, 16]); eG = per("j_eG", [128, 16]); sck = per("j_sck", [128, 4])
            gpad = per("j_gpad", [128, 128])
            for i in range(NJ):
                fw.memset('pool', gpad[i][:], 0.0, w=[gpad[i]])
            for j0 in range(0, 32, 4):
                fw.dma('pool', w1c.ap()[j0:j0 + 4], w_ff1b.ap()[l, j0:j0 + 4], w=[DB('w1c', j0 // 4)])
            for dc in range(8):
                fw.dma('pool', w2c.ap()[dc], w_ff2b.ap()[l, dc], w=[DB('w2c', dc)])
            Ugj = per("j_Ug", [128, 4, 128]); decj = per("j_dec", [128, 4, 128]); decTj = per("j_decT", [128, 4, 128])
            eGBj = per("j_eGB", [128, 4, 128]); nm32j = per("j_nm32", [128, 4, 128])
            Pj = per("j_P", [128, 4, 128], BF16); Qj = per("j_Q", [128, 4, 128], BF16); QMj = per("j_QM", [128, 5, 4, 128], BF16)
            TT16j = per("j_TT16", [128, 4, 128], BF16)
            T16j = per("j_T16", [128, 4, 128], BF16); X16j = per("j_X16", [128, 4, 128], BF16)
            vbj = per("j_vb", [128, 4, 128], BF16); kbgj = per("j_kbg", [128, 4, 128], BF16)
            kdj = per("j_kd", [128, 4, 128], BF16); wTj = per("j_wT", [128, 4, 128], BF16)
            qkj = per("j_qk", [128, 4, 128], BF16); qdj = per("j_qd", [128, 4, 128], BF16)
            u32j = per("j_u", [128, 4, 128]); vnj = per("j_vn", [128, 4, 128], BF16); oTj = per("j_oT", [128, 4, 128])
            NCH = 8
            S32c = [alloc(st, "j_S32_%d" % i, [128, 4, 128]) for i in range(NCH)]
            S16c = [alloc(st, "j_S16_%d" % i, [128, 4, 128], BF16) for i in range(NCH)]
            negones32 = c32[:, C_NEGONES, :]

            def job(slot, ci, t, d, cidx, is_last, seq_i, turn, my_idx):
                bl = blobj[slot]; b = bgt[slot]; G = Gs[slot]; e_ = eG[slot]; sk = sck[slot]; gp = gpad[slot]
                Ug = Ugj[slot]; dec = decj[slot]; decT = decTj[slot]; eGB = eGBj[slot]; nm32 = nm32j[slot]
                P = Pj[slot]; Q = Qj[slot]; QM = QMj[slot]; TT16 = TT16j[slot]; T16 = T16j[slot]; X16 = X16j[slot]
                vb16 = vbj[slot]; kbg16 = kbgj[slot]; kd = kdj[slot]; wT = wTj[slot]; qk = qkj[slot]; qd = qdj[slot]
                u = u32j[slot]; vn = vnj[slot]; oT = oTj[slot]
                pa = ps[2 * slot]; pb_ = ps[2 * slot + 1]; ia = 2 * slot; ib = 2 * slot + 1
                fw.dma('sp', bl[:].rearrange("p a h c -> p (a h c)"), blobs.ap()[t], r=[DB('blobs', t)], w=[bl])
                fw.dma('sp', b[:], bgs.ap()[t * 128:(t + 1) * 128, :], r=[DB('bgs', t)], w=[b])
                g4 = b[:, 8 + d * 4:12 + d * 4]; b4 = b[:, d * 4:d * 4 + 4]
                U = c32[:, C_UF + d, :]
                fw.cp('dve', gp[:, 0:4], g4, r=[b], w=[gp])
                fw.mm(pa[:, 0:128], U, gp[:], r=[c32, gp], w=[pa])
                fw.mm(pa[:, 128:256], c32[:, C_SAME, :], gp[:], r=[c32, gp], w=[pa])
                fw.mm(pa[:, 256:384], c32[:, C_CH0, :], gp[:], r=[c32, gp], w=[pa])
                fw.mm(pa[:, 384:512], c32[:, C_CH1, :], gp[:], r=[c32, gp], w=[pa])
                for h in range(4):
                    fw.act(Ug[:, h, :], U, AF.Identity, scale=g4[:, h:h + 1], r=[c32, b], w=[Ug])
                yield
                fw.cp('dve', G[:].rearrange("p (a b) -> p a b", a=4), ps3(ia)[:, :, 0:4], r=[pa], w=[G])
                fw.tt('dve', G[:, 4:8], G[:, 4:8], G[:, 0:4], ALU.subtract, r=[G], w=[G])
                fw.act(e_[:], G[:], AF.Exp, r=[G], w=[e_])
                for h in range(4):
                    fw.mm(pb_[:, h * 128:(h + 1) * 128], Ug[:, h, :], ones32, start=True, stop=False, r=[Ug, c32], w=[pb_])
                    fw.mm(pb_[:, h * 128:(h + 1) * 128], negones32, Ug[:, h, :], start=False, stop=True, r=[Ug, c32], w=[pb_])
                for h in range(4):
                    fw.mm(pa[:, h * 128:(h + 1) * 128], ones32, Ug[:, h, :], r=[Ug, c32], w=[pa])
                yield
                fw.tt('dve', sk[:], b4, e_[:, 0:4], ALU.mult, r=[b, e_], w=[sk])
                fw.act(dec[:], ps3(ib), AF.Relu, scale=-1.0, r=[pb_], w=[dec])
                fw.act(decT[:], ps3(ib), AF.Relu, scale=1.0, r=[pb_], w=[decT])
                fw.act(eGB[:], ps3(ia), AF.Exp, bias=LNSCALE, r=[pa], w=[eGB])
                for h in range(4):
                    fw.mm(pb_[:, h * 128:(h + 1) * 128], bl[:, 1, h, :], bl[:, 1, h, :], r=[bl], w=[pb_])
                yield
                fw.act(dec[:], dec[:], AF.Exp, scale=-1.0, r=[dec], w=[dec])
                fw.act(decT[:], decT[:], AF.Exp, scale=-1.0, r=[decT], w=[decT])
                ktok = bl[:, 2, :, :]; vtok = bl[:, 3, :, :]
                fw.tt('pool', vb16[:], vtok, b4.unsqueeze(2).to_broadcast([128, 4, 128]), ALU.mult, r=[bl, b], w=[vb16])
                fw.tt('pool', kbg16[:], ktok, sk[:].unsqueeze(2).to_broadcast([128, 4, 128]), ALU.mult, r=[bl, sk], w=[kbg16])
                fw.tt('pool', kd[:], ktok, e_[:, 4:8].unsqueeze(2).to_broadcast([128, 4, 128]), ALU.mult, r=[bl, e_], w=[kd])
                yield
                fw.tt('dve', nm32[:], ps3(ib), dec[:], ALU.mult, r=[pb_, dec], w=[nm32])
                fw.tt('pool', nm32[:], nm32[:], c32[:, C_NSTRF + d, :].unsqueeze(1).to_broadcast([128, 4, 128]), ALU.mult, r=[nm32, c32], w=[nm32])
                yield
                fw.tt('dve', P[:], nm32[:], b4.unsqueeze(2).to_broadcast([128, 4, 128]), ALU.mult, r=[nm32, b], w=[P])
                pqv = psbf(ia)
                for h in range(4):
                    fw.tr(pqv[:, h * 128:(h + 1) * 128], P[:, h, :], ident16, r=[P, c16], w=[pa])
                for h in range(4):
                    fw.mm(pb_[:, h * 128:(h + 1) * 128], bl[:, 1, h, :], bl[:, 0, h, :], r=[bl], w=[pb_])
                yield
                fw.cp('act', Q[:], pqv[:, 0:512].rearrange("p (h c) -> p h c", h=4), r=[pa], w=[Q])
                fw.tt('dve', nm32[:], ps3(ib), decT[:], ALU.mult, r=[pb_, decT], w=[nm32])
                fw.tt('dve', qd[:], eGB[:], bl[:, 0, :, :], ALU.mult, r=[eGB, bl], w=[qd])
                yield
                fw.tt('pool', qk[:], nm32[:], c32[:, C_INTF + d, :].unsqueeze(1).to_broadcast([128, 4, 128]), ALU.mult, r=[nm32, c32], w=[qk])
                for li in range(1, 6):
                    fw.tt('pool', QM[:, li - 1, :, :], Q[:], c16[:, C_MB + li, :].unsqueeze(1).to_broadcast([128, 4, 128]),
                          ALU.mult, r=[Q, c16], w=[QM])
                mb1 = c32[:, C_MB, :].unsqueeze(1).to_broadcast([128, 4, 128])
                idb = c32[:, C_ID, :].unsqueeze(1).to_broadcast([128, 4, 128])
                fw.tt('dve', eGB[:], P[:], mb1, ALU.mult, r=[P, c32], w=[eGB])
                fw.tt('pool', nm32[:], Q[:], mb1, ALU.mult, r=[Q, c32], w=[nm32])
                yield
                fw.tt('dve', T16[:], eGB[:], idb, ALU.add, r=[eGB, c32], w=[T16])
                fw.tt('pool', TT16[:], nm32[:], idb, ALU.add, r=[nm32, c32], w=[TT16])
                yield
                for li in range(1, 6):
                    for h in range(4):
                        fw.mm(pa[:, h * 128:(h + 1) * 128], QM[:, li - 1, h, :], T16[:, h, :], r=[QM, T16], w=[pa])
                    yield
                    fw.cp('act', X16[:], ps3(ia), r=[pa], w=[X16])
                    yield
                    if li < 5:
                        for h in range(4):
                            fw.mm(pb_[:, h * 128:(h + 1) * 128], TT16[:, h, :], X16[:, h, :], r=[TT16, X16], w=[pb_])
                    for h in range(4):
                        fw.mm(pa[:, h * 128:(h + 1) * 128], X16[:, h, :], TT16[:, h, :], r=[TT16, X16], w=[pa])
                    yield
                    if li < 5:
                        fw.tt('dve', T16[:], T16[:], ps3(ib), ALU.add, r=[T16, pb_], w=[T16])
                    fw.tt('dve', TT16[:], TT16[:], ps3(ia), ALU.add, r=[TT16, pa], w=[TT16])
                    yield
                for h in range(4):
                    fw.mm(pa[:, h * 128:(h + 1) * 128], kbg16[:, h, :], TT16[:, h, :], r=[kbg16, TT16], w=[pa])
                for h in range(4):
                    fw.mm(pb_[:, h * 128:(h + 1) * 128], TT16[:, h, :], vb16[:, h, :], r=[TT16, vb16], w=[pb_])
                yield
                fw.cp('act', wT[:], ps3(ia), r=[pa], w=[wT])
                fw.cp('act', u[:], ps3(ib), r=[pb_], w=[u])
                yield
                while turn[0] != my_idx:
                    yield
                Sf = S32c[ci]; Sb = S16c[ci]
                p6 = pa; p7 = pb_
                chs = (0, 1) if d == 0 else (1, 0)
                for ch in chs:
                    R = slice(ch * 64, (ch + 1) * 64)
                    for h in range(4):
                        fw.mm(p6[:, h * 128:(h + 1) * 128], wT[:, h, :], Sb[:, h, :], r=[wT, Sb], w=[p6])
                    yield
                    fw.tt('dve', vn[R, :, :], u[R, :, :], ps3(ia)[R, :, :], ALU.subtract, r=[u, p6], w=[vn])
                    yield
                    for h in range(4):
                        fw.mm(p7[:, h * 64:(h + 1) * 64], Sb[:, h, :], qd[:, h, R], start=True, stop=False, r=[Sb, qd], w=[p7])
                        fw.mm(p7[:, h * 64:(h + 1) * 64], vn[R, h, :], qk[R, h, R], start=False, stop=True, r=[vn, qk], w=[p7])
                    for h in range(4):
                        fw.mm(p6[:, h * 128:(h + 1) * 128], kd[R, h, :], vn[R, h, :], r=[kd, vn], w=[p6])
                    yield
                    fw.cp('act', oT[:, :, R], p7.t[:, 0:256].rearrange("p (h c) -> p h c", h=4), r=[p7], w=[oT])
                    for h in range(4):
                        fw.stt('dve', Sf[:, h, :], Sf[:, h, :], e_[:, 8 + ch * 4 + h:9 + ch * 4 + h], ps3(ia)[:, h, :],
                               ALU.mult, ALU.add, r=[Sf, e_, p6], w=[Sf])
                    yield
                    fw.cp('act', Sb[:], Sf[:], r=[Sf], w=[Sb])
                    yield
                turn[0] += 1
                fw.dma('act', (ofs if d == 0 else ofs2).ap()[t].rearrange("p (h c) -> p h c", h=4), oT[:], r=[oT],
                       w=[DB('ofs%d' % d, t)])
                if is_last and cidx == 0:
                    fw.dma('act', ns.ap()[seq_i, l, d].rearrange("h k v -> k h v"), Sf[:], r=[Sf])

            def run_chains(chain_defs):
                state = []
                for (ci, seq_i, sstart, slen, cidx, d) in chain_defs:
                    Sf = S32c[ci]; Sb = S16c[ci]
                    if cidx == 0:
                        fw.memset('pool', Sf[:], 0.0, w=[Sf])
                    else:
                        fw.dma('sp', Sf[:], s0.ap()[l, d].rearrange("h k v -> k h v"), w=[Sf])
                    fw.cp('act', Sb[:], Sf[:], r=[Sf], w=[Sb])
                    tiles = list(range(sstart // 128, (sstart + slen) // 128))
                    if d == 1:
                        tiles = tiles[::-1]
                    state.append(dict(ci=ci, seq_i=seq_i, cidx=cidx, d=d, tiles=tiles, nxt=0, turn=[0], inflight=0))
                active = [None] * NJ
                while True:
                    progressed = False
                    for slot in range(NJ):
                        if active[slot] is None:
                            cands = [c for c in state if c['nxt'] < len(c['tiles']) and c['inflight'] < 2]
                            if cands:
                                c = min(cands, key=lambda c: (c['inflight'], c['nxt']))
                                k = c['nxt']; c['nxt'] += 1; c['inflight'] += 1
                                g = job(slot, c['ci'], c['tiles'][k], c['d'], c['cidx'], k == len(c['tiles']) - 1, c['seq_i'], c['turn'], k)
                                active[slot] = (g, c)
                        if active[slot] is not None:
                            progressed = True
                            g, c = active[slot]
                            try:
                                next(g)
                            except StopIteration:
                                c['inflight'] -= 1
                                active[slot] = None
                    if not progressed and all(c['nxt'] >= len(c['tiles']) for c in state):
                        break

            pr = []
            for seq_i, (sstart, slen, cidx) in enumerate(SEQS[:4]):
                for d in range(2):
                    pr.append((seq_i * 2 + d, seq_i, sstart, slen, cidx, d))
            run_chains(pr)
            sstart, slen, cidx = SEQS[4]
            run_chains([(0, 4, sstart, slen, cidx, 0), (1, 4, sstart, slen, cidx, 1)])
            fw.emit()

        if stop == 'b12':
            break
        with contextlib.ExitStack() as st:
            wo = alloc(st, "wo", [128, 8, 1024], BF16)
            fw.dma('pool', wo[:], w_out.ap()[l].rearrange("(kc p) n -> p kc n", p=128), w=[wo])
            NE = 3
            def per3(name, shape, dt=F32):
                return [alloc(st, "%s%d" % (name, i), shape, dt) for i in range(NE)]
            ofl = per3("e_ofl", [128, 4, 128]); ofl2 = per3("e_ofl2", [128, 4, 128]); osq = per3("e_osq", [128, 4, 128], BF16)
            ors = per3("e_ors", [128, 4, 128]); gTl = per3("e_gT", [128, 4, 128]); catA = per3("e_catA", [128, 4, 128], BF16)
            catB = per3("e_catB", [128, 4, 128], BF16); xres = per3("e_x", [128, 8, 128])

            def b3job(t):
                def gen(sl):
                    cidx = 0 if t < 8 else 1
                    of_ = ofl[sl]; of2 = ofl2[sl]; gt = gTl[sl]; cb = catB[sl]; xr = xres[sl]; ca = catA[sl]; oq = osq[sl]; orr = ors[sl]
                    ia = 2 * sl; ib = 2 * sl + 1
                    fw.dma('sp', of_[:], ofs.ap()[t].rearrange("p (h c) -> p h c", h=4), r=[DB('ofs0', t)], w=[of_])
                    fw.dma('sp', of2[:], ofs2.ap()[t].rearrange("p (h c) -> p h c", h=4), r=[DB('ofs1', t)], w=[of2])
                    fw.dma('sp', gt[:], gates_v[:, :, t * 128:(t + 1) * 128], r=[DB('gates', t)], w=[gt])
                    fw.dma('sp', cb[:], obs_v[:, :, t * 128:(t + 1) * 128], r=[DB('obs', t)], w=[cb])
                    fw.dma('sp', xr[:], xs_v[:, :, t * 128:(t + 1) * 128], r=[DB('xs', t)], w=[xr])
                    yield
                    fw.tt('pool', of_[:], of_[:], of2[:], ALU.add, r=[of_, of2], w=[of_])
                    yield
                    fw.act(oq[:], of_[:], AF.Square, r=[of_], w=[oq])
                    yield
                    fw.mm(ps[ia][:, :], ones16, oq[:], r=[c16, oq], w=[ps[ia]])
                    yield
                    fw.act(orr[:], ps3(ia), AF.Sqrt, bias=epsT[:, 0:1], scale=1.0 / 128, r=[ps[ia], epsT], w=[orr])
                    yield
                    fw.recip(orr[:], orr[:], r=[orr], w=[orr])
                    yield
                    fw.tt('dve', of_[:], of_[:], orr[:], ALU.mult, r=[of_, orr], w=[of_])
                    yield
                    fw.stt('dve', ca[:], of_[:], onT[:, l:l + 1], gt[:], ALU.mult, ALU.mult, r=[of_, onT, gt], w=[ca])
                    yield
                    for half in range(2):
                        pbn = ia if half == 0 else ib
                        pb2 = ps[pbn]
                        for jj in range(4):
                            dc = half * 4 + jj
                            for kc in range(8):
                                rhs = ca[:, kc, :] if kc < 4 else cb[:, kc - 4, :]
                                fw.mm(pb2[:, jj * 128:(jj + 1) * 128], wo[:, kc, dc * 128:(dc + 1) * 128], rhs,
                                      start=(kc == 0), stop=(kc == 7), r=[wo, ca, cb], w=[pb2])
                        yield
                    for half in range(2):
                        pbn = ia if half == 0 else ib
                        for jj in range(4):
                            dc = half * 4 + jj
                            fw.stt('dve', xr[:, dc, :], ps3(pbn)[:, jj, :], modT[:, 16 + dc, cidx:cidx + 1], xr[:, dc, :],
                                   ALU.mult, ALU.add, r=[ps[pbn], modT, xr], w=[xr])
                        yield
                    fw.dma('act', xs_v[:, :, t * 128:(t + 1) * 128], xr[:], r=[xr], w=[DB('xs', t)])
                return gen
            run_jobs([b3job(t) for t in range(NTL)], NE)
            fw.emit()

        if stop == 'b':
            break
        with contextlib.ExitStack() as st:
            x32 = alloc(st, "c_x", [128, 8, 1024])
            sqc = [alloc(st, "c_sq%d" % i, [128, 1024], BF16) for i in range(2)]
            rs = alloc(st, "c_rs", [128, 1024])
            tmp = [alloc(st, "c_tmp%d" % i, [128, 1024]) for i in range(2)]
            h16 = alloc(st, "c_h", [128, 8, 1024], BF16)
            a16 = alloc(st, "c_a", [128, 32, 1024], BF16)
            w1s = [alloc(st, "c_w1s%d" % i, [128, 2, 8, 128], BF16) for i in range(2)]
            w2s = [alloc(st, "c_w2s%d" % i, [128, 32, 128], BF16) for i in range(2)]
            rl = [alloc(st, "c_rl%d" % i, [128, 512]) for i in range(3)]
            ri = 0
            pi = 0
            for gi in range(NT // 1024):
                g0 = gi * 1024
                c = 0 if g0 < 1024 else 1
                tls = DBs('xs', gi * 8, gi * 8 + 8)
                fw.dma('sp', x32[:], xs_v[:, :, g0:g0 + 1024], r=tls, w=[x32])
                for kc in range(8):
                    s_ = sqc[kc % 2]
                    fw.act(s_[:], x32[:, kc, :], AF.Square, r=[x32], w=[s_])
                    for half in range(2):
                        fw.mm(ps[6 + half][:, :], ones16, s_[:, half * 512:(half + 1) * 512], start=(kc == 0), stop=(kc == 7),
                              r=[c16, s_], w=[ps[6 + half]])
                for half in range(2):
                    rstd(rs[:, half * 512:(half + 1) * 512], ps[6 + half][:, :], 1.0 / 1024, r=[ps[6 + half]], w=[rs])
                for kc in range(8):
                    tm = tmp[kc % 2]
                    fw.stt('dve', tm[:], x32[:, kc, :], w2T[:, kc, c:c + 1], rs[:], ALU.mult, ALU.mult, r=[x32, w2T, rs], w=[tm])
                    fw.act(h16[:, kc, :], tm[:], AF.Identity, bias=modT[:, 24 + kc, c:c + 1], r=[tm, modT], w=[h16])
                for j in range(32):
                    wv2 = w1s[(j // 2) % 2]
                    if j % 2 == 0:
                        fw.dma('sp', wv2[:].rearrange("p j k c -> p j (k c)"), w1c.ap()[j:j + 2].rearrange("j p n -> p j n"),
                               r=[DB('w1c', j // 4)], w=[wv2])
                    wv = T(wv2.t[:, j % 2, :, :]); wv.b = wv2.b
                    for half in range(2):
                        pb = ps[pi % 4]; pi += 1
                        for kc in range(8):
                            fw.mm(pb[:, :], wv[:, kc, :], h16[:, kc, half * 512:(half + 1) * 512], start=(kc == 0), stop=(kc == 7),
                                  r=[wv, h16], w=[pb])
                        r_ = rl[ri % 3]; ri += 1
                        fw.act(r_[:], pb[:, :], AF.Relu, r=[pb], w=[r_])
                        fw.tt('dve', a16[:, j, half * 512:(half + 1) * 512], r_[:], r_[:], ALU.mult, r=[r_], w=[a16])
                for dc in range(8):
                    wv = w2s[dc % 2]
                    fw.dma('sp', wv[:].rearrange("p j c -> p (j c)"), w2c.ap()[dc], r=[DB('w2c', dc)], w=[wv])
                    for half in range(2):
                        pb = ps[4 + (pi % 2)]; pi += 1
                        for j in range(32):
                            fw.mm(pb[:, :], wv[:, j, :], a16[:, j, half * 512:(half + 1) * 512], start=(j == 0), stop=(j == 31),
                                  r=[wv, a16], w=[pb])
                        xv = x32[:, dc, half * 512:(half + 1) * 512]
                        fw.stt('dve', xv, pb[:, :], modT[:, 40 + dc, c:c + 1], xv, ALU.mult, ALU.add, r=[pb, modT, x32], w=[x32])
                fw.dma('act', xs_v[:, :, g0:g0 + 1024], x32[:], r=[x32], w=tls)
            fw.emit()

    with contextlib.ExitStack() as st:
        xf = [alloc(st, "f_x%d" % i, [128, 8, 128]) for i in range(2)]
        sqf = alloc(st, "f_sq", [128, 8, 128], BF16)
        rsf = alloc(st, "f_rs", [128, 128])
        yo = [alloc(st, "f_y%d" % i, [128, 1024]) for i in range(2)]
        for t in range(0 if SKIPA else NTL):
            x = xf[t % 2]; yy = yo[t % 2]
            fw.dma('sp', x[:], xs_v[:, :, t * 128:(t + 1) * 128], r=[DB('xs', t)], w=[x])
            fw.act(sqf[:], x[:], AF.Square, r=[x], w=[sqf])
            pst = ps[4]
            for kc in range(8):
                fw.mm(pst[:, 0:128], ones16, sqf[:, kc, :], start=(kc == 0), stop=(kc == 7), r=[c16, sqf], w=[pst])
            rstd(rsf[:], pst[:, 0:128], 1.0 / 1024, r=[pst], w=[rsf])
            for kc in range(8):
                fw.stt('dve', x[:, kc, :], x[:, kc, :], nfT[:, kc:kc + 1], rsf[:], ALU.mult, ALU.mult, r=[x, nfT, rsf], w=[x])
            for half in range(2):
                pb = ps[(t % 2) * 2 + half]
                for j in range(4):
                    kc = half * 4 + j
                    fw.tr(pb[:, j * 128:(j + 1) * 128], x[:, kc, :], ident32, r=[x, c32], w=[pb])
                fw.cp('dve' if half == 0 else 'act', yy[:, half * 512:(half + 1) * 512], pb[:, :], r=[pb], w=[yy])
            fw.dma('act', y.ap()[t * 128:(t + 1) * 128, :], yy[:], r=[yy])
        fw.emit()
    stack0.close()
    return nc


def make_consts():
    idx = np.arange(128)
    same = (idx[:, None] // 64) == (idx[None, :] // 64)
    c = np.zeros((NCONST, 128, 128), np.float32)
    c[C_ID] = np.eye(128)
    c[C_ONES] = 1.0
    c[C_NEGONES] = -1.0
    c[C_UF] = same & (idx[:, None] <= idx[None, :])
    c[C_UB] = same & (idx[:, None] >= idx[None, :])
    c[C_SAME] = same
    c[C_CH0] = (idx[:, None] < 64) & np.ones((1, 128), bool)
    c[C_CH1] = (idx[:, None] >= 64) & np.ones((1, 128), bool)
    strict_f = same & (idx[None, :] < idx[:, None])
    strict_b = same & (idx[None, :] > idx[:, None])
    c[C_NSTRF] = -strict_f.astype(np.float32)
    c[C_NSTRB] = -strict_b.astype(np.float32)
    incl_f = same & (idx[None, :] <= idx[:, None])
    incl_b = same & (idx[None, :] >= idx[:, None])
    c[C_INTF] = incl_f.T.astype(np.float32) * (128.0 ** -0.5)
    c[C_INTB] = incl_b.T.astype(np.float32) * (128.0 ** -0.5)
    for i in range(6):
        b = 2 ** i
        c[C_MB + i] = ((idx[:, None] // (2 * b)) == (idx[None, :] // (2 * b))) & ((idx[:, None] // b) != (idx[None, :] // b))
    return c


_NC_CACHE = {}


def prep_inputs(inp, core):
    b = core // 4
    f = lambda a: np.ascontiguousarray(np.asarray(a, dtype=np.float32))
    xin = np.concatenate([f(inp['x_prompt'])[4 * core:4 * core + 4].reshape(1024, 1024), f(inp['x_sample'])[b]], axis=0)
    w1 = f(inp['w_ff1']).reshape(4, 8, 128, 32, 128).transpose(0, 3, 2, 1, 4).reshape(4, 32, 128, 1024)
    w2 = f(inp['w_ff2']).reshape(4, 32, 128, 8, 128).transpose(0, 3, 2, 1, 4).reshape(4, 8, 128, 4096)
    return {
        'xin': f(xin), 'cond': f(np.stack([f(inp['c_ctx']), f(inp['c'])[b]])), 's0': f(f(inp['state_delta'])[b]),
        'w_mod': f(inp['w_mod']), 'b_mod': f(inp['b_mod']), 'norm_mix': f(inp['norm_mix']), 'w_in': f(inp['w_in']),
        'conv_qkv': f(inp['conv_qkv']), 'a_log': f(inp['a_log']).reshape(4, 8), 'dt_bias': f(inp['dt_bias']).reshape(4, 8),
        'o_norm': f(inp['o_norm']), 'wsT': f(np.transpose(f(inp['w_spatial']), (0, 1, 3, 2))), 'b_sp': f(inp['b_spatial']),
        'w_out': f(inp['w_out']), 'norm_ffn': f(inp['norm_ffn']), 'w_ff1b': f(w1), 'w_ff2b': f(w2),
        'norm_final': f(inp['norm_final']), 'consts': make_consts(),
    }


def kernel(**inputs):
    if 'nc' not in _NC_CACHE:
        _NC_CACHE['nc'] = build()
    nc = _NC_CACHE['nc']
    shared = None
    in_maps = []
    for core in range(8):
        m = prep_inputs(inputs, core) if shared is None else None
        if shared is None:
            shared = m
        else:
            m = dict(shared)
            b = core // 4
            f = lambda a: np.ascontiguousarray(np.asarray(a, dtype=np.float32))
            m['xin'] = np.concatenate([f(inputs['x_prompt'])[4 * core:4 * core + 4].reshape(1024, 1024), f(inputs['x_sample'])[b]], axis=0)
            m['cond'] = f(np.stack([f(inputs['c_ctx']), f(inputs['c'])[b]]))
            m['s0'] = f(f(inputs['state_delta'])[b])
        in_maps.append(m)
    res = run_bass_kernel_spmd(nc, in_maps, core_ids=list(range(8)))
    outs = res.results
    y_prompt = np.concatenate([np.asarray(outs[c]['y'])[:1024].reshape(4, 256, 1024) for c in range(8)], axis=0)
    y_sample = np.stack([np.asarray(outs[0]['y'])[1024:], np.asarray(outs[4]['y'])[1024:]], axis=0)
    nsd = np.concatenate([np.asarray(outs[c]['ns']) for c in range(8)], axis=0)
    return (y_prompt.astype(np.float32), y_sample.astype(np.float32), nsd.astype(np.float32))
```

```python
import contextlib
import numpy as np
import concourse.bass as bass
import concourse.mybir as mybir
from concourse.bass_utils import run_bass_kernel_spmd

F32 = mybir.dt.float32
BF16 = mybir.dt.bfloat16
AF = mybir.ActivationFunctionType
ALU = mybir.AluOpType
AX = mybir.AxisListType

NT = 5120
NTL = NT // 128
EPS = 1e-6
LNSCALE = -0.5 * 4.852030263919617
SEQS = [(0, 256, 0), (256, 256, 0), (512, 256, 0), (768, 256, 0), (1024, 4096, 1)]
C_ID, C_ONES, C_UF, C_UB, C_SAME, C_CH0, C_CH1, C_NSTRF, C_NSTRB, C_INTF, C_INTB = range(11)
C_MB = 11
C_NEGONES = 17
NCONST = 18


class Buf:
    __slots__ = ('name', 'w', 'r', 'excl')

    def __init__(s, name=''):
        s.name = name
        s.w = None
        s.r = {}
        s.excl = False


class T:
    def __init__(s, t, name=''):
        s.t = t
        s.b = Buf(name)

    def __getitem__(s, k):
        return s.t[k]


def _b(x):
    return x.b if isinstance(x, T) else x


class FW:
    EPOCH = 8000
    NSW = 8
    NHW = 16
    MAXEP = 10
    SAME = ('act', 'dve', 'pool')

    def __init__(s, nc):
        s.nc = nc
        s.eng = {'pe': nc.tensor, 'act': nc.scalar, 'dve': nc.vector, 'pool': nc.gpsimd, 'sp': nc.sync}
        s.ops = {e: [] for e in s.eng}
        s.cnt = {e: 0 for e in s.eng}
        s.epoch = {e: 0 for e in s.eng}
        s.seen = {e: {} for e in s.eng}
        s.sems = {}
        for e in s.eng:
            for ep in range(s.MAXEP):
                k = (e, ep)
                s.sems[k] = nc.alloc_semaphore(name='s_%s_%d' % k)
        for cls, n in (('sw', s.NSW), ('hw', s.NHW)):
            for i in range(n):
                k = ('dma' + cls, i)
                s.sems[k] = nc.alloc_semaphore(name='s_%s_%d' % k)
        s.dma_i = {'sw': 0, 'hw': 0}
        s.dma_use = {'sw': [0] * s.NSW, 'hw': [0] * s.NHW}
        s.nops = 0

    def _deps(s, eng, reads, writes):
        need = {}

        def add(ev):
            if ev is None:
                return
            k, v = ev
            if need.get(k, 0) < v:
                need[k] = v
        for b in reads:
            b = _b(b)
            add(b.w)
            if b.excl:
                for k, v in b.r.items():
                    if k[0] != eng:
                        add((k, v))
        for b in writes:
            b = _b(b)
            add(b.w)
            for k, v in b.r.items():
                add((k, v))
        waits = []
        for k, v in need.items():
            if k[0] == eng and eng not in s.SAME:
                continue
            if s.seen[eng].get(k, 0) < v:
                s.seen[eng][k] = v
                waits.append((k, v))
        return waits

    def _commit(s, ev, reads, writes):
        k, v = ev
        for b in writes:
            b = _b(b)
            b.w = ev
            b.r = {}
        wset = set(id(_b(b)) for b in writes)
        for b in reads:
            b = _b(b)
            if id(b) in wset:
                continue
            if b.r.get(k, 0) < v:
                b.r[k] = v

    def op(s, eng, fn, reads=(), writes=()):
        waits = s._deps(eng, reads, writes)
        if s.cnt[eng] >= s.EPOCH:
            s.epoch[eng] += 1
            s.cnt[eng] = 0
        s.cnt[eng] += 1
        key = (eng, s.epoch[eng])
        val = s.cnt[eng]
        s.ops[eng].append((waits, fn, key, 1))
        s.nops += 1
        s._commit((key, val), reads, writes)

    def dma(s, q, out, in_, r=(), w=(), **kw):
        cls = 'sw' if q == 'pool' else 'hw'
        n = s.NSW if cls == 'sw' else s.NHW
        slot = s.dma_i[cls] % n
        s.dma_i[cls] += 1
        prev = s.dma_use[cls][slot]
        key = ('dma' + cls, slot)
        waits = s._deps(q, r, w)
        if prev > 0 and s.seen[q].get(key, 0) < 16 * prev:
            s.seen[q][key] = 16 * prev
            waits.append((key, 16 * prev))
        s.dma_use[cls][slot] = prev + 1
        s.ops[q].append((waits, lambda e: e.dma_start(out=out, in_=in_, **kw), key, 16))
        s.nops += 1
        s._commit((key, 16 * (prev + 1)), r, w)

    def barrier(s):
        evs = []
        for e in s.eng:
            for ep in range(s.epoch[e] + 1):
                c = s.cnt[e] if ep == s.epoch[e] else s.EPOCH
                if c > 0:
                    evs.append(((e, ep), c))
        for cls in ('sw', 'hw'):
            for slot in range(len(s.dma_use[cls])):
                if s.dma_use[cls][slot] > 0:
                    evs.append((('dma' + cls, slot), 16 * s.dma_use[cls][slot]))
        for e in s.eng:
            waits = []
            for k, v in evs:
                if s.seen[e].get(k, 0) < v:
                    s.seen[e][k] = v
                    waits.append((k, v))
            if waits:
                s.ops[e].append((waits, None, None, 0))

    def emit(s):
        s.barrier()
        nc = s.nc
        keys = set()
        for e in s.ops:
            for (w, f, k, i) in s.ops[e]:
                if k is not None:
                    keys.add(k)
                for (kk, v) in w:
                    keys.add(kk)
        for k in sorted(keys, key=str):
            if k not in s.sems:
                s.sems[k] = nc.alloc_semaphore(name='s_' + '_'.join(str(x) for x in k))
        ops = s.ops
        s.ops = {e: [] for e in s.eng}
        with nc.Block() as block:
            def mk(ename):
                def body(e):
                    for waits, fn, key, inc in ops[ename]:
                        for k, v in waits:
                            e.wait_ge(s.sems[k], v)
                        if fn is not None:
                            fn(e).then_inc(s.sems[key], inc)
                return body
            block.tensor(mk('pe'))
            block.scalar(mk('act'))
            block.vector(mk('dve'))
            block.gpsimd(mk('pool'))
            block.sync(mk('sp'))

    def mm(s, out, lhsT, rhs, start=True, stop=True, r=(), w=()):
        s.op('pe', lambda e: e.matmul(out, lhsT, rhs, start=start, stop=stop), r, w)

    def tr(s, out, in_, ident, r=(), w=()):
        s.op('pe', lambda e: e.transpose(out, in_, ident), r, w)

    def act(s, out, in_, func, bias=0.0, scale=1.0, r=(), w=()):
        s.op('act', lambda e: e.activation(out, in_, func, bias=bias, scale=scale), r, w)

    def tt(s, eng, out, in0, in1, op, r=(), w=()):
        s.op(eng, lambda e: e.tensor_tensor(out, in0, in1, op), r, w)

    def ts(s, eng, out, in0, s1, s2, op0, op1=None, r=(), w=()):
        if op1 is None:
            s.op(eng, lambda e: e.tensor_scalar(out, in0, s1, None, op0), r, w)
        else:
            s.op(eng, lambda e: e.tensor_scalar(out, in0, s1, s2, op0, op1), r, w)

    def stt(s, eng, out, in0, scalar, in1, op0, op1, r=(), w=()):
        s.op(eng, lambda e: e.scalar_tensor_tensor(out, in0, scalar, in1, op0, op1), r, w)

    def cp(s, eng, out, in_, r=(), w=()):
        if eng == 'act':
            s.op(eng, lambda e: e.copy(out, in_), r, w)
        else:
            s.op(eng, lambda e: e.tensor_copy(out, in_), r, w)

    def memset(s, eng, ap, val, w=()):
        s.op(eng, lambda e: e.memset(ap, val), (), w)

    def recip(s, out, in_, r=(), w=()):
        s.op('dve', lambda e: e.reciprocal(out, in_), r, w)


def build(L=4, dbg=False, stop=None):
    nc = bass.Bass("TRN2", target_bir_lowering=False)
    fw = FW(nc)

    def din(name, shape, dt=F32):
        return nc.dram_tensor(name, list(shape), dt, kind="ExternalInput")

    def dscr(name, shape, dt=F32):
        return nc.dram_tensor(name, list(shape), dt, kind=("ExternalOutput" if dbg else "Internal"))

    xin = din("xin", [NT, 1024]); cond = din("cond", [2, 1024]); s0 = din("s0", [4, 2, 4, 128, 128])
    w_mod = din("w_mod", [4, 1024, 6144]); b_mod = din("b_mod", [4, 6144]); norm_mix = din("norm_mix", [4, 1024])
    w_in = din("w_in", [4, 1024, 3088]); conv_qkv = din("conv_qkv", [4, 3, 1536]); a_log = din("a_log", [4, 8])
    dt_bias = din("dt_bias", [4, 8]); o_norm = din("o_norm", [4, 128]); wsT = din("wsT", [4, 4, 128, 128])
    b_sp = din("b_sp", [4, 4, 128]); w_out = din("w_out", [4, 1024, 1024]); norm_ffn = din("norm_ffn", [4, 1024])
    w_ff1b = din("w_ff1b", [4, 32, 128, 1024]); w_ff2b = din("w_ff2b", [4, 8, 128, 4096])
    norm_final = din("norm_final", [1024]); consts = din("consts", [NCONST, 128, 128])
    y = nc.dram_tensor("y", [NT, 1024], F32, kind="ExternalOutput")
    ns = nc.dram_tensor("ns", [4, 4, 2, 4, 128, 128], F32, kind="ExternalOutput")
    xs = dscr("xs", [1024, NT]); qkvs = dscr("qkvs", [1536, NT]); gates = dscr("gates", [512, NT])
    obs = dscr("obs", [512, NT], BF16); bgs = dscr("bgs", [NT, 16]); blobs = dscr("blobs", [NTL, 128, 2048], BF16)
    ofs = dscr("ofs", [NTL, 128, 512]); ofs2 = dscr("ofs2", [NTL, 128, 512])
    w1c = nc.dram_tensor("w1c", [32, 128, 1024], BF16, kind="Internal"); w2c = nc.dram_tensor("w2c", [8, 128, 4096], BF16, kind="Internal")

    dbufs = {}

    def DB(name, i):
        k = (name, i)
        if k not in dbufs:
            dbufs[k] = Buf(name + str(i))
        return dbufs[k]

    def DBs(name, t0, t1):
        return [DB(name, t) for t in range(max(t0, 0), min(t1, NTL))]

    xs_v = xs.ap().rearrange("(kc p) t -> p kc t", p=128)
    qkvs_v = qkvs.ap().rearrange("(n p) t -> p n t", p=128)
    gates_v = gates.ap().rearrange("(n p) t -> p n t", p=128)
    obs_v = obs.ap().rearrange("(n p) t -> p n t", p=128)

    stack0 = contextlib.ExitStack()

    uid = [0]

    def alloc(stack, name, shape, dt=F32):
        uid[0] += 1
        name = "%s_u%d" % (name, uid[0])
        return T(stack.enter_context(nc.sbuf_tensor(name, list(shape), dt)), name)

    ps = [T(stack0.enter_context(nc.psum_tensor("psb%d" % i, [128, 512], F32)), "ps%d" % i) for i in range(8)]
    for p_ in ps:
        p_.b.excl = True

    def ps3(i, h=4):
        return ps[i].t[:].rearrange("p (h c) -> p h c", h=h)

    def psbf(i):
        return ps[i].t[:].bitcast(BF16)

    c32 = alloc(stack0, "c32", [128, NCONST, 128], F32)
    c16 = alloc(stack0, "c16", [128, NCONST, 128], BF16)
    fw.dma('sp', c32[:], consts.ap().rearrange("n p c -> p n c"), w=[c32])
    fw.dma('pool', c16[:], consts.ap().rearrange("n p c -> p n c"), w=[c16])
    epsT = alloc(stack0, "epsT", [128, 1], F32)
    fw.memset('pool', epsT[:], EPS, w=[epsT])
    condT = alloc(stack0, "condT", [128, 8, 2], F32)
    scT = alloc(stack0, "scT", [128, 8, 2], BF16)
    for c in range(2):
        fw.dma('sp', condT[:, :, c], cond.ap()[c].rearrange("(kc p) -> p kc", p=128), w=[condT],
               allow_slow_non_contiguous=True)
    fw.act(scT[:], condT[:], AF.Silu, r=[condT], w=[scT])
    nfT = alloc(stack0, "nfT", [128, 8], F32)
    fw.dma('sp', nfT[:], norm_final.ap().rearrange("(kc p) -> p kc", p=128), w=[nfT], allow_slow_non_contiguous=True)
    nmT = alloc(stack0, "nmT", [128, 4, 8], F32); nffT = alloc(stack0, "nffT", [128, 4, 8], F32)
    bmT = alloc(stack0, "bmT", [128, 4, 48], F32); cwT = alloc(stack0, "cwT", [128, 4, 12, 3], F32)
    onT = alloc(stack0, "onT", [128, 4], F32)
    dtb = alloc(stack0, "dtb", [128, 4, 8], F32); nA = alloc(stack0, "nA", [128, 4, 8], F32)
    ws16 = alloc(stack0, "ws16", [128, 4, 4, 128], BF16); bs16 = alloc(stack0, "bs16", [1, 4, 4, 128], BF16)
    for l in range(L):
        fw.dma('sp', nmT[:, l, :], norm_mix.ap()[l].rearrange("(kc p) -> p kc", p=128), w=[nmT], allow_slow_non_contiguous=True)
        fw.dma('sp', nffT[:, l, :], norm_ffn.ap()[l].rearrange("(kc p) -> p kc", p=128), w=[nffT], allow_slow_non_contiguous=True)
        fw.dma('sp', bmT[:, l, :], b_mod.ap()[l].rearrange("(n p) -> p n", p=128), w=[bmT], allow_slow_non_contiguous=True)
        for wi in range(3):
            fw.dma('sp', cwT[:, l, :, wi], conv_qkv.ap()[l, wi].rearrange("(n p) -> p n", p=128), w=[cwT], allow_slow_non_contiguous=True)
        fw.dma('sp', onT[:, l:l + 1], o_norm.ap()[l].rearrange("(p o) -> p o", o=1), w=[onT], allow_slow_non_contiguous=True)
        fw.dma('sp', dtb[:, l, :], dt_bias.ap()[l].partition_broadcast(128), w=[dtb])
        fw.dma('sp', nA[:, l, :], a_log.ap()[l].partition_broadcast(128), w=[nA])
        fw.dma('pool', ws16[:, l, :, :], wsT.ap()[l].rearrange("g s t -> s g t"), w=[ws16])
        fw.dma('pool', bs16[:, l, :, :], b_sp.ap()[l:l + 1], w=[bs16])
    fw.act(nA[:, 0:L, :], nA[:, 0:L, :], AF.Exp, r=[nA], w=[nA])
    fw.ts('dve', nA[:, 0:L, :], nA[:, 0:L, :], -1.0, None, ALU.mult, r=[nA], w=[nA])

    ident32 = c32[:, C_ID, :]
    ident16 = c16[:, C_ID, :]
    ones16 = c16[:, C_ONES, :]
    ones32 = c32[:, C_ONES, :]

    def rstd(out, in_, scale, r, w):
        fw.act(out, in_, AF.Sqrt, bias=epsT[:, 0:1], scale=scale, r=list(r) + [epsT], w=w)
        fw.recip(out, out, r=w, w=w)

    import os
    SKIPA = os.environ.get('SKIPA') == '1'
    def run_jobs(factories, NJ, stagger=0):
        pending = list(factories)
        active = [None] * NJ
        delay = [sl * stagger for sl in range(NJ)]
        while pending or any(a is not None for a in active):
            for sl in range(NJ):
                if delay[sl] > 0:
                    delay[sl] -= 1
                    continue
                if active[sl] is None and pending:
                    active[sl] = pending.pop(0)(sl)
                if active[sl] is not None:
                    try:
                        next(active[sl])
                    except StopIteration:
                        active[sl] = None

    with contextlib.ExitStack() as st:
        xt = [alloc(st, "x0t%d" % i, [128, 1024]) for i in range(2)]
        xo = [alloc(st, "x0o%d" % i, [128, 8, 128]) for i in range(2)]
        for t in range(0 if SKIPA else NTL):
            a = xt[t % 2]; o = xo[t % 2]
            fw.dma('sp', a[:], xin.ap()[t * 128:(t + 1) * 128, :], w=[a])
            for half in range(2):
                pb = ps[(t % 2) * 2 + half]
                for j in range(4):
                    kc = half * 4 + j
                    fw.tr(pb[:, j * 128:(j + 1) * 128], a[:, kc * 128:(kc + 1) * 128], ident32, r=[a, c32], w=[pb])
                fw.cp('dve' if half == 0 else 'act', o[:, half * 4:(half + 1) * 4, :], ps3((t % 2) * 2 + half), r=[pb], w=[o])
            fw.dma('act', xs_v[:, :, t * 128:(t + 1) * 128], o[:], r=[o], w=[DB('xs', t)])
        fw.emit()

    modT = alloc(stack0, "modT", [128, 48, 2], F32)
    w1T = alloc(stack0, "w1T", [128, 8, 2], F32)
    w2T = alloc(stack0, "w2T", [128, 8, 2], F32)

    for l in range(L):
        if stop == 'x0':
            break
        with contextlib.ExitStack() as st:
            wm = [alloc(st, "wm%d" % i, [128, 8, 512], BF16) for i in range(2)]
            pm = ps[0]
            for nb in range(0 if SKIPA else 12):
                wb = wm[nb % 2]
                fw.dma('pool', wb[:], w_mod.ap()[l].rearrange("(kc p) n -> p kc n", p=128)[:, :, nb * 512:(nb + 1) * 512], w=[wb])
                for cc in range(4):
                    n = nb * 4 + cc
                    for kc in range(8):
                        fw.mm(pm[:, n * 2:n * 2 + 2], wb[:, kc, cc * 128:(cc + 1) * 128], scT[:, kc, :],
                              start=(kc == 0), stop=(kc == 7), r=[wb, scT], w=[pm])
            pmv = pm.t[:, 0:96].rearrange("p (n c) -> p n c", c=2)
            for c in range(0 if SKIPA else 2):
                fw.tt('dve', modT[:, :, c], pmv[:, :, c], bmT[:, l, :], ALU.add, r=[pm, bmT], w=[modT])
                fw.stt('dve', w1T[:, :, c], modT[:, 8:16, c], 1.0, nmT[:, l, :], ALU.add, ALU.mult, r=[modT, nmT], w=[w1T])
                fw.stt('dve', w2T[:, :, c], modT[:, 32:40, c], 1.0, nffT[:, l, :], ALU.add, ALU.mult, r=[modT, nffT], w=[w2T])
            fw.emit()

        if stop == 'm':
            break
        with contextlib.ExitStack() as st:
            win = alloc(st, "win", [128, 8, 3088], BF16)
            w_in_v = w_in.ap()[l].rearrange("(kc p) n -> p kc n", p=128)
            for kc in range(8):
                fw.dma('pool', win[:, kc, :], w_in_v[:, kc, :], w=[win])
            NA = 2
            def perA(name, shape, dt=F32, k=1):
                return [[alloc(st, "%s%d_%d" % (name, i, j), shape, dt) for j in range(k)] for i in range(NA)]
            xT = perA("a_x", [128, 8, 512]); sqA = perA("a_sq", [128, 8, 512], BF16); rsA = perA("a_rs", [128, 512])
            tmpA = perA("a_tmp", [128, 512], F32, 2); hA = perA("a_h", [128, 8, 512], BF16); stgA = perA("a_stg", [128, 512], F32, 4)
            uA = perA("a_u", [128, 4, 512]); bgA = perA("a_bg", [128, 16], F32, 2); vgfA = perA("a_vgf", [128, 4, 128], F32, 2)
            vgnA = perA("a_vgn", [128, 4, 128], BF16, 2); st4A = perA("a_st4", [128, 4], F32, 2); obA = perA("a_ob", [128, 4, 128], BF16, 2)
            COLS = [n * 128 for n in range(12)] + [1536 + n * 128 for n in range(4)] + [2064 + n * 128 for n in range(4)]

            def ajob(gi):
                def gen(sl):
                    base = 4 * sl
                    g0 = gi * 512
                    c = 0 if g0 < 1024 else 1
                    x = xT[sl][0]; sq = sqA[sl][0]; rs = rsA[sl][0]; h16 = hA[sl][0]; uT = uA[sl][0]
                    tls = DBs('xs', gi * 4, gi * 4 + 4)
                    fw.dma('sp', x[:], xs_v[:, :, g0:g0 + 512], r=tls, w=[x])
                    yield
                    fw.act(sq[:], x[:], AF.Square, r=[x], w=[sq])
                    yield
                    pst = ps[base + 2]
                    for kc in range(8):
                        fw.mm(pst[:, :], ones16, sq[:, kc, :], start=(kc == 0), stop=(kc == 7), r=[c16, sq], w=[pst])
                    yield
                    fw.act(rs[:], pst[:, :], AF.Sqrt, bias=epsT[:, 0:1], scale=1.0 / 1024, r=[pst, epsT], w=[rs])
                    yield
                    fw.recip(rs[:], rs[:], r=[rs], w=[rs])
                    yield
                    for kc in range(8):
                        tm = tmpA[sl][kc % 2]
                        fw.stt('dve', tm[:], x[:, kc, :], w1T[:, kc, c:c + 1], rs[:], ALU.mult, ALU.mult, r=[x, w1T, rs], w=[tm])
                        fw.act(h16[:, kc, :], tm[:], AF.Identity, bias=modT[:, kc, c:c + 1], r=[tm, modT], w=[h16])
                        if kc % 2 == 1:
                            yield
                    si = 0
                    for n in range(20):
                        pb = ps[base + (n % 2)]
                        col = COLS[n]
                        for kc in range(8):
                            fw.mm(pb[:, :], win[:, kc, col:col + 128], h16[:, kc, :], start=(kc == 0), stop=(kc == 7),
                                  r=[win, h16], w=[pb])
                        if n < 12:
                            sg = stgA[sl][si % 4]; si += 1
                            fw.cp('dve' if n % 2 == 0 else 'act', sg[:], pb[:, :], r=[pb], w=[sg])
                            fw.dma('sp', qkvs_v[:, n, g0:g0 + 512], sg[:], r=[sg], w=DBs('qkvs', gi * 4, gi * 4 + 4))
                        elif n < 16:
                            sg = stgA[sl][si % 4]; si += 1
                            fw.act(sg[:], pb[:, :], AF.Silu, r=[pb], w=[sg])
                            fw.dma('sp', gates_v[:, n - 12, g0:g0 + 512], sg[:], r=[sg], w=DBs('gates', gi * 4, gi * 4 + 4))
                        else:
                            fw.act(uT[:, n - 16, :], pb[:, :], AF.Gelu, r=[pb], w=[uT])
                        yield
                    for tt_ in range(4):
                        t = gi * 4 + tt_
                        tok = slice(tt_ * 128, (tt_ + 1) * 128)
                        pab = ps[base + 2]; pvg = ps[base + 3]
                        b = bgA[sl][tt_ % 2]; vf = vgfA[sl][tt_ % 2]; vn = vgnA[sl][tt_ % 2]; s4 = st4A[sl][tt_ % 2]; ob = obA[sl][tt_ % 2]
                        for kc in range(8):
                            fw.mm(pab[:, 0:16], h16[:, kc, tok], win[:, kc, 2048:2064], start=(kc == 0), stop=(kc == 7),
                                  r=[h16, win], w=[pab])
                        for kc in range(8):
                            fw.mm(pvg[:, :], h16[:, kc, tok], win[:, kc, 2576:3088], start=(kc == 0), stop=(kc == 7),
                                  r=[h16, win], w=[pvg])
                        yield
                        fw.act(b[:, 0:8], pab[:, 0:8], AF.Sigmoid, r=[pab], w=[b])
                        fw.act(vf[:], ps3(base + 3), AF.Gelu, r=[pvg], w=[vf])
                        yield
                        fw.tt('dve', b[:, 8:16], pab[:, 8:16], dtb[:, l, :], ALU.add, r=[pab, dtb], w=[b])
                        fw.op('dve', lambda e, s4=s4, vf=vf: e.reduce_sum(s4[:], vf[:], AX.X), [vf], [s4])
                        yield
                        fw.act(b[:, 8:16], b[:, 8:16], AF.Exp, r=[b], w=[b])
                        fw.ts('dve', s4[:], s4[:], -1.0 / 128, None, ALU.mult, r=[s4], w=[s4])
                        yield
                        fw.act(b[:, 8:16], b[:, 8:16], AF.Ln, bias=1.0, r=[b], w=[b])
                        fw.tt('dve', vf[:], vf[:], s4[:].unsqueeze(2).to_broadcast([128, 4, 128]), ALU.add, r=[vf, s4], w=[vf])
                        yield
                        fw.tt('dve', b[:, 8:16], b[:, 8:16], nA[:, l, :], ALU.mult, r=[b, nA], w=[b])
                        sg = stgA[sl][si % 4]; si += 1
                        sg3 = sg.t[:].rearrange("p (h c) -> p h c", h=4)
                        fw.tt('pool', sg3, vf[:], vf[:], ALU.mult, r=[vf], w=[sg])
                        yield
                        fw.dma('act', bgs.ap()[t * 128:(t + 1) * 128, :], b[:], r=[b], w=[DB('bgs', t)])
                        fw.op('dve', lambda e, s4=s4, sg3=sg3: e.reduce_sum(s4[:], sg3, AX.X), [sg], [s4])
                        yield
                        fw.act(s4[:], s4[:], AF.Sqrt, bias=epsT[:, 0:1], scale=1.0 / 128, r=[s4, epsT], w=[s4])
                        yield
                        fw.recip(s4[:], s4[:], r=[s4], w=[s4])
                        yield
                        fw.tt('dve', vn[:], vf[:], s4[:].unsqueeze(2).to_broadcast([128, 4, 128]), ALU.mult, r=[vf, s4], w=[vn])
                        yield
                        pmx = ps[base + 2]
                        for g in range(4):
                            fw.mm(pmx[:, g * 128:(g + 1) * 128], vn[:, g, :], ws16[:, l, g, :], start=True, stop=False, r=[vn, ws16], w=[pmx])
                            fw.mm(pmx[:, g * 128:(g + 1) * 128], c16[0:1, C_ONES, :], bs16[0:1, l, g, :], start=False, stop=True,
                                  r=[c16, bs16], w=[pmx])
                        yield
                        fw.tt('dve', ob[:], uT[:, :, tok], ps3(base + 2), ALU.mult, r=[uT, pmx], w=[ob])
                        yield
                        fw.dma('act', obs_v[:, :, t * 128:(t + 1) * 128], ob[:], r=[ob], w=[DB('obs', t)])
                return gen
            run_jobs([ajob(gi) for gi in range(0 if SKIPA else NT // 512)], NA, stagger=40)
            fw.emit()

        if stop == 'a':
            break
        import os
        with contextlib.ExitStack() as st:
            NB0 = 3
            def per0(name, shape, dt=F32):
                return [alloc(st, "%s%d" % (name, i), shape, dt) for i in range(NB0)]
            pre = per0("b_pre", [128, 12, 130]); cv = per0("b_cv", [128, 12, 128]); cv2 = per0("b_cv2", [128, 12, 128])
            cv3 = per0("b_cv3", [128, 12, 128]); sq16 = per0("b_sq16", [128, 8, 128], BF16); rn = per0("b_rn", [128, 8, 128])
            vT16 = per0("b_vT16", [128, 4, 128], BF16); blob0 = per0("b_blob", [128, 4, 4, 128], BF16)
            cw = cwT.t[:, l, :, :]

            def b0job(t, sstart, slen):
                def gen(sl):
                    p = pre[sl]; c_ = cv[sl]; c2 = cv2[sl]; c3 = cv3[sl]; bl = blob0[sl]; sq_ = sq16[sl]; rn_ = rn[sl]; vt_ = vT16[sl]
                    ia = 2 * sl; ib = 2 * sl + 1
                    t0 = t * 128
                    lo = max(t0 - 1, sstart); hi = min(t0 + 129, sstart + slen)
                    if lo > t0 - 1:
                        fw.memset('pool', p[:, :, 0:1], 0.0, w=[p])
                    if hi < t0 + 129:
                        fw.memset('pool', p[:, :, 129:130], 0.0, w=[p])
                    for n3 in range(3):
                        fw.dma('sp', p[:, n3 * 4:(n3 + 1) * 4, lo - (t0 - 1):hi - (t0 - 1)], qkvs_v[:, n3 * 4:(n3 + 1) * 4, lo:hi],
                               r=DBs('qkvs', t - 1, t + 2), w=[p])
                    yield
                    fw.tt('dve', c_[:], p[:, :, 0:128], cw[:, :, 0:1].to_broadcast([128, 12, 128]), ALU.mult, r=[p, cwT], w=[c_])
                    fw.tt('pool', c2[:], p[:, :, 1:129], cw[:, :, 1:2].to_broadcast([128, 12, 128]), ALU.mult, r=[p, cwT], w=[c2])
                    fw.tt('pool', c3[:], p[:, :, 2:130], cw[:, :, 2:3].to_broadcast([128, 12, 128]), ALU.mult, r=[p, cwT], w=[c3])
                    yield
                    fw.tt('dve', c_[:], c_[:], c2[:], ALU.add, r=[c_, c2], w=[c_])
                    yield
                    fw.tt('dve', c_[:], c_[:], c3[:], ALU.add, r=[c_, c3], w=[c_])
                    yield
                    fw.act(c_[:], c_[:], AF.Silu, r=[c_], w=[c_])
                    yield
                    fw.act(sq_[:], c_[:, 0:8, :], AF.Square, r=[c_], w=[sq_])
                    fw.cp('act', vt_[:], c_[:, 8:12, :], r=[c_], w=[vt_])
                    yield
                    fw.mm(ps[ia][:, :], ones16, sq_[:, 0:4, :], r=[c16, sq_], w=[ps[ia]])
                    fw.mm(ps[ib][:, :], ones16, sq_[:, 4:8, :], r=[c16, sq_], w=[ps[ib]])
                    yield
                    fw.act(rn_[:, 0:4, :], ps3(ia), AF.Sqrt, bias=epsT[:, 0:1], scale=1.0, r=[ps[ia], epsT], w=[rn_])
                    fw.act(rn_[:, 4:8, :], ps3(ib), AF.Sqrt, bias=epsT[:, 0:1], scale=1.0, r=[ps[ib], epsT], w=[rn_])
                    yield
                    fw.recip(rn_[:], rn_[:], r=[rn_], w=[rn_])
                    yield
                    fw.tt('dve', bl[:, 0:2, :, :], c_[:, 0:8, :].rearrange("p (a h) c -> p a h c", a=2),
                          rn_[:].rearrange("p (a h) c -> p a h c", a=2), ALU.mult, r=[c_, rn_], w=[bl])
                    yield
                    pv = psbf(ia)
                    for h in range(4):
                        fw.tr(pv[:, h * 128:(h + 1) * 128], bl[:, 1, h, :], ident16, r=[bl, c16], w=[ps[ia]])
                        fw.tr(pv[:, 512 + h * 128:512 + (h + 1) * 128], vt_[:, h, :], ident16, r=[vt_, c16], w=[ps[ia]])
                    yield
                    fw.cp('dve', bl[:, 2:4, :, :], pv.rearrange("p (a h c) -> p a h c", a=2, h=4), r=[ps[ia]], w=[bl])
                    yield
                    fw.dma('act', blobs.ap()[t], bl[:].rearrange("p a h c -> p (a h c)"), r=[bl], w=[DB('blobs', t)])
                return gen
            jobs0 = []
            for (sstart, slen, cidx) in SEQS:
                for t in range(sstart // 128, (sstart + slen) // 128):
                    jobs0.append(b0job(t, sstart, slen))
            run_jobs(jobs0, NB0, stagger=4)
            fw.emit()

        if stop == 'b0':
            break
        with contextlib.ExitStack() as st:
            NJ = 4
            def per(name, shape, dt=F32):
                return [alloc(st, "%s%d" % (name, i), shape, dt) for i in range(NJ)]
            blobj = per("j_blob", [128, 4, 4, 128], BF16)
            bgt = per("j_bg", [128, 16]); Gs = per("j_Gs", [128, 16]); eG = per("j_eG", [128, 16]); sck = per("j_sck", [128, 4])
            gpad = per("j_gpad", [128, 128])
            for i in range(NJ):
                fw.memset('pool', gpad[i][:], 0.0, w=[gpad[i]])
            for j0 in range(0, 32, 4):
                fw.dma('pool', w1c.ap()[j0:j0 + 4], w_ff1b.ap()[l, j0:j0 + 4], w=[DB('w1c', j0 // 4)])
            for dc in range(8):
                fw.dma('pool', w2c.ap()[dc], w_ff2b.ap()[l, dc], w=[DB('w2c', dc)])
            Ugj = per("j_Ug", [128, 4, 128]); decj = per("j_dec", [128, 4, 128]); decTj = per("j_decT", [128, 4, 128])
            eGBj = per("j_eGB", [128, 4, 128]); nm32j = per("j_nm32", [128, 4, 128])
            Pj = per("j_P", [128, 4, 128], BF16); Qj = per("j_Q", [128, 4, 128], BF16); QMj = per("j_QM", [128, 5, 4, 128], BF16)
            TT16j = per("j_TT16", [128, 4, 128], BF16)
            T16j = per("j_T16", [128, 4, 128], BF16); X16j = per("j_X16", [128, 4, 128], BF16)
            vbj = per("j_vb", [128, 4, 128], BF16); kbgj = per("j_kbg", [128, 4, 128], BF16)
            kdj = per("j_kd", [128, 4, 128], BF16); wTj = per("j_wT", [128, 4, 128], BF16)
            qkj = per("j_qk", [128, 4, 128], BF16); qdj = per("j_qd", [128, 4, 128], BF16)
            u32j = per("j_u", [128, 4, 128]); vnj = per("j_vn", [128, 4, 128], BF16); oTj = per("j_oT", [128, 4, 128])
            NCH = 8
            S32c = [alloc(st, "j_S32_%d" % i, [128, 4, 128]) for i in range(NCH)]
            S16c = [alloc(st, "j_S16_%d" % i, [128, 4, 128], BF16) for i in range(NCH)]
            negones32 = c32[:, C_NEGONES, :]

            def job(slot, ci, t, d, cidx, is_last, seq_i, turn, my_idx):
                bl = blobj[slot]; b = bgt[slot]; G = Gs[slot]; e_ = eG[slot]; sk = sck[slot]; gp = gpad[slot]
                Ug = Ugj[slot]; dec = decj[slot]; decT = decTj[slot]; eGB = eGBj[slot]; nm32 = nm32j[slot]
                P = Pj[slot]; Q = Qj[slot]; QM = QMj[slot]; TT16 = TT16j[slot]; T16 = T16j[slot]; X16 = X16j[slot]
                vb16 = vbj[slot]; kbg16 = kbgj[slot]; kd = kdj[slot]; wT = wTj[slot]; qk = qkj[slot]; qd = qdj[slot]
                u = u32j[slot]; vn = vnj[slot]; oT = oTj[slot]
                pa = ps[2 * slot]; pb_ = ps[2 * slot + 1]; ia = 2 * slot; ib = 2 * slot + 1
                fw.dma('sp', bl[:].rearrange("p a h c -> p (a h c)"), blobs.ap()[t], r=[DB('blobs', t)], w=[bl])
                fw.dma('sp', b[:], bgs.ap()[t * 128:(t + 1) * 128, :], r=[DB('bgs', t)], w=[b])
                g4 = b[:, 8 + d * 4:12 + d * 4]; b4 = b[:, d * 4:d * 4 + 4]
                U = c32[:, C_UF + d, :]
                fw.cp('dve', gp[:, 0:4], g4, r=[b], w=[gp])
                fw.mm(pa[:, 0:128], U, gp[:], r=[c32, gp], w=[pa])
                fw.mm(pa[:, 128:256], c32[:, C_SAME, :], gp[:], r=[c32, gp], w=[pa])
                fw.mm(pa[:, 256:384], c32[:, C_CH0, :], gp[:], r=[c32, gp], w=[pa])
                fw.mm(pa[:, 384:512], c32[:, C_CH1, :], gp[:], r=[c32, gp], w=[pa])
                for h in range(4):
                    fw.act(Ug[:, h, :], U, AF.Identity, scale=g4[:, h:h + 1], r=[c32, b], w=[Ug])
                yield
                fw.cp('dve', G[:].rearrange("p (a b) -> p a b", a=4), ps3(ia)[:, :, 0:4], r=[pa], w=[G])
                fw.tt('dve', G[:, 4:8], G[:, 4:8], G[:, 0:4], ALU.subtract, r=[G], w=[G])
                fw.act(e_[:], G[:], AF.Exp, r=[G], w=[e_])
                for h in range(4):
                    fw.mm(pb_[:, h * 128:(h + 1) * 128], Ug[:, h, :], ones32, start=True, stop=False, r=[Ug, c32], w=[pb_])
                    fw.mm(pb_[:, h * 128:(h + 1) * 128], negones32, Ug[:, h, :], start=False, stop=True, r=[Ug, c32], w=[pb_])
                for h in range(4):
                    fw.mm(pa[:, h * 128:(h + 1) * 128], ones32, Ug[:, h, :], r=[Ug, c32], w=[pa])
                yield
                fw.tt('dve', sk[:], b4, e_[:, 0:4], ALU.mult, r=[b, e_], w=[sk])
                fw.act(dec[:], ps3(ib), AF.Relu, scale=-1.0, r=[pb_], w=[dec])
                fw.act(decT[:], ps3(ib), AF.Relu, scale=1.0, r=[pb_], w=[decT])
                fw.act(eGB[:], ps3(ia), AF.Exp, bias=LNSCALE, r=[pa], w=[eGB])
                for h in range(4):
                    fw.mm(pb_[:, h * 128:(h + 1) * 128], bl[:, 1, h, :], bl[:, 1, h, :], r=[bl], w=[pb_])
                yield
                fw.act(dec[:], dec[:], AF.Exp, scale=-1.0, r=[dec], w=[dec])
                fw.act(decT[:], decT[:], AF.Exp, scale=-1.0, r=[decT], w=[decT])
                ktok = bl[:, 2, :, :]; vtok = bl[:, 3, :, :]
                fw.tt('pool', vb16[:], vtok, b4.unsqueeze(2).to_broadcast([128, 4, 128]), ALU.mult, r=[bl, b], w=[vb16])
                fw.tt('pool', kbg16[:], ktok, sk[:].unsqueeze(2).to_broadcast([128, 4, 128]), ALU.mult, r=[bl, sk], w=[kbg16])
                fw.tt('pool', kd[:], ktok, e_[:, 4:8].unsqueeze(2).to_broadcast([128, 4, 128]), ALU.mult, r=[bl, e_], w=[kd])
                yield
                fw.tt('dve', nm32[:], ps3(ib), dec[:], ALU.mult, r=[pb_, dec], w=[nm32])
                fw.tt('pool', nm32[:], nm32[:], c32[:, C_NSTRF + d, :].unsqueeze(1).to_broadcast([128, 4, 128]), ALU.mult, r=[nm32, c32], w=[nm32])
                yield
                fw.tt('dve', P[:], nm32[:], b4.unsqueeze(2).to_broadcast([128, 4, 128]), ALU.mult, r=[nm32, b], w=[P])
                pqv = psbf(ia)
                for h in range(4):
                    fw.tr(pqv[:, h * 128:(h + 1) * 128], P[:, h, :], ident16, r=[P, c16], w=[pa])
                for h in range(4):
                    fw.mm(pb_[:, h * 128:(h + 1) * 128], bl[:, 1, h, :], bl[:, 0, h, :], r=[bl], w=[pb_])
                yield
                fw.cp('act', Q[:], pqv[:, 0:512].rearrange("p (h c) -> p h c", h=4), r=[pa], w=[Q])
                fw.tt('dve', nm32[:], ps3(ib), decT[:], ALU.mult, r=[pb_, decT], w=[nm32])
                fw.tt('dve', qd[:], eGB[:], bl[:, 0, :, :], ALU.mult, r=[eGB, bl], w=[qd])
                yield
                fw.tt('pool', qk[:], nm32[:], c32[:, C_INTF + d, :].unsqueeze(1).to_broadcast([128, 4, 128]), ALU.mult, r=[nm32, c32], w=[qk])
                for li in range(1, 6):
                    fw.tt('pool', QM[:, li - 1, :, :], Q[:], c16[:, C_MB + li, :].unsqueeze(1).to_broadcast([128, 4, 128]),
                          ALU.mult, r=[Q, c16], w=[QM])
                mb1 = c32[:, C_MB, :].unsqueeze(1).to_broadcast([128, 4, 128])
                idb = c32[:, C_ID, :].unsqueeze(1).to_broadcast([128, 4, 128])
                fw.tt('dve', eGB[:], P[:], mb1, ALU.mult, r=[P, c32], w=[eGB])
                fw.tt('pool', nm32[:], Q[:], mb1, ALU.mult, r=[Q, c32], w=[nm32])
                yield
                fw.tt('dve', T16[:], eGB[:], idb, ALU.add, r=[eGB, c32], w=[T16])
                fw.tt('pool', TT16[:], nm32[:], idb, ALU.add, r=[nm32, c32], w=[TT16])
                yield
                for li in range(1, 6):
                    for h in range(4):
                        fw.mm(pa[:, h * 128:(h + 1) * 128], QM[:, li - 1, h, :], T16[:, h, :], r=[QM, T16], w=[pa])
                    yield
                    fw.cp('act', X16[:], ps3(ia), r=[pa], w=[X16])
                    yield
                    if li < 5:
                        for h in range(4):
                            fw.mm(pb_[:, h * 128:(h + 1) * 128], TT16[:, h, :], X16[:, h, :], r=[TT16, X16], w=[pb_])
                    for h in range(4):
                        fw.mm(pa[:, h * 128:(h + 1) * 128], X16[:, h, :], TT16[:, h, :], r=[TT16, X16], w=[pa])
                    yield
                    if li < 5:
                        fw.tt('dve', T16[:], T16[:], ps3(ib), ALU.add, r=[T16, pb_], w=[T16])
                    fw.tt('dve', TT16[:], TT16[:], ps3(ia), ALU.add, r=[TT16, pa], w=[TT16])
                    yield
                for h in range(4):
                    fw.mm(pa[:, h * 128:(h + 1) * 128], kbg16[:, h, :], TT16[:, h, :], r=[kbg16, TT16], w=[pa])
                for h in range(4):
                    fw.mm(pb_[:, h * 128:(h + 1) * 128], TT16[:, h, :], vb16[:, h, :], r=[TT16, vb16], w=[pb_])
                yield
                fw.cp('act', wT[:], ps3(ia), r=[pa], w=[wT])
                fw.cp('act', u[:], ps3(ib), r=[pb_], w=[u])
                yield
                while turn[0] != my_idx:
                    yield
                Sf = S32c[ci]; Sb = S16c[ci]
                p6 = pa; p7 = pb_
                chs = (0, 1) if d == 0 else (1, 0)
                for ch in chs:
                    R = slice(ch * 64, (ch + 1) * 64)
                    for h in range(4):
                        fw.mm(p6[:, h * 128:(h + 1) * 128], wT[:, h, :], Sb[:, h, :], r=[wT, Sb], w=[p6])
                    yield
                    fw.tt('dve', vn[R, :, :], u[R, :, :], ps3(ia)[R, :, :], ALU.subtract, r=[u, p6], w=[vn])
                    yield
                    for h in range(4):
                        fw.mm(p7[:, h * 64:(h + 1) * 64], Sb[:, h, :], qd[:, h, R], start=True, stop=False, r=[Sb, qd], w=[p7])
                        fw.mm(p7[:, h * 64:(h + 1) * 64], vn[R, h, :], qk[R, h, R], start=False, stop=True, r=[vn, qk], w=[p7])
                    for h in range(4):
                        fw.mm(p6[:, h * 128:(h + 1) * 128], kd[R, h, :], vn[R, h, :], r=[kd, vn], w=[p6])
                    yield
                    fw.cp('act', oT[:, :, R], p7.t[:, 0:256].rearrange("p (h c) -> p h c", h=4), r=[p7], w=[oT])
                    for h in range(4):
                        fw.stt('dve', Sf[:, h, :], Sf[:, h, :], e_[:, 8 + ch * 4 + h:9 + ch * 4 + h], ps3(ia)[:, h, :],
                               ALU.mult, ALU.add, r=[Sf, e_, p6], w=[Sf])
                    yield
                    fw.cp('act', Sb[:], Sf[:], r=[Sf], w=[Sb])
                    yield
                turn[0] += 1
                fw.dma('act', (ofs if d == 0 else ofs2).ap()[t].rearrange("p (h c) -> p h c", h=4), oT[:], r=[oT],
                       w=[DB('ofs%d' % d, t)])
                if is_last and cidx == 0:
                    fw.dma('act', ns.ap()[seq_i, l, d].rearrange("h k v -> k h v"), Sf[:], r=[Sf])

            def run_chains(chain_defs):
                state = []
                for (ci, seq_i, sstart, slen, cidx, d) in chain_defs:
                    Sf = S32c[ci]; Sb = S16c[ci]
                    if cidx == 0:
                        fw.memset('pool', Sf[:], 0.0, w=[Sf])
                    else:
                        fw.dma('sp', Sf[:], s0.ap()[l, d].rearrange("h k v -> k h v"), w=[Sf])
                    fw.cp('act', Sb[:], Sf[:], r=[Sf], w=[Sb])
                    tiles = list(range(sstart // 128, (sstart + slen) // 128))
                    if d == 1:
                        tiles = tiles[::-1]
                    state.append(dict(ci=ci, seq_i=seq_i, cidx=cidx, d=d, tiles=tiles, nxt=0, turn=[0], inflight=0))
                active = [None] * NJ
                delay = [sl * 11 for sl in range(NJ)]
                while True:
                    progressed = False
                    for slot in range(NJ):
                        if delay[slot] > 0:
                            delay[slot] -= 1
                            progressed = True
                            continue
                        if active[slot] is None:
                            cands = [c for c in state if c['nxt'] < len(c['tiles']) and c['inflight'] < 2]
                            if cands:
                                c = min(cands, key=lambda c: (c['inflight'], c['nxt']))
                                k = c['nxt']; c['nxt'] += 1; c['inflight'] += 1
                                g = job(slot, c['ci'], c['tiles'][k], c['d'], c['cidx'], k == len(c['tiles']) - 1, c['seq_i'], c['turn'], k)
                                active[slot] = (g, c)
                        if active[slot] is not None:
                            progressed = True
                            g, c = active[slot]
                            try:
                                next(g)
                            except StopIteration:
                                c['inflight'] -= 1
                                active[slot] = None
                    if not progressed and all(c['nxt'] >= len(c['tiles']) for c in state):
                        break

            pr = []
            for seq_i, (sstart, slen, cidx) in enumerate(SEQS[:4]):
                for d in range(2):
                    pr.append((seq_i * 2 + d, seq_i, sstart, slen, cidx, d))
            run_chains(pr)
            sstart, slen, cidx = SEQS[4]
            run_chains([(0, 4, sstart, slen, cidx, 0), (1, 4, sstart, slen, cidx, 1)])
            fw.emit()

        if stop == 'b12':
            break
        with contextlib.ExitStack() as st:
            wo = alloc(st, "wo", [128, 8, 1024], BF16)
            fw.dma('pool', wo[:], w_out.ap()[l].rearrange("(kc p) n -> p kc n", p=128), w=[wo])
            NE = 3
            def per3(name, shape, dt=F32):
                return [alloc(st, "%s%d" % (name, i), shape, dt) for i in range(NE)]
            ofl = per3("e_ofl", [128, 4, 128]); ofl2 = per3("e_ofl2", [128, 4, 128]); osq = per3("e_osq", [128, 4, 128], BF16)
            ors = per3("e_ors", [128, 4, 128]); gTl = per3("e_gT", [128, 4, 128]); catA = per3("e_catA", [128, 4, 128], BF16)
            catB = per3("e_catB", [128, 4, 128], BF16); xres = per3("e_x", [128, 8, 128])

            def b3job(t):
                def gen(sl):
                    cidx = 0 if t < 8 else 1
                    of_ = ofl[sl]; of2 = ofl2[sl]; gt = gTl[sl]; cb = catB[sl]; xr = xres[sl]; ca = catA[sl]; oq = osq[sl]; orr = ors[sl]
                    ia = 2 * sl; ib = 2 * sl + 1
                    fw.dma('sp', of_[:], ofs.ap()[t].rearrange("p (h c) -> p h c", h=4), r=[DB('ofs0', t)], w=[of_])
                    fw.dma('sp', of2[:], ofs2.ap()[t].rearrange("p (h c) -> p h c", h=4), r=[DB('ofs1', t)], w=[of2])
                    fw.dma('sp', gt[:], gates_v[:, :, t * 128:(t + 1) * 128], r=[DB('gates', t)], w=[gt])
                    fw.dma('sp', cb[:], obs_v[:, :, t * 128:(t + 1) * 128], r=[DB('obs', t)], w=[cb])
                    fw.dma('sp', xr[:], xs_v[:, :, t * 128:(t + 1) * 128], r=[DB('xs', t)], w=[xr])
                    yield
                    fw.tt('pool', of_[:], of_[:], of2[:], ALU.add, r=[of_, of2], w=[of_])
                    yield
                    fw.act(oq[:], of_[:], AF.Square, r=[of_], w=[oq])
                    yield
                    fw.mm(ps[ia][:, :], ones16, oq[:], r=[c16, oq], w=[ps[ia]])
                    yield
                    fw.act(orr[:], ps3(ia), AF.Sqrt, bias=epsT[:, 0:1], scale=1.0 / 128, r=[ps[ia], epsT], w=[orr])
                    yield
                    fw.recip(orr[:], orr[:], r=[orr], w=[orr])
                    yield
                    fw.tt('dve', of_[:], of_[:], orr[:], ALU.mult, r=[of_, orr], w=[of_])
                    yield
                    fw.stt('dve', ca[:], of_[:], onT[:, l:l + 1], gt[:], ALU.mult, ALU.mult, r=[of_, onT, gt], w=[ca])
                    yield
                    for half in range(2):
                        pbn = ia if half == 0 else ib
                        pb2 = ps[pbn]
                        for jj in range(4):
                            dc = half * 4 + jj
                            for kc in range(8):
                                rhs = ca[:, kc, :] if kc < 4 else cb[:, kc - 4, :]
                                fw.mm(pb2[:, jj * 128:(jj + 1) * 128], wo[:, kc, dc * 128:(dc + 1) * 128], rhs,
                                      start=(kc == 0), stop=(kc == 7), r=[wo, ca, cb], w=[pb2])
                        yield
                    for half in range(2):
                        pbn = ia if half == 0 else ib
                        for jj in range(4):
                            dc = half * 4 + jj
                            fw.stt('dve', xr[:, dc, :], ps3(pbn)[:, jj, :], modT[:, 16 + dc, cidx:cidx + 1], xr[:, dc, :],
                                   ALU.mult, ALU.add, r=[ps[pbn], modT, xr], w=[xr])
                        yield
                    fw.dma('act', xs_v[:, :, t * 128:(t + 1) * 128], xr[:], r=[xr], w=[DB('xs', t)])
                return gen
            run_jobs([b3job(t) for t in range(NTL)], NE, stagger=4)
            fw.emit()

        if stop == 'b':
            break
        with contextlib.ExitStack() as st:
            x32 = alloc(st, "c_x", [128, 8, 1024])
            sqc = [alloc(st, "c_sq%d" % i, [128, 1024], BF16) for i in range(2)]
            rs = alloc(st, "c_rs", [128, 1024])
            tmp = [alloc(st, "c_tmp%d" % i, [128, 1024]) for i in range(2)]
            h16 = alloc(st, "c_h", [128, 8, 1024], BF16)
            a16 = alloc(st, "c_a", [128, 32, 1024], BF16)
            w1s = [alloc(st, "c_w1s%d" % i, [128, 2, 8, 128], BF16) for i in range(2)]
            w2s = [alloc(st, "c_w2s%d" % i, [128, 32, 128], BF16) for i in range(2)]
            rl = [alloc(st, "c_rl%d" % i, [128, 512]) for i in range(3)]
            ri = 0
            pi = 0
            for gi in range(NT // 1024):
                g0 = gi * 1024
                c = 0 if g0 < 1024 else 1
                tls = DBs('xs', gi * 8, gi * 8 + 8)
                fw.dma('sp', x32[:], xs_v[:, :, g0:g0 + 1024], r=tls, w=[x32])
                for kc in range(8):
                    s_ = sqc[kc % 2]
                    fw.act(s_[:], x32[:, kc, :], AF.Square, r=[x32], w=[s_])
                    for half in range(2):
                        fw.mm(ps[6 + half][:, :], ones16, s_[:, half * 512:(half + 1) * 512], start=(kc == 0), stop=(kc == 7),
                              r=[c16, s_], w=[ps[6 + half]])
                for half in range(2):
                    rstd(rs[:, half * 512:(half + 1) * 512], ps[6 + half][:, :], 1.0 / 1024, r=[ps[6 + half]], w=[rs])
                for kc in range(8):
                    tm = tmp[kc % 2]
                    fw.stt('dve', tm[:], x32[:, kc, :], w2T[:, kc, c:c + 1], rs[:], ALU.mult, ALU.mult, r=[x32, w2T, rs], w=[tm])
                    fw.act(h16[:, kc, :], tm[:], AF.Identity, bias=modT[:, 24 + kc, c:c + 1], r=[tm, modT], w=[h16])
                for j in range(32):
                    wv2 = w1s[(j // 2) % 2]
                    if j % 2 == 0:
                        fw.dma('sp', wv2[:].rearrange("p j k c -> p j (k c)"), w1c.ap()[j:j + 2].rearrange("j p n -> p j n"),
                               r=[DB('w1c', j // 4)], w=[wv2])
                    wv = T(wv2.t[:, j % 2, :, :]); wv.b = wv2.b
                    for half in range(2):
                        pb = ps[pi % 4]; pi += 1
                        for kc in range(8):
                            fw.mm(pb[:, :], wv[:, kc, :], h16[:, kc, half * 512:(half + 1) * 512], start=(kc == 0), stop=(kc == 7),
                                  r=[wv, h16], w=[pb])
                        r_ = rl[ri % 3]; ri += 1
                        fw.act(r_[:], pb[:, :], AF.Relu, r=[pb], w=[r_])
                        fw.tt('dve', a16[:, j, half * 512:(half + 1) * 512], r_[:], r_[:], ALU.mult, r=[r_], w=[a16])
                for dc in range(8):
                    wv = w2s[dc % 2]
                    fw.dma('sp', wv[:].rearrange("p j c -> p (j c)"), w2c.ap()[dc], r=[DB('w2c', dc)], w=[wv])
                    for half in range(2):
                        pb = ps[4 + (pi % 2)]; pi += 1
                        for j in range(32):
                            fw.mm(pb[:, :], wv[:, j, :], a16[:, j, half * 512:(half + 1) * 512], start=(j == 0), stop=(j == 31),
                                  r=[wv, a16], w=[pb])
                        xv = x32[:, dc, half * 512:(half + 1) * 512]
                        fw.stt('dve', xv, pb[:, :], modT[:, 40 + dc, c:c + 1], xv, ALU.mult, ALU.add, r=[pb, modT, x32], w=[x32])
                fw.dma('act', xs_v[:, :, g0:g0 + 1024], x32[:], r=[x32], w=tls)
            fw.emit()

    with contextlib.ExitStack() as st:
        xf = [alloc(st, "f_x%d" % i, [128, 8, 128]) for i in range(2)]
        sqf = alloc(st, "f_sq", [128, 8, 128], BF16)
        rsf = alloc(st, "f_rs", [128, 128])
        yo = [alloc(st, "f_y%d" % i, [128, 1024]) for i in range(2)]
        for t in range(0 if SKIPA else NTL):
            x = xf[t % 2]; yy = yo[t % 2]
            fw.dma('sp', x[:], xs_v[:, :, t * 128:(t + 1) * 128], r=[DB('xs', t)], w=[x])
            fw.act(sqf[:], x[:], AF.Square, r=[x], w=[sqf])
            pst = ps[4]
            for kc in range(8):
                fw.mm(pst[:, 0:128], ones16, sqf[:, kc, :], start=(kc == 0), stop=(kc == 7), r=[c16, sqf], w=[pst])
            rstd(rsf[:], pst[:, 0:128], 1.0 / 1024, r=[pst], w=[rsf])
            for kc in range(8):
                fw.stt('dve', x[:, kc, :], x[:, kc, :], nfT[:, kc:kc + 1], rsf[:], ALU.mult, ALU.mult, r=[x, nfT, rsf], w=[x])
            for half in range(2):
                pb = ps[(t % 2) * 2 + half]
                for j in range(4):
                    kc = half * 4 + j
                    fw.tr(pb[:, j * 128:(j + 1) * 128], x[:, kc, :], ident32, r=[x, c32], w=[pb])
                fw.cp('dve' if half == 0 else 'act', yy[:, half * 512:(half + 1) * 512], pb[:, :], r=[pb], w=[yy])
            fw.dma('act', y.ap()[t * 128:(t + 1) * 128, :], yy[:], r=[yy])
        fw.emit()
    stack0.close()
    return nc


def make_consts():
    idx = np.arange(128)
    same = (idx[:, None] // 64) == (idx[None, :] // 64)
    c = np.zeros((NCONST, 128, 128), np.float32)
    c[C_ID] = np.eye(128)
    c[C_ONES] = 1.0
    c[C_NEGONES] = -1.0
    c[C_UF] = same & (idx[:, None] <= idx[None, :])
    c[C_UB] = same & (idx[:, None] >= idx[None, :])
    c[C_SAME] = same
    c[C_CH0] = (idx[:, None] < 64) & np.ones((1, 128), bool)
    c[C_CH1] = (idx[:, None] >= 64) & np.ones((1, 128), bool)
    strict_f = same & (idx[None, :] < idx[:, None])
    strict_b = same & (idx[None, :] > idx[:, None])
    c[C_NSTRF] = -strict_f.astype(np.float32)
    c[C_NSTRB] = -strict_b.astype(np.float32)
    incl_f = same & (idx[None, :] <= idx[:, None])
    incl_b = same & (idx[None, :] >= idx[:, None])
    c[C_INTF] = incl_f.T.astype(np.float32) * (128.0 ** -0.5)
    c[C_INTB] = incl_b.T.astype(np.float32) * (128.0 ** -0.5)
    for i in range(6):
        b = 2 ** i
        c[C_MB + i] = ((idx[:, None] // (2 * b)) == (idx[None, :] // (2 * b))) & ((idx[:, None] // b) != (idx[None, :] // b))
    return c


_NC_CACHE = {}


def prep_inputs(inp, core):
    b = core // 4
    f = lambda a: np.ascontiguousarray(np.asarray(a, dtype=np.float32))
    xin = np.concatenate([f(inp['x_prompt'])[4 * core:4 * core + 4].reshape(1024, 1024), f(inp['x_sample'])[b]], axis=0)
    w1 = f(inp['w_ff1']).reshape(4, 8, 128, 32, 128).transpose(0, 3, 2, 1, 4).reshape(4, 32, 128, 1024)
    w2 = f(inp['w_ff2']).reshape(4, 32, 128, 8, 128).transpose(0, 3, 2, 1, 4).reshape(4, 8, 128, 4096)
    return {
        'xin': f(xin), 'cond': f(np.stack([f(inp['c_ctx']), f(inp['c'])[b]])), 's0': f(f(inp['state_delta'])[b]),
        'w_mod': f(inp['w_mod']), 'b_mod': f(inp['b_mod']), 'norm_mix': f(inp['norm_mix']), 'w_in': f(inp['w_in']),
        'conv_qkv': f(inp['conv_qkv']), 'a_log': f(inp['a_log']).reshape(4, 8), 'dt_bias': f(inp['dt_bias']).reshape(4, 8),
        'o_norm': f(inp['o_norm']), 'wsT': f(np.transpose(f(inp['w_spatial']), (0, 1, 3, 2))), 'b_sp': f(inp['b_spatial']),
        'w_out': f(inp['w_out']), 'norm_ffn': f(inp['norm_ffn']), 'w_ff1b': f(w1), 'w_ff2b': f(w2),
        'norm_final': f(inp['norm_final']), 'consts': make_consts(),
    }


def kernel(**inputs):
    if 'nc' not in _NC_CACHE:
        _NC_CACHE['nc'] = build()
    nc = _NC_CACHE['nc']
    shared = None
    in_maps = []
    for core in range(8):
        m = prep_inputs(inputs, core) if shared is None else None
        if shared is None:
            shared = m
        else:
            m = dict(shared)
            b = core // 4
            f = lambda a: np.ascontiguousarray(np.asarray(a, dtype=np.float32))
            m['xin'] = np.concatenate([f(inputs['x_prompt'])[4 * core:4 * core + 4].reshape(1024, 1024), f(inputs['x_sample'])[b]], axis=0)
            m['cond'] = f(np.stack([f(inputs['c_ctx']), f(inputs['c'])[b]]))
            m['s0'] = f(f(inputs['state_delta'])[b])
        in_maps.append(m)
    res = run_bass_kernel_spmd(nc, in_maps, core_ids=list(range(8)))
    outs = res.results
    y_prompt = np.concatenate([np.asarray(outs[c]['y'])[:1024].reshape(4, 256, 1024) for c in range(8)], axis=0)
    y_sample = np.stack([np.asarray(outs[0]['y'])[1024:], np.asarray(outs[4]['y'])[1024:]], axis=0)
    nsd = np.concatenate([np.asarray(outs[c]['ns']) for c in range(8)], axis=0)
    return (y_prompt.astype(np.float32), y_sample.astype(np.float32), nsd.astype(np.float32))
```

```python
import contextlib
import numpy as np
import concourse.bass as bass
import concourse.mybir as mybir
from concourse.bass_utils import run_bass_kernel_spmd

F32 = mybir.dt.float32
BF16 = mybir.dt.bfloat16
AF = mybir.ActivationFunctionType
ALU = mybir.AluOpType
AX = mybir.AxisListType

NT = 5120
NTL = NT // 128
EPS = 1e-6
LNSCALE = -0.5 * 4.852030263919617
SEQS = [(0, 256, 0), (256, 256, 0), (512, 256, 0), (768, 256, 0), (1024, 4096, 1)]
C_ID, C_ONES, C_UF, C_UB, C_SAME, C_CH0, C_CH1, C_NSTRF, C_NSTRB, C_INTF, C_INTB = range(11)
C_MB = 11
C_NEGONES = 17
NCONST = 18


class Buf:
    __slots__ = ('name', 'w', 'r', 'excl')

    def __init__(s, name=''):
        s.name = name
        s.w = None
        s.r = {}
        s.excl = False


class T:
    def __init__(s, t, name=''):
        s.t = t
        s.b = Buf(name)

    def __getitem__(s, k):
        return s.t[k]


def _b(x):
    return x.b if isinstance(x, T) else x


class FW:
    EPOCH = 8000
    NSW = 8
    NHW = 16
    MAXEP = 10
    SAME = ('act', 'dve', 'pool')

    def __init__(s, nc):
        s.nc = nc
        s.eng = {'pe': nc.tensor, 'act': nc.scalar, 'dve': nc.vector, 'pool': nc.gpsimd, 'sp': nc.sync}
        s.ops = {e: [] for e in s.eng}
        s.cnt = {e: 0 for e in s.eng}
        s.epoch = {e: 0 for e in s.eng}
        s.seen = {e: {} for e in s.eng}
        s.sems = {}
        for e in s.eng:
            for ep in range(s.MAXEP):
                k = (e, ep)
                s.sems[k] = nc.alloc_semaphore(name='s_%s_%d' % k)
        for cls, n in (('sw', s.NSW), ('hw', s.NHW)):
            for i in range(n):
                k = ('dma' + cls, i)
                s.sems[k] = nc.alloc_semaphore(name='s_%s_%d' % k)
        s.dma_i = {'sw': 0, 'hw': 0}
        s.dma_use = {'sw': [0] * s.NSW, 'hw': [0] * s.NHW}
        s.nops = 0

    def _deps(s, eng, reads, writes):
        need = {}

        def add(ev):
            if ev is None:
                return
            k, v = ev
            if need.get(k, 0) < v:
                need[k] = v
        for b in reads:
            b = _b(b)
            add(b.w)
            if b.excl:
                for k, v in b.r.items():
                    if k[0] != eng:
                        add((k, v))
        for b in writes:
            b = _b(b)
            add(b.w)
            for k, v in b.r.items():
                add((k, v))
        waits = []
        for k, v in need.items():
            if k[0] == eng and eng not in s.SAME:
                continue
            if s.seen[eng].get(k, 0) < v:
                s.seen[eng][k] = v
                waits.append((k, v))
        return waits

    def _commit(s, ev, reads, writes):
        k, v = ev
        for b in writes:
            b = _b(b)
            b.w = ev
            b.r = {}
        wset = set(id(_b(b)) for b in writes)
        for b in reads:
            b = _b(b)
            if id(b) in wset:
                continue
            if b.r.get(k, 0) < v:
                b.r[k] = v

    def op(s, eng, fn, reads=(), writes=()):
        waits = s._deps(eng, reads, writes)
        if s.cnt[eng] >= s.EPOCH:
            s.epoch[eng] += 1
            s.cnt[eng] = 0
        s.cnt[eng] += 1
        key = (eng, s.epoch[eng])
        val = s.cnt[eng]
        s.ops[eng].append((waits, fn, key, 1))
        s.nops += 1
        s._commit((key, val), reads, writes)

    def dma(s, q, out, in_, r=(), w=(), **kw):
        cls = 'sw' if q == 'pool' else 'hw'
        n = s.NSW if cls == 'sw' else s.NHW
        slot = s.dma_i[cls] % n
        s.dma_i[cls] += 1
        prev = s.dma_use[cls][slot]
        key = ('dma' + cls, slot)
        waits = s._deps(q, r, w)
        if prev > 0 and s.seen[q].get(key, 0) < 16 * prev:
            s.seen[q][key] = 16 * prev
            waits.append((key, 16 * prev))
        s.dma_use[cls][slot] = prev + 1
        s.ops[q].append((waits, lambda e: e.dma_start(out=out, in_=in_, **kw), key, 16))
        s.nops += 1
        s._commit((key, 16 * (prev + 1)), r, w)

    def barrier(s):
        evs = []
        for e in s.eng:
            for ep in range(s.epoch[e] + 1):
                c = s.cnt[e] if ep == s.epoch[e] else s.EPOCH
                if c > 0:
                    evs.append(((e, ep), c))
        for cls in ('sw', 'hw'):
            for slot in range(len(s.dma_use[cls])):
                if s.dma_use[cls][slot] > 0:
                    evs.append((('dma' + cls, slot), 16 * s.dma_use[cls][slot]))
        for e in s.eng:
            waits = []
            for k, v in evs:
                if s.seen[e].get(k, 0) < v:
                    s.seen[e][k] = v
                    waits.append((k, v))
            if waits:
                s.ops[e].append((waits, None, None, 0))

    def emit(s):
        s.barrier()
        nc = s.nc
        keys = set()
        for e in s.ops:
            for (w, f, k, i) in s.ops[e]:
                if k is not None:
                    keys.add(k)
                for (kk, v) in w:
                    keys.add(kk)
        for k in sorted(keys, key=str):
            if k not in s.sems:
                s.sems[k] = nc.alloc_semaphore(name='s_' + '_'.join(str(x) for x in k))
        ops = s.ops
        s.ops = {e: [] for e in s.eng}
        with nc.Block() as block:
            def mk(ename):
                def body(e):
                    for waits, fn, key, inc in ops[ename]:
                        for k, v in waits:
                            e.wait_ge(s.sems[k], v)
                        if fn is not None:
                            fn(e).then_inc(s.sems[key], inc)
                return body
            block.tensor(mk('pe'))
            block.scalar(mk('act'))
            block.vector(mk('dve'))
            block.gpsimd(mk('pool'))
            block.sync(mk('sp'))

    def mm(s, out, lhsT, rhs, start=True, stop=True, r=(), w=()):
        s.op('pe', lambda e: e.matmul(out, lhsT, rhs, start=start, stop=stop), r, w)

    def tr(s, out, in_, ident, r=(), w=()):
        s.op('pe', lambda e: e.transpose(out, in_, ident), r, w)

    def act(s, out, in_, func, bias=0.0, scale=1.0, r=(), w=()):
        s.op('act', lambda e: e.activation(out, in_, func, bias=bias, scale=scale), r, w)

    def tt(s, eng, out, in0, in1, op, r=(), w=()):
        s.op(eng, lambda e: e.tensor_tensor(out, in0, in1, op), r, w)

    def ts(s, eng, out, in0, s1, s2, op0, op1=None, r=(), w=()):
        if op1 is None:
            s.op(eng, lambda e: e.tensor_scalar(out, in0, s1, None, op0), r, w)
        else:
            s.op(eng, lambda e: e.tensor_scalar(out, in0, s1, s2, op0, op1), r, w)

    def stt(s, eng, out, in0, scalar, in1, op0, op1, r=(), w=()):
        s.op(eng, lambda e: e.scalar_tensor_tensor(out, in0, scalar, in1, op0, op1), r, w)

    def cp(s, eng, out, in_, r=(), w=()):
        if eng == 'act':
            s.op(eng, lambda e: e.copy(out, in_), r, w)
        else:
            s.op(eng, lambda e: e.tensor_copy(out, in_), r, w)

    def memset(s, eng, ap, val, w=()):
        s.op(eng, lambda e: e.memset(ap, val), (), w)

    def recip(s, out, in_, r=(), w=()):
        s.op('dve', lambda e: e.reciprocal(out, in_), r, w)


def build(L=4, dbg=False, stop=None):
    nc = bass.Bass("TRN2", target_bir_lowering=False)
    fw = FW(nc)

    def din(name, shape, dt=F32):
        return nc.dram_tensor(name, list(shape), dt, kind="ExternalInput")

    def dscr(name, shape, dt=F32):
        return nc.dram_tensor(name, list(shape), dt, kind=("ExternalOutput" if dbg else "Internal"))

    xin = din("xin", [NT, 1024]); cond = din("cond", [2, 1024]); s0 = din("s0", [4, 2, 4, 128, 128])
    w_mod = din("w_mod", [4, 1024, 6144]); b_mod = din("b_mod", [4, 6144]); norm_mix = din("norm_mix", [4, 1024])
    w_in = din("w_in", [4, 1024, 3088]); conv_qkv = din("conv_qkv", [4, 3, 1536]); a_log = din("a_log", [4, 8])
    dt_bias = din("dt_bias", [4, 8]); o_norm = din("o_norm", [4, 128]); wsT = din("wsT", [4, 4, 128, 128])
    b_sp = din("b_sp", [4, 4, 128]); w_out = din("w_out", [4, 1024, 1024]); norm_ffn = din("norm_ffn", [4, 1024])
    w_ff1b = din("w_ff1b", [4, 32, 128, 1024]); w_ff2b = din("w_ff2b", [4, 8, 128, 4096])
    norm_final = din("norm_final", [1024]); consts = din("consts", [NCONST, 128, 128])
    y = nc.dram_tensor("y", [NT, 1024], F32, kind="ExternalOutput")
    ns = nc.dram_tensor("ns", [4, 4, 2, 4, 128, 128], F32, kind="ExternalOutput")
    xs = dscr("xs", [1024, NT]); qkvs = dscr("qkvs", [1536, NT]); gates = dscr("gates", [512, NT])
    obs = dscr("obs", [512, NT], BF16); bgs = dscr("bgs", [NT, 16]); blobs = dscr("blobs", [NTL, 128, 2048], BF16)
    ofs = dscr("ofs", [NTL, 128, 512]); ofs2 = dscr("ofs2", [NTL, 128, 512])
    w1c = nc.dram_tensor("w1c", [32, 128, 1024], BF16, kind="Internal"); w2c = nc.dram_tensor("w2c", [8, 128, 4096], BF16, kind="Internal")

    dbufs = {}

    def DB(name, i):
        k = (name, i)
        if k not in dbufs:
            dbufs[k] = Buf(name + str(i))
        return dbufs[k]

    def DBs(name, t0, t1):
        return [DB(name, t) for t in range(max(t0, 0), min(t1, NTL))]

    xs_v = xs.ap().rearrange("(kc p) t -> p kc t", p=128)
    qkvs_v = qkvs.ap().rearrange("(n p) t -> p n t", p=128)
    gates_v = gates.ap().rearrange("(n p) t -> p n t", p=128)
    obs_v = obs.ap().rearrange("(n p) t -> p n t", p=128)

    stack0 = contextlib.ExitStack()

    uid = [0]

    def alloc(stack, name, shape, dt=F32):
        uid[0] += 1
        name = "%s_u%d" % (name, uid[0])
        return T(stack.enter_context(nc.sbuf_tensor(name, list(shape), dt)), name)

    ps = [T(stack0.enter_context(nc.psum_tensor("psb%d" % i, [128, 512], F32)), "ps%d" % i) for i in range(8)]
    for p_ in ps:
        p_.b.excl = True

    def ps3(i, h=4):
        return ps[i].t[:].rearrange("p (h c) -> p h c", h=h)

    def psbf(i):
        return ps[i].t[:].bitcast(BF16)

    c32 = alloc(stack0, "c32", [128, NCONST, 128], F32)
    c16 = alloc(stack0, "c16", [128, NCONST, 128], BF16)
    fw.dma('sp', c32[:], consts.ap().rearrange("n p c -> p n c"), w=[c32])
    fw.dma('pool', c16[:], consts.ap().rearrange("n p c -> p n c"), w=[c16])
    epsT = alloc(stack0, "epsT", [128, 1], F32)
    fw.memset('pool', epsT[:], EPS, w=[epsT])
    condT = alloc(stack0, "condT", [128, 8, 2], F32)
    scT = alloc(stack0, "scT", [128, 8, 2], BF16)
    for c in range(2):
        fw.dma('sp', condT[:, :, c], cond.ap()[c].rearrange("(kc p) -> p kc", p=128), w=[condT],
               allow_slow_non_contiguous=True)
    fw.act(scT[:], condT[:], AF.Silu, r=[condT], w=[scT])
    nfT = alloc(stack0, "nfT", [128, 8], F32)
    fw.dma('sp', nfT[:], norm_final.ap().rearrange("(kc p) -> p kc", p=128), w=[nfT], allow_slow_non_contiguous=True)
    nmT = alloc(stack0, "nmT", [128, 4, 8], F32); nffT = alloc(stack0, "nffT", [128, 4, 8], F32)
    bmT = alloc(stack0, "bmT", [128, 4, 48], F32); cwT = alloc(stack0, "cwT", [128, 4, 12, 3], F32)
    onT = alloc(stack0, "onT", [128, 4], F32)
    dtb = alloc(stack0, "dtb", [128, 4, 8], F32); nA = alloc(stack0, "nA", [128, 4, 8], F32)
    ws16 = alloc(stack0, "ws16", [128, 4, 4, 128], BF16); bs16 = alloc(stack0, "bs16", [1, 4, 4, 128], BF16)
    for l in range(L):
        fw.dma('sp', nmT[:, l, :], norm_mix.ap()[l].rearrange("(kc p) -> p kc", p=128), w=[nmT], allow_slow_non_contiguous=True)
        fw.dma('sp', nffT[:, l, :], norm_ffn.ap()[l].rearrange("(kc p) -> p kc", p=128), w=[nffT], allow_slow_non_contiguous=True)
        fw.dma('sp', bmT[:, l, :], b_mod.ap()[l].rearrange("(n p) -> p n", p=128), w=[bmT], allow_slow_non_contiguous=True)
        for wi in range(3):
            fw.dma('sp', cwT[:, l, :, wi], conv_qkv.ap()[l, wi].rearrange("(n p) -> p n", p=128), w=[cwT], allow_slow_non_contiguous=True)
        fw.dma('sp', onT[:, l:l + 1], o_norm.ap()[l].rearrange("(p o) -> p o", o=1), w=[onT], allow_slow_non_contiguous=True)
        fw.dma('sp', dtb[:, l, :], dt_bias.ap()[l].partition_broadcast(128), w=[dtb])
        fw.dma('sp', nA[:, l, :], a_log.ap()[l].partition_broadcast(128), w=[nA])
        fw.dma('pool', ws16[:, l, :, :], wsT.ap()[l].rearrange("g s t -> s g t"), w=[ws16])
        fw.dma('pool', bs16[:, l, :, :], b_sp.ap()[l:l + 1], w=[bs16])
    fw.act(nA[:, 0:L, :], nA[:, 0:L, :], AF.Exp, r=[nA], w=[nA])
    fw.ts('dve', nA[:, 0:L, :], nA[:, 0:L, :], -1.0, None, ALU.mult, r=[nA], w=[nA])

    ident32 = c32[:, C_ID, :]
    ident16 = c16[:, C_ID, :]
    ones16 = c16[:, C_ONES, :]
    ones32 = c32[:, C_ONES, :]

    def rstd(out, in_, scale, r, w):
        fw.act(out, in_, AF.Ln, bias=epsT[:, 0:1], scale=scale, r=list(r) + [epsT], w=w)
        fw.act(out, out, AF.Exp, scale=-0.5, r=w, w=w)

    import os
    SKIPA = os.environ.get('SKIPA') == '1'
    def run_jobs(factories, NJ, stagger=0):
        pending = list(factories)
        active = [None] * NJ
        delay = [sl * stagger for sl in range(NJ)]
        while pending or any(a is not None for a in active):
            for sl in range(NJ):
                if delay[sl] > 0:
                    delay[sl] -= 1
                    continue
                if active[sl] is None and pending:
                    active[sl] = pending.pop(0)(sl)
                if active[sl] is not None:
                    try:
                        next(active[sl])
                    except StopIteration:
                        active[sl] = None

    with contextlib.ExitStack() as st:
        xt = [alloc(st, "x0t%d" % i, [128, 1024]) for i in range(2)]
        xo = [alloc(st, "x0o%d" % i, [128, 8, 128]) for i in range(2)]
        for t in range(0 if SKIPA else NTL):
            a = xt[t % 2]; o = xo[t % 2]
            fw.dma('sp', a[:], xin.ap()[t * 128:(t + 1) * 128, :], w=[a])
            for half in range(2):
                pb = ps[(t % 2) * 2 + half]
                for j in range(4):
                    kc = half * 4 + j
                    fw.tr(pb[:, j * 128:(j + 1) * 128], a[:, kc * 128:(kc + 1) * 128], ident32, r=[a, c32], w=[pb])
                fw.cp('dve' if half == 0 else 'act', o[:, half * 4:(half + 1) * 4, :], ps3((t % 2) * 2 + half), r=[pb], w=[o])
            fw.dma('act', xs_v[:, :, t * 128:(t + 1) * 128], o[:], r=[o], w=[DB('xs', t)])
        fw.emit()

    modT = alloc(stack0, "modT", [128, 48, 2], F32)
    w1T = alloc(stack0, "w1T", [128, 8, 2], F32)
    w2T = alloc(stack0, "w2T", [128, 8, 2], F32)

    for l in range(L):
        if stop == 'x0':
            break
        with contextlib.ExitStack() as st:
            wm = [alloc(st, "wm%d" % i, [128, 8, 512], BF16) for i in range(2)]
            pm = ps[0]
            for nb in range(0 if SKIPA else 12):
                wb = wm[nb % 2]
                fw.dma('pool', wb[:], w_mod.ap()[l].rearrange("(kc p) n -> p kc n", p=128)[:, :, nb * 512:(nb + 1) * 512], w=[wb])
                for cc in range(4):
                    n = nb * 4 + cc
                    for kc in range(8):
                        fw.mm(pm[:, n * 2:n * 2 + 2], wb[:, kc, cc * 128:(cc + 1) * 128], scT[:, kc, :],
                              start=(kc == 0), stop=(kc == 7), r=[wb, scT], w=[pm])
            pmv = pm.t[:, 0:96].rearrange("p (n c) -> p n c", c=2)
            for c in range(0 if SKIPA else 2):
                fw.tt('dve', modT[:, :, c], pmv[:, :, c], bmT[:, l, :], ALU.add, r=[pm, bmT], w=[modT])
                fw.stt('dve', w1T[:, :, c], modT[:, 8:16, c], 1.0, nmT[:, l, :], ALU.add, ALU.mult, r=[modT, nmT], w=[w1T])
                fw.stt('dve', w2T[:, :, c], modT[:, 32:40, c], 1.0, nffT[:, l, :], ALU.add, ALU.mult, r=[modT, nffT], w=[w2T])
            fw.emit()

        if stop == 'm':
            break
        with contextlib.ExitStack() as st:
            win = alloc(st, "win", [128, 8, 3088], BF16)
            w_in_v = w_in.ap()[l].rearrange("(kc p) n -> p kc n", p=128)
            for kc in range(8):
                fw.dma('pool', win[:, kc, :], w_in_v[:, kc, :], w=[win])
            NA = 2
            def perA(name, shape, dt=F32, k=1):
                return [[alloc(st, "%s%d_%d" % (name, i, j), shape, dt) for j in range(k)] for i in range(NA)]
            xT = perA("a_x", [128, 8, 512]); sqA = perA("a_sq", [128, 8, 512], BF16); rsA = perA("a_rs", [128, 512])
            tmpA = perA("a_tmp", [128, 512], F32, 2); hA = perA("a_h", [128, 8, 512], BF16); stgA = perA("a_stg", [128, 512], F32, 4)
            uA = perA("a_u", [128, 4, 512]); bgA = perA("a_bg", [128, 16], F32, 2); vgfA = perA("a_vgf", [128, 4, 128], F32, 2)
            vgnA = perA("a_vgn", [128, 4, 128], BF16, 2); st4A = perA("a_st4", [128, 4], F32, 2); obA = perA("a_ob", [128, 4, 128], BF16, 2)
            COLS = [n * 128 for n in range(12)] + [1536 + n * 128 for n in range(4)] + [2064 + n * 128 for n in range(4)]

            def ajob(gi):
                def gen(sl):
                    base = 4 * sl
                    g0 = gi * 512
                    c = 0 if g0 < 1024 else 1
                    x = xT[sl][0]; sq = sqA[sl][0]; rs = rsA[sl][0]; h16 = hA[sl][0]; uT = uA[sl][0]
                    tls = DBs('xs', gi * 4, gi * 4 + 4)
                    fw.dma('sp', x[:], xs_v[:, :, g0:g0 + 512], r=tls, w=[x])
                    yield
                    fw.act(sq[:], x[:], AF.Square, r=[x], w=[sq])
                    yield
                    pst = ps[base + 2]
                    for kc in range(8):
                        fw.mm(pst[:, :], ones16, sq[:, kc, :], start=(kc == 0), stop=(kc == 7), r=[c16, sq], w=[pst])
                    yield
                    fw.act(rs[:], pst[:, :], AF.Ln, bias=epsT[:, 0:1], scale=1.0 / 1024, r=[pst, epsT], w=[rs])
                    yield
                    fw.act(rs[:], rs[:], AF.Exp, scale=-0.5, r=[rs], w=[rs])
                    yield
                    for kc in range(8):
                        tm = tmpA[sl][kc % 2]
                        fw.stt('dve', tm[:], x[:, kc, :], w1T[:, kc, c:c + 1], rs[:], ALU.mult, ALU.mult, r=[x, w1T, rs], w=[tm])
                        fw.act(h16[:, kc, :], tm[:], AF.Identity, bias=modT[:, kc, c:c + 1], r=[tm, modT], w=[h16])
                        if kc % 2 == 1:
                            yield
                    si = 0
                    for n in range(20):
                        pb = ps[base + (n % 2)]
                        col = COLS[n]
                        for kc in range(8):
                            fw.mm(pb[:, :], win[:, kc, col:col + 128], h16[:, kc, :], start=(kc == 0), stop=(kc == 7),
                                  r=[win, h16], w=[pb])
                        if n < 12:
                            sg = stgA[sl][si % 4]; si += 1
                            fw.cp('dve' if n % 2 == 0 else 'act', sg[:], pb[:, :], r=[pb], w=[sg])
                            fw.dma('sp', qkvs_v[:, n, g0:g0 + 512], sg[:], r=[sg], w=DBs('qkvs', gi * 4, gi * 4 + 4))
                        elif n < 16:
                            sg = stgA[sl][si % 4]; si += 1
                            fw.act(sg[:], pb[:, :], AF.Silu, r=[pb], w=[sg])
                            fw.dma('sp', gates_v[:, n - 12, g0:g0 + 512], sg[:], r=[sg], w=DBs('gates', gi * 4, gi * 4 + 4))
                        else:
                            fw.act(uT[:, n - 16, :], pb[:, :], AF.Gelu, r=[pb], w=[uT])
                        yield
                    for tt_ in range(4):
                        t = gi * 4 + tt_
                        tok = slice(tt_ * 128, (tt_ + 1) * 128)
                        pab = ps[base + 2]; pvg = ps[base + 3]
                        b = bgA[sl][tt_ % 2]; vf = vgfA[sl][tt_ % 2]; vn = vgnA[sl][tt_ % 2]; s4 = st4A[sl][tt_ % 2]; ob = obA[sl][tt_ % 2]
                        for kc in range(8):
                            fw.mm(pab[:, 0:16], h16[:, kc, tok], win[:, kc, 2048:2064], start=(kc == 0), stop=(kc == 7),
                                  r=[h16, win], w=[pab])
                        for kc in range(8):
                            fw.mm(pvg[:, :], h16[:, kc, tok], win[:, kc, 2576:3088], start=(kc == 0), stop=(kc == 7),
                                  r=[h16, win], w=[pvg])
                        yield
                        fw.act(b[:, 0:8], pab[:, 0:8], AF.Sigmoid, r=[pab], w=[b])
                        fw.act(vf[:], ps3(base + 3), AF.Gelu, r=[pvg], w=[vf])
                        yield
                        fw.tt('dve', b[:, 8:16], pab[:, 8:16], dtb[:, l, :], ALU.add, r=[pab, dtb], w=[b])
                        fw.op('dve', lambda e, s4=s4, vf=vf: e.reduce_sum(s4[:], vf[:], AX.X), [vf], [s4])
                        yield
                        fw.act(b[:, 8:16], b[:, 8:16], AF.Exp, r=[b], w=[b])
                        fw.ts('dve', s4[:], s4[:], -1.0 / 128, None, ALU.mult, r=[s4], w=[s4])
                        yield
                        fw.act(b[:, 8:16], b[:, 8:16], AF.Ln, bias=1.0, r=[b], w=[b])
                        fw.tt('dve', vf[:], vf[:], s4[:].unsqueeze(2).to_broadcast([128, 4, 128]), ALU.add, r=[vf, s4], w=[vf])
                        yield
                        fw.tt('dve', b[:, 8:16], b[:, 8:16], nA[:, l, :], ALU.mult, r=[b, nA], w=[b])
                        sg = stgA[sl][si % 4]; si += 1
                        sg3 = sg.t[:].rearrange("p (h c) -> p h c", h=4)
                        fw.tt('pool', sg3, vf[:], vf[:], ALU.mult, r=[vf], w=[sg])
                        yield
                        fw.dma('act', bgs.ap()[t * 128:(t + 1) * 128, :], b[:], r=[b], w=[DB('bgs', t)])
                        fw.op('dve', lambda e, s4=s4, sg3=sg3: e.reduce_sum(s4[:], sg3, AX.X), [sg], [s4])
                        yield
                        fw.act(s4[:], s4[:], AF.Ln, bias=epsT[:, 0:1], scale=1.0 / 128, r=[s4, epsT], w=[s4])
                        yield
                        fw.act(s4[:], s4[:], AF.Exp, scale=-0.5, r=[s4], w=[s4])
                        yield
                        fw.tt('dve', vn[:], vf[:], s4[:].unsqueeze(2).to_broadcast([128, 4, 128]), ALU.mult, r=[vf, s4], w=[vn])
                        yield
                        pmx = ps[base + 2]
                        for g in range(4):
                            fw.mm(pmx[:, g * 128:(g + 1) * 128], vn[:, g, :], ws16[:, l, g, :], start=True, stop=False, r=[vn, ws16], w=[pmx])
                            fw.mm(pmx[:, g * 128:(g + 1) * 128], c16[0:1, C_ONES, :], bs16[0:1, l, g, :], start=False, stop=True,
                                  r=[c16, bs16], w=[pmx])
                        yield
                        fw.tt('dve', ob[:], uT[:, :, tok], ps3(base + 2), ALU.mult, r=[uT, pmx], w=[ob])
                        yield
                        fw.dma('act', obs_v[:, :, t * 128:(t + 1) * 128], ob[:], r=[ob], w=[DB('obs', t)])
                return gen
            run_jobs([ajob(gi) for gi in range(0 if SKIPA else NT // 512)], NA, stagger=40)
            fw.emit()

        if stop == 'a':
            break
        import os
        with contextlib.ExitStack() as st:
            NB0 = 3
            def per0(name, shape, dt=F32):
                return [alloc(st, "%s%d" % (name, i), shape, dt) for i in range(NB0)]
            pre = per0("b_pre", [128, 12, 130]); cv = per0("b_cv", [128, 12, 128]); cv2 = per0("b_cv2", [128, 12, 128])
            cv3 = per0("b_cv3", [128, 12, 128]); sq16 = per0("b_sq16", [128, 8, 128], BF16); rn = per0("b_rn", [128, 8, 128])
            vT16 = per0("b_vT16", [128, 4, 128], BF16); blob0 = per0("b_blob", [128, 4, 4, 128], BF16)
            cw = cwT.t[:, l, :, :]

            def b0job(t, sstart, slen):
                def gen(sl):
                    p = pre[sl]; c_ = cv[sl]; c2 = cv2[sl]; c3 = cv3[sl]; bl = blob0[sl]; sq_ = sq16[sl]; rn_ = rn[sl]; vt_ = vT16[sl]
                    ia = 2 * sl; ib = 2 * sl + 1
                    t0 = t * 128
                    lo = max(t0 - 1, sstart); hi = min(t0 + 129, sstart + slen)
                    if lo > t0 - 1:
                        fw.memset('pool', p[:, :, 0:1], 0.0, w=[p])
                    if hi < t0 + 129:
                        fw.memset('pool', p[:, :, 129:130], 0.0, w=[p])
                    for n3 in range(3):
                        fw.dma('sp', p[:, n3 * 4:(n3 + 1) * 4, lo - (t0 - 1):hi - (t0 - 1)], qkvs_v[:, n3 * 4:(n3 + 1) * 4, lo:hi],
                               r=DBs('qkvs', t - 1, t + 2), w=[p])
                    yield
                    fw.tt('dve', c_[:], p[:, :, 0:128], cw[:, :, 0:1].to_broadcast([128, 12, 128]), ALU.mult, r=[p, cwT], w=[c_])
                    fw.tt('pool', c2[:], p[:, :, 1:129], cw[:, :, 1:2].to_broadcast([128, 12, 128]), ALU.mult, r=[p, cwT], w=[c2])
                    fw.tt('pool', c3[:], p[:, :, 2:130], cw[:, :, 2:3].to_broadcast([128, 12, 128]), ALU.mult, r=[p, cwT], w=[c3])
                    yield
                    fw.tt('dve', c_[:], c_[:], c2[:], ALU.add, r=[c_, c2], w=[c_])
                    yield
                    fw.tt('dve', c_[:], c_[:], c3[:], ALU.add, r=[c_, c3], w=[c_])
                    yield
                    fw.act(c_[:], c_[:], AF.Silu, r=[c_], w=[c_])
                    yield
                    fw.act(sq_[:], c_[:, 0:8, :], AF.Square, r=[c_], w=[sq_])
                    fw.cp('act', vt_[:], c_[:, 8:12, :], r=[c_], w=[vt_])
                    yield
                    fw.mm(ps[ia][:, :], ones16, sq_[:, 0:4, :], r=[c16, sq_], w=[ps[ia]])
                    fw.mm(ps[ib][:, :], ones16, sq_[:, 4:8, :], r=[c16, sq_], w=[ps[ib]])
                    yield
                    fw.act(rn_[:, 0:4, :], ps3(ia), AF.Ln, bias=epsT[:, 0:1], scale=1.0, r=[ps[ia], epsT], w=[rn_])
                    fw.act(rn_[:, 4:8, :], ps3(ib), AF.Ln, bias=epsT[:, 0:1], scale=1.0, r=[ps[ib], epsT], w=[rn_])
                    yield
                    fw.act(rn_[:], rn_[:], AF.Exp, scale=-0.5, r=[rn_], w=[rn_])
                    yield
                    fw.tt('dve', bl[:, 0:2, :, :], c_[:, 0:8, :].rearrange("p (a h) c -> p a h c", a=2),
                          rn_[:].rearrange("p (a h) c -> p a h c", a=2), ALU.mult, r=[c_, rn_], w=[bl])
                    yield
                    pv = psbf(ia)
                    for h in range(4):
                        fw.tr(pv[:, h * 128:(h + 1) * 128], bl[:, 1, h, :], ident16, r=[bl, c16], w=[ps[ia]])
                        fw.tr(pv[:, 512 + h * 128:512 + (h + 1) * 128], vt_[:, h, :], ident16, r=[vt_, c16], w=[ps[ia]])
                    yield
                    fw.cp('dve', bl[:, 2:4, :, :], pv.rearrange("p (a h c) -> p a h c", a=2, h=4), r=[ps[ia]], w=[bl])
                    yield
                    fw.dma('act', blobs.ap()[t], bl[:].rearrange("p a h c -> p (a h c)"), r=[bl], w=[DB('blobs', t)])
                return gen
            jobs0 = []
            for (sstart, slen, cidx) in SEQS:
                for t in range(sstart // 128, (sstart + slen) // 128):
                    jobs0.append(b0job(t, sstart, slen))
            run_jobs(jobs0, NB0, stagger=4)
            fw.emit()

        if stop == 'b0':
            break
        with contextlib.ExitStack() as st:
            NJ = 4
            def per(name, shape, dt=F32):
                return [alloc(st, "%s%d" % (name, i), shape, dt) for i in range(NJ)]
            blobj = per("j_blob", [128, 4, 4, 128], BF16)
            bgt = per("j_bg", [128, 16]); Gs = per("j_Gs", [128, 16]); eG = per("j_eG", [128, 16]); sck = per("j_sck", [128, 4])
            gpad = per("j_gpad", [128, 128])
            for i in range(NJ):
                fw.memset('pool', gpad[i][:], 0.0, w=[gpad[i]])
            for j0 in range(0, 32, 4):
                fw.dma('pool', w1c.ap()[j0:j0 + 4], w_ff1b.ap()[l, j0:j0 + 4], w=[DB('w1c', j0 // 4)])
            for dc in range(8):
                fw.dma('pool', w2c.ap()[dc], w_ff2b.ap()[l, dc], w=[DB('w2c', dc)])
            Ugj = per("j_Ug", [128, 4, 128]); decj = per("j_dec", [128, 4, 128]); decTj = per("j_decT", [128, 4, 128])
            eGBj = per("j_eGB", [128, 4, 128]); nm32j = per("j_nm32", [128, 4, 128])
            Pj = per("j_P", [128, 4, 128], BF16); Qj = per("j_Q", [128, 4, 128], BF16); QMj = per("j_QM", [128, 5, 4, 128], BF16)
            TT16j = per("j_TT16", [128, 4, 128], BF16)
            T16j = per("j_T16", [128, 4, 128], BF16); X16j = per("j_X16", [128, 4, 128], BF16)
            vbj = per("j_vb", [128, 4, 128], BF16); kbgj = per("j_kbg", [128, 4, 128], BF16)
            kdj = per("j_kd", [128, 4, 128], BF16); wTj = per("j_wT", [128, 4, 128], BF16)
            qkj = per("j_qk", [128, 4, 128], BF16); qdj = per("j_qd", [128, 4, 128], BF16)
            u32j = per("j_u", [128, 4, 128]); vnj = per("j_vn", [128, 4, 128], BF16); oTj = per("j_oT", [128, 4, 128])
            NCH = 8
            S32c = [alloc(st, "j_S32_%d" % i, [128, 4, 128]) for i in range(NCH)]
            S16c = [alloc(st, "j_S16_%d" % i, [128, 4, 128], BF16) for i in range(NCH)]
            negones32 = c32[:, C_NEGONES, :]

            def job(slot, ci, t, d, cidx, is_last, seq_i, turn, my_idx):
                bl = blobj[slot]; b = bgt[slot]; G = Gs[slot]; e_ = eG[slot]; sk = sck[slot]; gp = gpad[slot]
                Ug = Ugj[slot]; dec = decj[slot]; decT = decTj[slot]; eGB = eGBj[slot]; nm32 = nm32j[slot]
                P = Pj[slot]; Q = Qj[slot]; QM = QMj[slot]; TT16 = TT16j[slot]; T16 = T16j[slot]; X16 = X16j[slot]
                vb16 = vbj[slot]; kbg16 = kbgj[slot]; kd = kdj[slot]; wT = wTj[slot]; qk = qkj[slot]; qd = qdj[slot]
                u = u32j[slot]; vn = vnj[slot]; oT = oTj[slot]
                pa = ps[2 * slot]; pb_ = ps[2 * slot + 1]; ia = 2 * slot; ib = 2 * slot + 1
                fw.dma('sp', bl[:].rearrange("p a h c -> p (a h c)"), blobs.ap()[t], r=[DB('blobs', t)], w=[bl])
                fw.dma('sp', b[:], bgs.ap()[t * 128:(t + 1) * 128, :], r=[DB('bgs', t)], w=[b])
                g4 = b[:, 8 + d * 4:12 + d * 4]; b4 = b[:, d * 4:d * 4 + 4]
                U = c32[:, C_UF + d, :]
                fw.cp('dve', gp[:, 0:4], g4, r=[b], w=[gp])
                fw.mm(pa[:, 0:128], U, gp[:], r=[c32, gp], w=[pa])
                fw.mm(pa[:, 128:256], c32[:, C_SAME, :], gp[:], r=[c32, gp], w=[pa])
                fw.mm(pa[:, 256:384], c32[:, C_CH0, :], gp[:], r=[c32, gp], w=[pa])
                fw.mm(pa[:, 384:512], c32[:, C_CH1, :], gp[:], r=[c32, gp], w=[pa])
                for h in range(4):
                    fw.act(Ug[:, h, :], U, AF.Identity, scale=g4[:, h:h + 1], r=[c32, b], w=[Ug])
                yield
                fw.cp('dve', G[:].rearrange("p (a b) -> p a b", a=4), ps3(ia)[:, :, 0:4], r=[pa], w=[G])
                fw.tt('dve', G[:, 4:8], G[:, 4:8], G[:, 0:4], ALU.subtract, r=[G], w=[G])
                fw.act(e_[:], G[:], AF.Exp, r=[G], w=[e_])
                for h in range(4):
                    fw.mm(pb_[:, h * 128:(h + 1) * 128], Ug[:, h, :], ones32, start=True, stop=False, r=[Ug, c32], w=[pb_])
                    fw.mm(pb_[:, h * 128:(h + 1) * 128], negones32, Ug[:, h, :], start=False, stop=True, r=[Ug, c32], w=[pb_])
                for h in range(4):
                    fw.mm(pa[:, h * 128:(h + 1) * 128], ones32, Ug[:, h, :], r=[Ug, c32], w=[pa])
                yield
                fw.tt('dve', sk[:], b4, e_[:, 0:4], ALU.mult, r=[b, e_], w=[sk])
                fw.act(dec[:], ps3(ib), AF.Relu, scale=-1.0, r=[pb_], w=[dec])
                fw.act(decT[:], ps3(ib), AF.Relu, scale=1.0, r=[pb_], w=[decT])
                fw.act(eGB[:], ps3(ia), AF.Exp, bias=LNSCALE, r=[pa], w=[eGB])
                for h in range(4):
                    fw.mm(pb_[:, h * 128:(h + 1) * 128], bl[:, 1, h, :], bl[:, 1, h, :], r=[bl], w=[pb_])
                yield
                fw.act(dec[:], dec[:], AF.Exp, scale=-1.0, r=[dec], w=[dec])
                fw.act(decT[:], decT[:], AF.Exp, scale=-1.0, r=[decT], w=[decT])
                ktok = bl[:, 2, :, :]; vtok = bl[:, 3, :, :]
                fw.tt('pool', vb16[:], vtok, b4.unsqueeze(2).to_broadcast([128, 4, 128]), ALU.mult, r=[bl, b], w=[vb16])
                fw.tt('pool', kbg16[:], ktok, sk[:].unsqueeze(2).to_broadcast([128, 4, 128]), ALU.mult, r=[bl, sk], w=[kbg16])
                fw.tt('pool', kd[:], ktok, e_[:, 4:8].unsqueeze(2).to_broadcast([128, 4, 128]), ALU.mult, r=[bl, e_], w=[kd])
                yield
                fw.tt('dve', nm32[:], ps3(ib), dec[:], ALU.mult, r=[pb_, dec], w=[nm32])
                fw.tt('pool', nm32[:], nm32[:], c32[:, C_NSTRF + d, :].unsqueeze(1).to_broadcast([128, 4, 128]), ALU.mult, r=[nm32, c32], w=[nm32])
                yield
                fw.tt('dve', P[:], nm32[:], b4.unsqueeze(2).to_broadcast([128, 4, 128]), ALU.mult, r=[nm32, b], w=[P])
                pqv = psbf(ia)
                for h in range(4):
                    fw.tr(pqv[:, h * 128:(h + 1) * 128], P[:, h, :], ident16, r=[P, c16], w=[pa])
                for h in range(4):
                    fw.mm(pb_[:, h * 128:(h + 1) * 128], bl[:, 1, h, :], bl[:, 0, h, :], r=[bl], w=[pb_])
                yield
                fw.cp('act', Q[:], pqv[:, 0:512].rearrange("p (h c) -> p h c", h=4), r=[pa], w=[Q])
                fw.tt('dve', nm32[:], ps3(ib), decT[:], ALU.mult, r=[pb_, decT], w=[nm32])
                fw.tt('dve', qd[:], eGB[:], bl[:, 0, :, :], ALU.mult, r=[eGB, bl], w=[qd])
                yield
                fw.tt('pool', qk[:], nm32[:], c32[:, C_INTF + d, :].unsqueeze(1).to_broadcast([128, 4, 128]), ALU.mult, r=[nm32, c32], w=[qk])
                for li in range(1, 6):
                    fw.tt('pool', QM[:, li - 1, :, :], Q[:], c16[:, C_MB + li, :].unsqueeze(1).to_broadcast([128, 4, 128]),
                          ALU.mult, r=[Q, c16], w=[QM])
                mb1 = c32[:, C_MB, :].unsqueeze(1).to_broadcast([128, 4, 128])
                idb = c32[:, C_ID, :].unsqueeze(1).to_broadcast([128, 4, 128])
                fw.tt('dve', eGB[:], P[:], mb1, ALU.mult, r=[P, c32], w=[eGB])
                fw.tt('pool', nm32[:], Q[:], mb1, ALU.mult, r=[Q, c32], w=[nm32])
                yield
                fw.tt('dve', T16[:], eGB[:], idb, ALU.add, r=[eGB, c32], w=[T16])
                fw.tt('pool', TT16[:], nm32[:], idb, ALU.add, r=[nm32, c32], w=[TT16])
                yield
                for li in range(1, 6):
                    for h in range(4):
                        fw.mm(pa[:, h * 128:(h + 1) * 128], QM[:, li - 1, h, :], T16[:, h, :], r=[QM, T16], w=[pa])
                    yield
                    fw.cp('act', X16[:], ps3(ia), r=[pa], w=[X16])
                    yield
                    if li < 5:
                        for h in range(4):
                            fw.mm(pb_[:, h * 128:(h + 1) * 128], TT16[:, h, :], X16[:, h, :], r=[TT16, X16], w=[pb_])
                    for h in range(4):
                        fw.mm(pa[:, h * 128:(h + 1) * 128], X16[:, h, :], TT16[:, h, :], r=[TT16, X16], w=[pa])
                    yield
                    if li < 5:
                        fw.tt('dve', T16[:], T16[:], ps3(ib), ALU.add, r=[T16, pb_], w=[T16])
                    fw.tt('dve', TT16[:], TT16[:], ps3(ia), ALU.add, r=[TT16, pa], w=[TT16])
                    yield
                for h in range(4):
                    fw.mm(pa[:, h * 128:(h + 1) * 128], kbg16[:, h, :], TT16[:, h, :], r=[kbg16, TT16], w=[pa])
                for h in range(4):
                    fw.mm(pb_[:, h * 128:(h + 1) * 128], TT16[:, h, :], vb16[:, h, :], r=[TT16, vb16], w=[pb_])
                yield
                fw.cp('act', wT[:], ps3(ia), r=[pa], w=[wT])
                fw.cp('act', u[:], ps3(ib), r=[pb_], w=[u])
                yield
                while turn[0] != my_idx:
                    yield
                Sf = S32c[ci]; Sb = S16c[ci]
                p6 = pa; p7 = pb_
                chs = (0, 1) if d == 0 else (1, 0)
                for ch in chs:
                    R = slice(ch * 64, (ch + 1) * 64)
                    for h in range(4):
                        fw.mm(p6[:, h * 128:(h + 1) * 128], wT[:, h, :], Sb[:, h, :], r=[wT, Sb], w=[p6])
                    yield
                    fw.tt('dve', vn[R, :, :], u[R, :, :], ps3(ia)[R, :, :], ALU.subtract, r=[u, p6], w=[vn])
                    yield
                    for h in range(4):
                        fw.mm(p7[:, h * 64:(h + 1) * 64], Sb[:, h, :], qd[:, h, R], start=True, stop=False, r=[Sb, qd], w=[p7])
                        fw.mm(p7[:, h * 64:(h + 1) * 64], vn[R, h, :], qk[R, h, R], start=False, stop=True, r=[vn, qk], w=[p7])
                    for h in range(4):
                        fw.mm(p6[:, h * 128:(h + 1) * 128], kd[R, h, :], vn[R, h, :], r=[kd, vn], w=[p6])
                    yield
                    fw.cp('act', oT[:, :, R], p7.t[:, 0:256].rearrange("p (h c) -> p h c", h=4), r=[p7], w=[oT])
                    for h in range(4):
                        fw.stt('dve', Sf[:, h, :], Sf[:, h, :], e_[:, 8 + ch * 4 + h:9 + ch * 4 + h], ps3(ia)[:, h, :],
                               ALU.mult, ALU.add, r=[Sf, e_, p6], w=[Sf])
                    yield
                    fw.cp('act', Sb[:], Sf[:], r=[Sf], w=[Sb])
                    yield
                turn[0] += 1
                fw.dma('act', (ofs if d == 0 else ofs2).ap()[t].rearrange("p (h c) -> p h c", h=4), oT[:], r=[oT],
                       w=[DB('ofs%d' % d, t)])
                if is_last and cidx == 0:
                    fw.dma('act', ns.ap()[seq_i, l, d].rearrange("h k v -> k h v"), Sf[:], r=[Sf])

            def run_chains(chain_defs):
                state = []
                for (ci, seq_i, sstart, slen, cidx, d) in chain_defs:
                    Sf = S32c[ci]; Sb = S16c[ci]
                    if cidx == 0:
                        fw.memset('pool', Sf[:], 0.0, w=[Sf])
                    else:
                        fw.dma('sp', Sf[:], s0.ap()[l, d].rearrange("h k v -> k h v"), w=[Sf])
                    fw.cp('act', Sb[:], Sf[:], r=[Sf], w=[Sb])
                    tiles = list(range(sstart // 128, (sstart + slen) // 128))
                    if d == 1:
                        tiles = tiles[::-1]
                    state.append(dict(ci=ci, seq_i=seq_i, cidx=cidx, d=d, tiles=tiles, nxt=0, turn=[0], inflight=0))
                active = [None] * NJ
                delay = [sl * 11 for sl in range(NJ)]
                while True:
                    progressed = False
                    for slot in range(NJ):
                        if delay[slot] > 0:
                            delay[slot] -= 1
                            progressed = True
                            continue
                        if active[slot] is None:
                            cands = [c for c in state if c['nxt'] < len(c['tiles']) and c['inflight'] < 2]
                            if cands:
                                c = min(cands, key=lambda c: (c['inflight'], c['nxt']))
                                k = c['nxt']; c['nxt'] += 1; c['inflight'] += 1
                                g = job(slot, c['ci'], c['tiles'][k], c['d'], c['cidx'], k == len(c['tiles']) - 1, c['seq_i'], c['turn'], k)
                                active[slot] = (g, c)
                        if active[slot] is not None:
                            progressed = True
                            g, c = active[slot]
                            try:
                                next(g)
                            except StopIteration:
                                c['inflight'] -= 1
                                active[slot] = None
                    if not progressed and all(c['nxt'] >= len(c['tiles']) for c in state):
                        break

            pr = []
            for seq_i, (sstart, slen, cidx) in enumerate(SEQS[:4]):
                for d in range(2):
                    pr.append((seq_i * 2 + d, seq_i, sstart, slen, cidx, d))
            run_chains(pr)
            sstart, slen, cidx = SEQS[4]
            run_chains([(0, 4, sstart, slen, cidx, 0), (1, 4, sstart, slen, cidx, 1)])
            fw.emit()

        if stop == 'b12':
            break
        with contextlib.ExitStack() as st:
            wo = alloc(st, "wo", [128, 8, 1024], BF16)
            fw.dma('pool', wo[:], w_out.ap()[l].rearrange("(kc p) n -> p kc n", p=128), w=[wo])
            NE = 3
            def per3(name, shape, dt=F32):
                return [alloc(st, "%s%d" % (name, i), shape, dt) for i in range(NE)]
            ofl = per3("e_ofl", [128, 4, 128]); ofl2 = per3("e_ofl2", [128, 4, 128]); osq = per3("e_osq", [128, 4, 128], BF16)
            ors = per3("e_ors", [128, 4, 128]); gTl = per3("e_gT", [128, 4, 128]); catA = per3("e_catA", [128, 4, 128], BF16)
            catB = per3("e_catB", [128, 4, 128], BF16); xres = per3("e_x", [128, 8, 128])

            def b3job(t):
                def gen(sl):
                    cidx = 0 if t < 8 else 1
                    of_ = ofl[sl]; of2 = ofl2[sl]; gt = gTl[sl]; cb = catB[sl]; xr = xres[sl]; ca = catA[sl]; oq = osq[sl]; orr = ors[sl]
                    ia = 2 * sl; ib = 2 * sl + 1
                    fw.dma('sp', of_[:], ofs.ap()[t].rearrange("p (h c) -> p h c", h=4), r=[DB('ofs0', t)], w=[of_])
                    fw.dma('sp', of2[:], ofs2.ap()[t].rearrange("p (h c) -> p h c", h=4), r=[DB('ofs1', t)], w=[of2])
                    fw.dma('sp', gt[:], gates_v[:, :, t * 128:(t + 1) * 128], r=[DB('gates', t)], w=[gt])
                    fw.dma('sp', cb[:], obs_v[:, :, t * 128:(t + 1) * 128], r=[DB('obs', t)], w=[cb])
                    fw.dma('sp', xr[:], xs_v[:, :, t * 128:(t + 1) * 128], r=[DB('xs', t)], w=[xr])
                    yield
                    fw.tt('pool', of_[:], of_[:], of2[:], ALU.add, r=[of_, of2], w=[of_])
                    yield
                    fw.act(oq[:], of_[:], AF.Square, r=[of_], w=[oq])
                    yield
                    fw.mm(ps[ia][:, :], ones16, oq[:], r=[c16, oq], w=[ps[ia]])
                    yield
                    fw.act(orr[:], ps3(ia), AF.Ln, bias=epsT[:, 0:1], scale=1.0 / 128, r=[ps[ia], epsT], w=[orr])
                    yield
                    fw.act(orr[:], orr[:], AF.Exp, scale=-0.5, r=[orr], w=[orr])
                    yield
                    fw.tt('dve', of_[:], of_[:], orr[:], ALU.mult, r=[of_, orr], w=[of_])
                    yield
                    fw.stt('dve', ca[:], of_[:], onT[:, l:l + 1], gt[:], ALU.mult, ALU.mult, r=[of_, onT, gt], w=[ca])
                    yield
                    for half in range(2):
                        pbn = ia if half == 0 else ib
                        pb2 = ps[pbn]
                        for jj in range(4):
                            dc = half * 4 + jj
                            for kc in range(8):
                                rhs = ca[:, kc, :] if kc < 4 else cb[:, kc - 4, :]
                                fw.mm(pb2[:, jj * 128:(jj + 1) * 128], wo[:, kc, dc * 128:(dc + 1) * 128], rhs,
                                      start=(kc == 0), stop=(kc == 7), r=[wo, ca, cb], w=[pb2])
                        yield
                    for half in range(2):
                        pbn = ia if half == 0 else ib
                        for jj in range(4):
                            dc = half * 4 + jj
                            fw.stt('dve', xr[:, dc, :], ps3(pbn)[:, jj, :], modT[:, 16 + dc, cidx:cidx + 1], xr[:, dc, :],
                                   ALU.mult, ALU.add, r=[ps[pbn], modT, xr], w=[xr])
                        yield
                    fw.dma('act', xs_v[:, :, t * 128:(t + 1) * 128], xr[:], r=[xr], w=[DB('xs', t)])
                return gen
            run_jobs([b3job(t) for t in range(NTL)], NE, stagger=4)
            fw.emit()

        if stop == 'b':
            break
        with contextlib.ExitStack() as st:
            x32 = alloc(st, "c_x", [128, 8, 1024])
            sqc = [alloc(st, "c_sq%d" % i, [128, 1024], BF16) for i in range(2)]
            rs = alloc(st, "c_rs", [128, 1024])
            tmp = [alloc(st, "c_tmp%d" % i, [128, 1024]) for i in range(2)]
            h16 = alloc(st, "c_h", [128, 8, 1024], BF16)
            a16 = alloc(st, "c_a", [128, 32, 1024], BF16)
            w1s = [alloc(st, "c_w1s%d" % i, [128, 2, 8, 128], BF16) for i in range(2)]
            w2s = [alloc(st, "c_w2s%d" % i, [128, 32, 128], BF16) for i in range(2)]
            rl = [alloc(st, "c_rl%d" % i, [128, 512]) for i in range(3)]
            ri = 0
            pi = 0
            for gi in range(NT // 1024):
                g0 = gi * 1024
                c = 0 if g0 < 1024 else 1
                tls = DBs('xs', gi * 8, gi * 8 + 8)
                fw.dma('sp', x32[:], xs_v[:, :, g0:g0 + 1024], r=tls, w=[x32])
                for kc in range(8):
                    s_ = sqc[kc % 2]
                    fw.act(s_[:], x32[:, kc, :], AF.Square, r=[x32], w=[s_])
                    for half in range(2):
                        fw.mm(ps[6 + half][:, :], ones16, s_[:, half * 512:(half + 1) * 512], start=(kc == 0), stop=(kc == 7),
                              r=[c16, s_], w=[ps[6 + half]])
                for half in range(2):
                    rstd(rs[:, half * 512:(half + 1) * 512], ps[6 + half][:, :], 1.0 / 1024, r=[ps[6 + half]], w=[rs])
                for kc in range(8):
                    tm = tmp[kc % 2]
                    fw.stt('dve', tm[:], x32[:, kc, :], w2T[:, kc, c:c + 1], rs[:], ALU.mult, ALU.mult, r=[x32, w2T, rs], w=[tm])
                    fw.act(h16[:, kc, :], tm[:], AF.Identity, bias=modT[:, 24 + kc, c:c + 1], r=[tm, modT], w=[h16])
                for j in range(32):
                    wv2 = w1s[(j // 2) % 2]
                    if j % 2 == 0:
                        fw.dma('sp', wv2[:].rearrange("p j k c -> p j (k c)"), w1c.ap()[j:j + 2].rearrange("j p n -> p j n"),
                               r=[DB('w1c', j // 4)], w=[wv2])
                    wv = T(wv2.t[:, j % 2, :, :]); wv.b = wv2.b
                    for half in range(2):
                        pb = ps[pi % 4]; pi += 1
                        for kc in range(8):
                            fw.mm(pb[:, :], wv[:, kc, :], h16[:, kc, half * 512:(half + 1) * 512], start=(kc == 0), stop=(kc == 7),
                                  r=[wv, h16], w=[pb])
                        r_ = rl[ri % 3]; ri += 1
                        fw.act(r_[:], pb[:, :], AF.Relu, r=[pb], w=[r_])
                        fw.tt('dve', a16[:, j, half * 512:(half + 1) * 512], r_[:], r_[:], ALU.mult, r=[r_], w=[a16])
                for dc in range(8):
                    wv = w2s[dc % 2]
                    fw.dma('sp', wv[:].rearrange("p j c -> p (j c)"), w2c.ap()[dc], r=[DB('w2c', dc)], w=[wv])
                    for half in range(2):
                        pb = ps[4 + (pi % 2)]; pi += 1
                        for j in range(32):
                            fw.mm(pb[:, :], wv[:, j, :], a16[:, j, half * 512:(half + 1) * 512], start=(j == 0), stop=(j == 31),
                                  r=[wv, a16], w=[pb])
                        xv = x32[:, dc, half * 512:(half + 1) * 512]
                        fw.stt('dve', xv, pb[:, :], modT[:, 40 + dc, c:c + 1], xv, ALU.mult, ALU.add, r=[pb, modT, x32], w=[x32])
                fw.dma('act', xs_v[:, :, g0:g0 + 1024], x32[:], r=[x32], w=tls)
            fw.emit()

    with contextlib.ExitStack() as st:
        xf = [alloc(st, "f_x%d" % i, [128, 8, 128]) for i in range(2)]
        sqf = alloc(st, "f_sq", [128, 8, 128], BF16)
        rsf = alloc(st, "f_rs", [128, 128])
        yo = [alloc(st, "f_y%d" % i, [128, 1024]) for i in range(2)]
        for t in range(0 if SKIPA else NTL):
            x = xf[t % 2]; yy = yo[t % 2]
            fw.dma('sp', x[:], xs_v[:, :, t * 128:(t + 1) * 128], r=[DB('xs', t)], w=[x])
            fw.act(sqf[:], x[:], AF.Square, r=[x], w=[sqf])
            pst = ps[4]
            for kc in range(8):
                fw.mm(pst[:, 0:128], ones16, sqf[:, kc, :], start=(kc == 0), stop=(kc == 7), r=[c16, sqf], w=[pst])
            rstd(rsf[:], pst[:, 0:128], 1.0 / 1024, r=[pst], w=[rsf])
            for kc in range(8):
                fw.stt('dve', x[:, kc, :], x[:, kc, :], nfT[:, kc:kc + 1], rsf[:], ALU.mult, ALU.mult, r=[x, nfT, rsf], w=[x])
            for half in range(2):
                pb = ps[(t % 2) * 2 + half]
                for j in range(4):
                    kc = half * 4 + j
                    fw.tr(pb[:, j * 128:(j + 1) * 128], x[:, kc, :], ident32, r=[x, c32], w=[pb])
                fw.cp('dve' if half == 0 else 'act', yy[:, half * 512:(half + 1) * 512], pb[:, :], r=[pb], w=[yy])
            fw.dma('act', y.ap()[t * 128:(t + 1) * 128, :], yy[:], r=[yy])
        fw.emit()
    stack0.close()
    return nc


def make_consts():
    idx = np.arange(128)
    same = (idx[:, None] // 64) == (idx[None, :] // 64)
    c = np.zeros((NCONST, 128, 128), np.float32)
    c[C_ID] = np.eye(128)
    c[C_ONES] = 1.0
    c[C_NEGONES] = -1.0
    c[C_UF] = same & (idx[:, None] <= idx[None, :])
    c[C_UB] = same & (idx[:, None] >= idx[None, :])
    c[C_SAME] = same
    c[C_CH0] = (idx[:, None] < 64) & np.ones((1, 128), bool)
    c[C_CH1] = (idx[:, None] >= 64) & np.ones((1, 128), bool)
    strict_f = same & (idx[None, :] < idx[:, None])
    strict_b = same & (idx[None, :] > idx[:, None])
    c[C_NSTRF] = -strict_f.astype(np.float32)
    c[C_NSTRB] = -strict_b.astype(np.float32)
    incl_f = same & (idx[None, :] <= idx[:, None])
    incl_b = same & (idx[None, :] >= idx[:, None])
    c[C_INTF] = incl_f.T.astype(np.float32) * (128.0 ** -0.5)
    c[C_INTB] = incl_b.T.astype(np.float32) * (128.0 ** -0.5)
    for i in range(6):
        b = 2 ** i
        c[C_MB + i] = ((idx[:, None] // (2 * b)) == (idx[None, :] // (2 * b))) & ((idx[:, None] // b) != (idx[None, :] // b))
    return c


_NC_CACHE = {}


def prep_inputs(inp, core):
    b = core // 4
    f = lambda a: np.ascontiguousarray(np.asarray(a, dtype=np.float32))
    xin = np.concatenate([f(inp['x_prompt'])[4 * core:4 * core + 4].reshape(1024, 1024), f(inp['x_sample'])[b]], axis=0)
    w1 = f(inp['w_ff1']).reshape(4, 8, 128, 32, 128).transpose(0, 3, 2, 1, 4).reshape(4, 32, 128, 1024)
    w2 = f(inp['w_ff2']).reshape(4, 32, 128, 8, 128).transpose(0, 3, 2, 1, 4).reshape(4, 8, 128, 4096)
    return {
        'xin': f(xin), 'cond': f(np.stack([f(inp['c_ctx']), f(inp['c'])[b]])), 's0': f(f(inp['state_delta'])[b]),
        'w_mod': f(inp['w_mod']), 'b_mod': f(inp['b_mod']), 'norm_mix': f(inp['norm_mix']), 'w_in': f(inp['w_in']),
        'conv_qkv': f(inp['conv_qkv']), 'a_log': f(inp['a_log']).reshape(4, 8), 'dt_bias': f(inp['dt_bias']).reshape(4, 8),
        'o_norm': f(inp['o_norm']), 'wsT': f(np.transpose(f(inp['w_spatial']), (0, 1, 3, 2))), 'b_sp': f(inp['b_spatial']),
        'w_out': f(inp['w_out']), 'norm_ffn': f(inp['norm_ffn']), 'w_ff1b': f(w1), 'w_ff2b': f(w2),
        'norm_final': f(inp['norm_final']), 'consts': make_consts(),
    }


def kernel(**inputs):
    if 'nc' not in _NC_CACHE:
        _NC_CACHE['nc'] = build()
    nc = _NC_CACHE['nc']
    shared = None
    in_maps = []
    for core in range(8):
        m = prep_inputs(inputs, core) if shared is None else None
        if shared is None:
            shared = m
        else:
            m = dict(shared)
            b = core // 4
            f = lambda a: np.ascontiguousarray(np.asarray(a, dtype=np.float32))
            m['xin'] = np.concatenate([f(inputs['x_prompt'])[4 * core:4 * core + 4].reshape(1024, 1024), f(inputs['x_sample'])[b]], axis=0)
            m['cond'] = f(np.stack([f(inputs['c_ctx']), f(inputs['c'])[b]]))
            m['s0'] = f(f(inputs['state_delta'])[b])
        in_maps.append(m)
    res = run_bass_kernel_spmd(nc, in_maps, core_ids=list(range(8)))
    outs = res.results
    y_prompt = np.concatenate([np.asarray(outs[c]['y'])[:1024].reshape(4, 256, 1024) for c in range(8)], axis=0)
    y_sample = np.stack([np.asarray(outs[0]['y'])[1024:], np.asarray(outs[4]['y'])[1024:]], axis=0)
    nsd = np.concatenate([np.asarray(outs[c]['ns']) for c in range(8)], axis=0)
    return (y_prompt.astype(np.float32), y_sample.astype(np.float32), nsd.astype(np.float32))
```

```python
import contextlib
import numpy as np
import concourse.bass as bass
import concourse.mybir as mybir
from concourse.bass_utils import run_bass_kernel_spmd

F32 = mybir.dt.float32
BF16 = mybir.dt.bfloat16
AF = mybir.ActivationFunctionType
ALU = mybir.AluOpType
AX = mybir.AxisListType

NT = 5120
NTL = NT // 128
EPS = 1e-6
LNSCALE = -0.5 * 4.852030263919617
SEQS = [(0, 256, 0), (256, 256, 0), (512, 256, 0), (768, 256, 0), (1024, 4096, 1)]
C_ID, C_ONES, C_UF, C_UB, C_SAME, C_CH0, C_CH1, C_NSTRF, C_NSTRB, C_INTF, C_INTB = range(11)
C_MB = 11
C_NEGONES = 17
NCONST = 18


class Buf:
    __slots__ = ('name', 'w', 'r', 'excl')

    def __init__(s, name=''):
        s.name = name
        s.w = None
        s.r = {}
        s.excl = False


class T:
    def __init__(s, t, name=''):
        s.t = t
        s.b = Buf(name)

    def __getitem__(s, k):
        return s.t[k]


def _b(x):
    return x.b if isinstance(x, T) else x


class FW:
    EPOCH = 8000
    NSW = 8
    NHW = 16
    MAXEP = 10
    SAME = ('act', 'dve', 'pool')

    def __init__(s, nc):
        s.nc = nc
        s.eng = {'pe': nc.tensor, 'act': nc.scalar, 'dve': nc.vector, 'pool': nc.gpsimd, 'sp': nc.sync}
        s.ops = {e: [] for e in s.eng}
        s.cnt = {e: 0 for e in s.eng}
        s.epoch = {e: 0 for e in s.eng}
        s.seen = {e: {} for e in s.eng}
        s.sems = {}
        for e in s.eng:
            for ep in range(s.MAXEP):
                k = (e, ep)
                s.sems[k] = nc.alloc_semaphore(name='s_%s_%d' % k)
        for cls, n in (('sw', s.NSW), ('hw', s.NHW)):
            for i in range(n):
                k = ('dma' + cls, i)
                s.sems[k] = nc.alloc_semaphore(name='s_%s_%d' % k)
        s.dma_i = {'sw': 0, 'hw': 0}
        s.dma_use = {'sw': [0] * s.NSW, 'hw': [0] * s.NHW}
        s.nops = 0

    def _deps(s, eng, reads, writes):
        need = {}

        def add(ev):
            if ev is None:
                return
            k, v = ev
            if need.get(k, 0) < v:
                need[k] = v
        for b in reads:
            b = _b(b)
            add(b.w)
            if b.excl:
                for k, v in b.r.items():
                    if k[0] != eng:
                        add((k, v))
        for b in writes:
            b = _b(b)
            add(b.w)
            for k, v in b.r.items():
                add((k, v))
        waits = []
        for k, v in need.items():
            if k[0] == eng and eng not in s.SAME:
                continue
            if s.seen[eng].get(k, 0) < v:
                s.seen[eng][k] = v
                waits.append((k, v))
        return waits

    def _commit(s, ev, reads, writes):
        k, v = ev
        for b in writes:
            b = _b(b)
            b.w = ev
            b.r = {}
        wset = set(id(_b(b)) for b in writes)
        for b in reads:
            b = _b(b)
            if id(b) in wset:
                continue
            if b.r.get(k, 0) < v:
                b.r[k] = v

    def op(s, eng, fn, reads=(), writes=()):
        waits = s._deps(eng, reads, writes)
        if s.cnt[eng] >= s.EPOCH:
            s.epoch[eng] += 1
            s.cnt[eng] = 0
        s.cnt[eng] += 1
        key = (eng, s.epoch[eng])
        val = s.cnt[eng]
        s.ops[eng].append((waits, fn, key, 1))
        s.nops += 1
        s._commit((key, val), reads, writes)

    def dma(s, q, out, in_, r=(), w=(), **kw):
        cls = 'sw' if q == 'pool' else 'hw'
        n = s.NSW if cls == 'sw' else s.NHW
        slot = s.dma_i[cls] % n
        s.dma_i[cls] += 1
        prev = s.dma_use[cls][slot]
        key = ('dma' + cls, slot)
        waits = s._deps(q, r, w)
        if prev > 0 and s.seen[q].get(key, 0) < 16 * prev:
            s.seen[q][key] = 16 * prev
            waits.append((key, 16 * prev))
        s.dma_use[cls][slot] = prev + 1
        s.ops[q].append((waits, lambda e: e.dma_start(out=out, in_=in_, **kw), key, 16))
        s.nops += 1
        s._commit((key, 16 * (prev + 1)), r, w)

    def barrier(s):
        evs = []
        for e in s.eng:
            for ep in range(s.epoch[e] + 1):
                c = s.cnt[e] if ep == s.epoch[e] else s.EPOCH
                if c > 0:
                    evs.append(((e, ep), c))
        for cls in ('sw', 'hw'):
            for slot in range(len(s.dma_use[cls])):
                if s.dma_use[cls][slot] > 0:
                    evs.append((('dma' + cls, slot), 16 * s.dma_use[cls][slot]))
        for e in s.eng:
            waits = []
            for k, v in evs:
                if s.seen[e].get(k, 0) < v:
                    s.seen[e][k] = v
                    waits.append((k, v))
            if waits:
                s.ops[e].append((waits, None, None, 0))

    def emit(s):
        s.barrier()
        nc = s.nc
        keys = set()
        for e in s.ops:
            for (w, f, k, i) in s.ops[e]:
                if k is not None:
                    keys.add(k)
                for (kk, v) in w:
                    keys.add(kk)
        for k in sorted(keys, key=str):
            if k not in s.sems:
                s.sems[k] = nc.alloc_semaphore(name='s_' + '_'.join(str(x) for x in k))
        ops = s.ops
        s.ops = {e: [] for e in s.eng}
        with nc.Block() as block:
            def mk(ename):
                def body(e):
                    for waits, fn, key, inc in ops[ename]:
                        for k, v in waits:
                            e.wait_ge(s.sems[k], v)
                        if fn is not None:
                            fn(e).then_inc(s.sems[key], inc)
                return body
            block.tensor(mk('pe'))
            block.scalar(mk('act'))
            block.vector(mk('dve'))
            block.gpsimd(mk('pool'))
            block.sync(mk('sp'))

    def mm(s, out, lhsT, rhs, start=True, stop=True, r=(), w=()):
        s.op('pe', lambda e: e.matmul(out, lhsT, rhs, start=start, stop=stop), r, w)

    def tr(s, out, in_, ident, r=(), w=()):
        s.op('pe', lambda e: e.transpose(out, in_, ident), r, w)

    def act(s, out, in_, func, bias=0.0, scale=1.0, r=(), w=()):
        s.op('act', lambda e: e.activation(out, in_, func, bias=bias, scale=scale), r, w)

    def tt(s, eng, out, in0, in1, op, r=(), w=()):
        s.op(eng, lambda e: e.tensor_tensor(out, in0, in1, op), r, w)

    def ts(s, eng, out, in0, s1, s2, op0, op1=None, r=(), w=()):
        if op1 is None:
            s.op(eng, lambda e: e.tensor_scalar(out, in0, s1, None, op0), r, w)
        else:
            s.op(eng, lambda e: e.tensor_scalar(out, in0, s1, s2, op0, op1), r, w)

    def stt(s, eng, out, in0, scalar, in1, op0, op1, r=(), w=()):
        s.op(eng, lambda e: e.scalar_tensor_tensor(out, in0, scalar, in1, op0, op1), r, w)

    def cp(s, eng, out, in_, r=(), w=()):
        if eng == 'act':
            s.op(eng, lambda e: e.copy(out, in_), r, w)
        else:
            s.op(eng, lambda e: e.tensor_copy(out, in_), r, w)

    def memset(s, eng, ap, val, w=()):
        s.op(eng, lambda e: e.memset(ap, val), (), w)

    def recip(s, out, in_, r=(), w=()):
        s.op('dve', lambda e: e.reciprocal(out, in_), r, w)


def build(L=4, dbg=False, stop=None):
    nc = bass.Bass("TRN2", target_bir_lowering=False)
    fw = FW(nc)

    def din(name, shape, dt=F32):
        return nc.dram_tensor(name, list(shape), dt, kind="ExternalInput")

    def dscr(name, shape, dt=F32):
        return nc.dram_tensor(name, list(shape), dt, kind=("ExternalOutput" if dbg else "Internal"))

    xin = din("xin", [NT, 1024]); cond = din("cond", [2, 1024]); s0 = din("s0", [4, 2, 4, 128, 128])
    w_mod = din("w_mod", [4, 1024, 6144]); b_mod = din("b_mod", [4, 6144]); norm_mix = din("norm_mix", [4, 1024])
    w_in = din("w_in", [4, 1024, 3088]); conv_qkv = din("conv_qkv", [4, 3, 1536]); a_log = din("a_log", [4, 8])
    dt_bias = din("dt_bias", [4, 8]); o_norm = din("o_norm", [4, 128]); wsT = din("wsT", [4, 4, 128, 128])
    b_sp = din("b_sp", [4, 4, 128]); w_out = din("w_out", [4, 1024, 1024]); norm_ffn = din("norm_ffn", [4, 1024])
    w_ff1b = din("w_ff1b", [4, 32, 128, 1024]); w_ff2b = din("w_ff2b", [4, 8, 128, 4096])
    norm_final = din("norm_final", [1024]); consts = din("consts", [NCONST, 128, 128])
    y = nc.dram_tensor("y", [NT, 1024], F32, kind="ExternalOutput")
    ns = nc.dram_tensor("ns", [4, 4, 2, 4, 128, 128], F32, kind="ExternalOutput")
    xs = dscr("xs", [1024, NT]); qkvs = dscr("qkvs", [1536, NT]); gates = dscr("gates", [512, NT])
    obs = dscr("obs", [512, NT], BF16); bgs = dscr("bgs", [NT, 16]); blobs = dscr("blobs", [NTL, 128, 2048], BF16)
    ofs = dscr("ofs", [NTL, 128, 512]); ofs2 = dscr("ofs2", [NTL, 128, 512])
    w1c = nc.dram_tensor("w1c", [32, 128, 1024], BF16, kind="Internal"); w2c = nc.dram_tensor("w2c", [8, 128, 4096], BF16, kind="Internal")

    dbufs = {}

    def DB(name, i):
        k = (name, i)
        if k not in dbufs:
            dbufs[k] = Buf(name + str(i))
        return dbufs[k]

    def DBs(name, t0, t1):
        return [DB(name, t) for t in range(max(t0, 0), min(t1, NTL))]

    xs_v = xs.ap().rearrange("(kc p) t -> p kc t", p=128)
    qkvs_v = qkvs.ap().rearrange("(n p) t -> p n t", p=128)
    gates_v = gates.ap().rearrange("(n p) t -> p n t", p=128)
    obs_v = obs.ap().rearrange("(n p) t -> p n t", p=128)

    stack0 = contextlib.ExitStack()

    uid = [0]

    def alloc(stack, name, shape, dt=F32):
        uid[0] += 1
        name = "%s_u%d" % (name, uid[0])
        return T(stack.enter_context(nc.sbuf_tensor(name, list(shape), dt)), name)

    ps = [T(stack0.enter_context(nc.psum_tensor("psb%d" % i, [128, 512], F32)), "ps%d" % i) for i in range(8)]
    for p_ in ps:
        p_.b.excl = True

    def ps3(i, h=4):
        return ps[i].t[:].rearrange("p (h c) -> p h c", h=h)

    def psbf(i):
        return ps[i].t[:].bitcast(BF16)

    c32 = alloc(stack0, "c32", [128, NCONST, 128], F32)
    c16 = alloc(stack0, "c16", [128, NCONST, 128], BF16)
    fw.dma('sp', c32[:], consts.ap().rearrange("n p c -> p n c"), w=[c32])
    fw.dma('pool', c16[:], consts.ap().rearrange("n p c -> p n c"), w=[c16])
    epsT = alloc(stack0, "epsT", [128, 1], F32)
    fw.memset('pool', epsT[:], EPS, w=[epsT])
    condT = alloc(stack0, "condT", [128, 8, 2], F32)
    scT = alloc(stack0, "scT", [128, 8, 2], BF16)
    for c in range(2):
        fw.dma('sp', condT[:, :, c], cond.ap()[c].rearrange("(kc p) -> p kc", p=128), w=[condT],
               allow_slow_non_contiguous=True)
    fw.act(scT[:], condT[:], AF.Silu, r=[condT], w=[scT])
    nfT = alloc(stack0, "nfT", [128, 8], F32)
    fw.dma('sp', nfT[:], norm_final.ap().rearrange("(kc p) -> p kc", p=128), w=[nfT], allow_slow_non_contiguous=True)
    nmT = alloc(stack0, "nmT", [128, 4, 8], F32); nffT = alloc(stack0, "nffT", [128, 4, 8], F32)
    bmT = alloc(stack0, "bmT", [128, 4, 48], F32); cwT = alloc(stack0, "cwT", [128, 4, 12, 3], F32)
    onT = alloc(stack0, "onT", [128, 4], F32)
    dtb = alloc(stack0, "dtb", [128, 4, 8], F32); nA = alloc(stack0, "nA", [128, 4, 8], F32)
    ws16 = alloc(stack0, "ws16", [128, 4, 4, 128], BF16); bs16 = alloc(stack0, "bs16", [1, 4, 4, 128], BF16)
    for l in range(L):
        fw.dma('sp', nmT[:, l, :], norm_mix.ap()[l].rearrange("(kc p) -> p kc", p=128), w=[nmT], allow_slow_non_contiguous=True)
        fw.dma('sp', nffT[:, l, :], norm_ffn.ap()[l].rearrange("(kc p) -> p kc", p=128), w=[nffT], allow_slow_non_contiguous=True)
        fw.dma('sp', bmT[:, l, :], b_mod.ap()[l].rearrange("(n p) -> p n", p=128), w=[bmT], allow_slow_non_contiguous=True)
        for wi in range(3):
            fw.dma('sp', cwT[:, l, :, wi], conv_qkv.ap()[l, wi].rearrange("(n p) -> p n", p=128), w=[cwT], allow_slow_non_contiguous=True)
        fw.dma('sp', onT[:, l:l + 1], o_norm.ap()[l].rearrange("(p o) -> p o", o=1), w=[onT], allow_slow_non_contiguous=True)
        fw.dma('sp', dtb[:, l, :], dt_bias.ap()[l].partition_broadcast(128), w=[dtb])
        fw.dma('sp', nA[:, l, :], a_log.ap()[l].partition_broadcast(128), w=[nA])
        fw.dma('pool', ws16[:, l, :, :], wsT.ap()[l].rearrange("g s t -> s g t"), w=[ws16])
        fw.dma('pool', bs16[:, l, :, :], b_sp.ap()[l:l + 1], w=[bs16])
    fw.act(nA[:, 0:L, :], nA[:, 0:L, :], AF.Exp, r=[nA], w=[nA])
    fw.ts('dve', nA[:, 0:L, :], nA[:, 0:L, :], -1.0, None, ALU.mult, r=[nA], w=[nA])

    ident32 = c32[:, C_ID, :]
    ident16 = c16[:, C_ID, :]
    ones16 = c16[:, C_ONES, :]
    ones32 = c32[:, C_ONES, :]

    def rstd(out, in_, scale, r, w):
        fw.act(out, in_, AF.Ln, bias=epsT[:, 0:1], scale=scale, r=list(r) + [epsT], w=w)
        fw.act(out, out, AF.Exp, scale=-0.5, r=w, w=w)

    import os
    SKIPA = os.environ.get('SKIPA') == '1'
    def run_jobs(factories, NJ, stagger=0):
        pending = list(factories)
        active = [None] * NJ
        delay = [sl * stagger for sl in range(NJ)]
        while pending or any(a is not None for a in active):
            for sl in range(NJ):
                if delay[sl] > 0:
                    delay[sl] -= 1
                    continue
                if active[sl] is None and pending:
                    active[sl] = pending.pop(0)(sl)
                if active[sl] is not None:
                    try:
                        next(active[sl])
                    except StopIteration:
                        active[sl] = None

    with contextlib.ExitStack() as st:
        xt = [alloc(st, "x0t%d" % i, [128, 1024]) for i in range(2)]
        xo = [alloc(st, "x0o%d" % i, [128, 8, 128]) for i in range(2)]
        for t in range(0 if SKIPA else NTL):
            a = xt[t % 2]; o = xo[t % 2]
            fw.dma('sp', a[:], xin.ap()[t * 128:(t + 1) * 128, :], w=[a])
            for half in range(2):
                pb = ps[(t % 2) * 2 + half]
                for j in range(4):
                    kc = half * 4 + j
                    fw.tr(pb[:, j * 128:(j + 1) * 128], a[:, kc * 128:(kc + 1) * 128], ident32, r=[a, c32], w=[pb])
                fw.cp('dve' if half == 0 else 'act', o[:, half * 4:(half + 1) * 4, :], ps3((t % 2) * 2 + half), r=[pb], w=[o])
            fw.dma('act', xs_v[:, :, t * 128:(t + 1) * 128], o[:], r=[o], w=[DB('xs', t)])
        fw.emit()

    modT = alloc(stack0, "modT", [128, 48, 2], F32)
    w1T = alloc(stack0, "w1T", [128, 8, 2], F32)
    w2T = alloc(stack0, "w2T", [128, 8, 2], F32)

    for l in range(L):
        if stop == 'x0':
            break
        with contextlib.ExitStack() as st:
            wm = [alloc(st, "wm%d" % i, [128, 8, 512], BF16) for i in range(2)]
            pm = ps[0]
            for nb in range(0 if SKIPA else 12):
                wb = wm[nb % 2]
                fw.dma('pool', wb[:], w_mod.ap()[l].rearrange("(kc p) n -> p kc n", p=128)[:, :, nb * 512:(nb + 1) * 512], w=[wb])
                for cc in range(4):
                    n = nb * 4 + cc
                    for kc in range(8):
                        fw.mm(pm[:, n * 2:n * 2 + 2], wb[:, kc, cc * 128:(cc + 1) * 128], scT[:, kc, :],
                              start=(kc == 0), stop=(kc == 7), r=[wb, scT], w=[pm])
            pmv = pm.t[:, 0:96].rearrange("p (n c) -> p n c", c=2)
            for c in range(0 if SKIPA else 2):
                fw.tt('dve', modT[:, :, c], pmv[:, :, c], bmT[:, l, :], ALU.add, r=[pm, bmT], w=[modT])
                fw.stt('dve', w1T[:, :, c], modT[:, 8:16, c], 1.0, nmT[:, l, :], ALU.add, ALU.mult, r=[modT, nmT], w=[w1T])
                fw.stt('dve', w2T[:, :, c], modT[:, 32:40, c], 1.0, nffT[:, l, :], ALU.add, ALU.mult, r=[modT, nffT], w=[w2T])
            fw.emit()

        if stop == 'm':
            break
        with contextlib.ExitStack() as st:
            win = alloc(st, "win", [128, 8, 3088], BF16)
            w_in_v = w_in.ap()[l].rearrange("(kc p) n -> p kc n", p=128)
            for kc in range(8):
                fw.dma('pool', win[:, kc, :], w_in_v[:, kc, :], w=[win])
            NA = 2
            def perA(name, shape, dt=F32, k=1):
                return [[alloc(st, "%s%d_%d" % (name, i, j), shape, dt) for j in range(k)] for i in range(NA)]
            xT = perA("a_x", [128, 8, 512]); sqA = perA("a_sq", [128, 8, 512], BF16); rsA = perA("a_rs", [128, 512])
            tmpA = perA("a_tmp", [128, 512], F32, 2); hA = perA("a_h", [128, 8, 512], BF16); stgA = perA("a_stg", [128, 512], F32, 4)
            uA = perA("a_u", [128, 4, 512]); bgA = perA("a_bg", [128, 16], F32, 2); vgfA = perA("a_vgf", [128, 4, 128], F32, 2)
            vgnA = perA("a_vgn", [128, 4, 128], BF16, 2); st4A = perA("a_st4", [128, 4], F32, 2); obA = perA("a_ob", [128, 4, 128], BF16, 2)
            COLS = [n * 128 for n in range(12)] + [1536 + n * 128 for n in range(4)] + [2064 + n * 128 for n in range(4)]

            def ajob(gi):
                def gen(sl):
                    base = 4 * sl
                    g0 = gi * 512
                    c = 0 if g0 < 1024 else 1
                    x = xT[sl][0]; sq = sqA[sl][0]; rs = rsA[sl][0]; h16 = hA[sl][0]; uT = uA[sl][0]
                    tls = DBs('xs', gi * 4, gi * 4 + 4)
                    fw.dma('sp', x[:], xs_v[:, :, g0:g0 + 512], r=tls, w=[x])
                    yield
                    fw.act(sq[:], x[:], AF.Square, r=[x], w=[sq])
                    yield
                    pst = ps[base + 2]
                    for kc in range(8):
                        fw.mm(pst[:, :], ones16, sq[:, kc, :], start=(kc == 0), stop=(kc == 7), r=[c16, sq], w=[pst])
                    yield
                    fw.act(rs[:], pst[:, :], AF.Ln, bias=epsT[:, 0:1], scale=1.0 / 1024, r=[pst, epsT], w=[rs])
                    yield
                    fw.act(rs[:], rs[:], AF.Exp, scale=-0.5, r=[rs], w=[rs])
                    yield
                    for kc in range(8):
                        tm = tmpA[sl][kc % 2]
                        fw.stt('dve', tm[:], x[:, kc, :], w1T[:, kc, c:c + 1], rs[:], ALU.mult, ALU.mult, r=[x, w1T, rs], w=[tm])
                        fw.act(h16[:, kc, :], tm[:], AF.Identity, bias=modT[:, kc, c:c + 1], r=[tm, modT], w=[h16])
                        if kc % 2 == 1:
                            yield
                    si = 0
                    for n in range(20):
                        pb = ps[base + (n % 2)]
                        col = COLS[n]
                        for kc in range(8):
                            fw.mm(pb[:, :], win[:, kc, col:col + 128], h16[:, kc, :], start=(kc == 0), stop=(kc == 7),
                                  r=[win, h16], w=[pb])
                        if n < 12:
                            sg = stgA[sl][si % 4]; si += 1
                            fw.cp('dve' if n % 2 == 0 else 'act', sg[:], pb[:, :], r=[pb], w=[sg])
                            fw.dma('sp', qkvs_v[:, n, g0:g0 + 512], sg[:], r=[sg], w=DBs('qkvs', gi * 4, gi * 4 + 4))
                        elif n < 16:
                            sg = stgA[sl][si % 4]; si += 1
                            fw.act(sg[:], pb[:, :], AF.Silu, r=[pb], w=[sg])
                            fw.dma('sp', gates_v[:, n - 12, g0:g0 + 512], sg[:], r=[sg], w=DBs('gates', gi * 4, gi * 4 + 4))
                        else:
                            fw.act(uT[:, n - 16, :], pb[:, :], AF.Gelu, r=[pb], w=[uT])
                        yield
                    for tt_ in range(4):
                        t = gi * 4 + tt_
                        tok = slice(tt_ * 128, (tt_ + 1) * 128)
                        pab = ps[base + 2]; pvg = ps[base + 3]
                        b = bgA[sl][tt_ % 2]; vf = vgfA[sl][tt_ % 2]; vn = vgnA[sl][tt_ % 2]; s4 = st4A[sl][tt_ % 2]; ob = obA[sl][tt_ % 2]
                        for kc in range(8):
                            fw.mm(pab[:, 0:16], h16[:, kc, tok], win[:, kc, 2048:2064], start=(kc == 0), stop=(kc == 7),
                                  r=[h16, win], w=[pab])
                        for kc in range(8):
                            fw.mm(pvg[:, :], h16[:, kc, tok], win[:, kc, 2576:3088], start=(kc == 0), stop=(kc == 7),
                                  r=[h16, win], w=[pvg])
                        yield
                        fw.act(b[:, 0:8], pab[:, 0:8], AF.Sigmoid, r=[pab], w=[b])
                        fw.act(vf[:], ps3(base + 3), AF.Gelu, r=[pvg], w=[vf])
                        yield
                        fw.tt('dve', b[:, 8:16], pab[:, 8:16], dtb[:, l, :], ALU.add, r=[pab, dtb], w=[b])
                        fw.op('dve', lambda e, s4=s4, vf=vf: e.reduce_sum(s4[:], vf[:], AX.X), [vf], [s4])
                        yield
                        fw.act(b[:, 8:16], b[:, 8:16], AF.Exp, r=[b], w=[b])
                        fw.ts('dve', s4[:], s4[:], -1.0 / 128, None, ALU.mult, r=[s4], w=[s4])
                        yield
                        fw.act(b[:, 8:16], b[:, 8:16], AF.Ln, bias=1.0, r=[b], w=[b])
                        fw.tt('dve', vf[:], vf[:], s4[:].unsqueeze(2).to_broadcast([128, 4, 128]), ALU.add, r=[vf, s4], w=[vf])
                        yield
                        fw.tt('dve', b[:, 8:16], b[:, 8:16], nA[:, l, :], ALU.mult, r=[b, nA], w=[b])
                        sg = stgA[sl][si % 4]; si += 1
                        sg3 = sg.t[:].rearrange("p (h c) -> p h c", h=4)
                        fw.tt('pool', sg3, vf[:], vf[:], ALU.mult, r=[vf], w=[sg])
                        yield
                        fw.dma('act', bgs.ap()[t * 128:(t + 1) * 128, :], b[:], r=[b], w=[DB('bgs', t)])
                        fw.op('dve', lambda e, s4=s4, sg3=sg3: e.reduce_sum(s4[:], sg3, AX.X), [sg], [s4])
                        yield
                        fw.act(s4[:], s4[:], AF.Ln, bias=epsT[:, 0:1], scale=1.0 / 128, r=[s4, epsT], w=[s4])
                        yield
                        fw.act(s4[:], s4[:], AF.Exp, scale=-0.5, r=[s4], w=[s4])
                        yield
                        fw.tt('dve', vn[:], vf[:], s4[:].unsqueeze(2).to_broadcast([128, 4, 128]), ALU.mult, r=[vf, s4], w=[vn])
                        yield
                        pmx = ps[base + 2]
                        for g in range(4):
                            fw.mm(pmx[:, g * 128:(g + 1) * 128], vn[:, g, :], ws16[:, l, g, :], start=True, stop=False, r=[vn, ws16], w=[pmx])
                            fw.mm(pmx[:, g * 128:(g + 1) * 128], c16[0:1, C_ONES, :], bs16[0:1, l, g, :], start=False, stop=True,
                                  r=[c16, bs16], w=[pmx])
                        yield
                        fw.tt('dve', ob[:], uT[:, :, tok], ps3(base + 2), ALU.mult, r=[uT, pmx], w=[ob])
                        yield
                        fw.dma('act', obs_v[:, :, t * 128:(t + 1) * 128], ob[:], r=[ob], w=[DB('obs', t)])
                return gen
            run_jobs([ajob(gi) for gi in range(0 if SKIPA else NT // 512)], NA, stagger=40)
            fw.emit()

        if stop == 'a':
            break
        import os
        with contextlib.ExitStack() as st:
            NB0 = 3
            def per0(name, shape, dt=F32):
                return [alloc(st, "%s%d" % (name, i), shape, dt) for i in range(NB0)]
            pre = per0("b_pre", [128, 12, 130]); cv = per0("b_cv", [128, 12, 128]); cv2 = per0("b_cv2", [128, 12, 128])
            cv3 = per0("b_cv3", [128, 12, 128]); sq16 = per0("b_sq16", [128, 8, 128], BF16); rn = per0("b_rn", [128, 8, 128])
            vT16 = per0("b_vT16", [128, 4, 128], BF16); blob0 = per0("b_blob", [128, 4, 4, 128], BF16)
            cw = cwT.t[:, l, :, :]

            def b0job(t, sstart, slen):
                def gen(sl):
                    p = pre[sl]; c_ = cv[sl]; c2 = cv2[sl]; c3 = cv3[sl]; bl = blob0[sl]; sq_ = sq16[sl]; rn_ = rn[sl]; vt_ = vT16[sl]
                    ia = 2 * sl; ib = 2 * sl + 1
                    t0 = t * 128
                    lo = max(t0 - 1, sstart); hi = min(t0 + 129, sstart + slen)
                    if lo > t0 - 1:
                        fw.memset('pool', p[:, :, 0:1], 0.0, w=[p])
                    if hi < t0 + 129:
                        fw.memset('pool', p[:, :, 129:130], 0.0, w=[p])
                    for n3 in range(3):
                        fw.dma('sp', p[:, n3 * 4:(n3 + 1) * 4, lo - (t0 - 1):hi - (t0 - 1)], qkvs_v[:, n3 * 4:(n3 + 1) * 4, lo:hi],
                               r=DBs('qkvs', t - 1, t + 2), w=[p])
                    yield
                    fw.tt('dve', c_[:], p[:, :, 0:128], cw[:, :, 0:1].to_broadcast([128, 12, 128]), ALU.mult, r=[p, cwT], w=[c_])
                    fw.tt('pool', c2[:], p[:, :, 1:129], cw[:, :, 1:2].to_broadcast([128, 12, 128]), ALU.mult, r=[p, cwT], w=[c2])
                    fw.tt('pool', c3[:], p[:, :, 2:130], cw[:, :, 2:3].to_broadcast([128, 12, 128]), ALU.mult, r=[p, cwT], w=[c3])
                    yield
                    fw.tt('dve', c_[:], c_[:], c2[:], ALU.add, r=[c_, c2], w=[c_])
                    yield
                    fw.tt('dve', c_[:], c_[:], c3[:], ALU.add, r=[c_, c3], w=[c_])
                    yield
                    fw.act(c_[:], c_[:], AF.Silu, r=[c_], w=[c_])
                    yield
                    fw.act(sq_[:], c_[:, 0:8, :], AF.Square, r=[c_], w=[sq_])
                    fw.cp('act', vt_[:], c_[:, 8:12, :], r=[c_], w=[vt_])
                    yield
                    fw.mm(ps[ia][:, :], ones16, sq_[:, 0:4, :], r=[c16, sq_], w=[ps[ia]])
                    fw.mm(ps[ib][:, :], ones16, sq_[:, 4:8, :], r=[c16, sq_], w=[ps[ib]])
                    yield
                    fw.act(rn_[:, 0:4, :], ps3(ia), AF.Ln, bias=epsT[:, 0:1], scale=1.0, r=[ps[ia], epsT], w=[rn_])
                    fw.act(rn_[:, 4:8, :], ps3(ib), AF.Ln, bias=epsT[:, 0:1], scale=1.0, r=[ps[ib], epsT], w=[rn_])
                    yield
                    fw.act(rn_[:], rn_[:], AF.Exp, scale=-0.5, r=[rn_], w=[rn_])
                    yield
                    fw.tt('dve', bl[:, 0:2, :, :], c_[:, 0:8, :].rearrange("p (a h) c -> p a h c", a=2),
                          rn_[:].rearrange("p (a h) c -> p a h c", a=2), ALU.mult, r=[c_, rn_], w=[bl])
                    yield
                    pv = psbf(ia)
                    for h in range(4):
                        fw.tr(pv[:, h * 128:(h + 1) * 128], bl[:, 1, h, :], ident16, r=[bl, c16], w=[ps[ia]])
                        fw.tr(pv[:, 512 + h * 128:512 + (h + 1) * 128], vt_[:, h, :], ident16, r=[vt_, c16], w=[ps[ia]])
                    yield
                    fw.cp('dve', bl[:, 2:4, :, :], pv.rearrange("p (a h c) -> p a h c", a=2, h=4), r=[ps[ia]], w=[bl])
                    yield
                    fw.dma('act', blobs.ap()[t], bl[:].rearrange("p a h c -> p (a h c)"), r=[bl], w=[DB('blobs', t)])
                return gen
            jobs0 = []
            for (sstart, slen, cidx) in SEQS:
                for t in range(sstart // 128, (sstart + slen) // 128):
                    jobs0.append(b0job(t, sstart, slen))
            run_jobs(jobs0, NB0, stagger=4)
            fw.emit()

        if stop == 'b0':
            break
        with contextlib.ExitStack() as st:
            NJ = 4
            def per(name, shape, dt=F32):
                return [alloc(st, "%s%d" % (name, i), shape, dt) for i in range(NJ)]
            blobj = per("j_blob", [128, 4, 4, 128], BF16)
            bgt = per("j_bg", [128, 16]); Gs = per("j_Gs", [128, 16]); eG = per("j_eG", [128, 16]); sck = per("j_sck", [128, 4])
            gpad = per("j_gpad", [128, 128]); ngj = per("j_ng", [128, 4])
            for i in range(NJ):
                fw.memset('pool', gpad[i][:], 0.0, w=[gpad[i]])
            for j0 in range(0, 32, 4):
                fw.dma('pool', w1c.ap()[j0:j0 + 4], w_ff1b.ap()[l, j0:j0 + 4], w=[DB('w1c', j0 // 4)])
            for dc in range(8):
                fw.dma('pool', w2c.ap()[dc], w_ff2b.ap()[l, dc], w=[DB('w2c', dc)])
            Ugj = per("j_Ug", [128, 4, 128]); decj = per("j_dec", [128, 4, 128]); decTj = per("j_decT", [128, 4, 128])
            eGBj = per("j_eGB", [128, 4, 128]); nm32j = per("j_nm32", [128, 4, 128])
            Pj = per("j_P", [128, 4, 128], BF16); Qj = per("j_Q", [128, 4, 128], BF16); QMj = per("j_QM", [128, 5, 4, 128], BF16)
            TT16j = per("j_TT16", [128, 4, 128], BF16)
            T16j = per("j_T16", [128, 4, 128], BF16); X16j = per("j_X16", [128, 4, 128], BF16)
            vbj = per("j_vb", [128, 4, 128], BF16); kbgj = per("j_kbg", [128, 4, 128], BF16)
            kdj = per("j_kd", [128, 4, 128], BF16); wTj = per("j_wT", [128, 4, 128], BF16)
            qkj = per("j_qk", [128, 4, 128], BF16); qdj = per("j_qd", [128, 4, 128], BF16)
            u32j = per("j_u", [128, 4, 128]); vnj = per("j_vn", [128, 4, 128], BF16); oTj = per("j_oT", [128, 4, 128])
            NCH = 8
            S32c = [alloc(st, "j_S32_%d" % i, [128, 4, 128]) for i in range(NCH)]
            S16c = [alloc(st, "j_S16_%d" % i, [128, 4, 128], BF16) for i in range(NCH)]
            negones32 = c32[:, C_NEGONES, :]

            def job(slot, ci, t, d, cidx, is_last, seq_i, turn, my_idx):
                bl = blobj[slot]; b = bgt[slot]; G = Gs[slot]; e_ = eG[slot]; sk = sck[slot]; gp = gpad[slot]; ng = ngj[slot]
                Ug = Ugj[slot]; dec = decj[slot]; decT = decTj[slot]; eGB = eGBj[slot]; nm32 = nm32j[slot]
                P = Pj[slot]; Q = Qj[slot]; QM = QMj[slot]; TT16 = TT16j[slot]; T16 = T16j[slot]; X16 = X16j[slot]
                vb16 = vbj[slot]; kbg16 = kbgj[slot]; kd = kdj[slot]; wT = wTj[slot]; qk = qkj[slot]; qd = qdj[slot]
                u = u32j[slot]; vn = vnj[slot]; oT = oTj[slot]
                pa = ps[2 * slot]; pb_ = ps[2 * slot + 1]; ia = 2 * slot; ib = 2 * slot + 1
                fw.dma('sp', bl[:].rearrange("p a h c -> p (a h c)"), blobs.ap()[t], r=[DB('blobs', t)], w=[bl])
                fw.dma('sp', b[:], bgs.ap()[t * 128:(t + 1) * 128, :], r=[DB('bgs', t)], w=[b])
                g4 = b[:, 8 + d * 4:12 + d * 4]; b4 = b[:, d * 4:d * 4 + 4]
                U = c32[:, C_UF + d, :]
                fw.cp('dve', gp[:, 0:4], g4, r=[b], w=[gp])
                fw.mm(pa[:, 0:128], U, gp[:], r=[c32, gp], w=[pa])
                fw.mm(pa[:, 128:256], c32[:, C_SAME, :], gp[:], r=[c32, gp], w=[pa])
                fw.mm(pa[:, 256:384], c32[:, C_CH0, :], gp[:], r=[c32, gp], w=[pa])
                fw.mm(pa[:, 384:512], c32[:, C_CH1, :], gp[:], r=[c32, gp], w=[pa])
                for h in range(4):
                    fw.act(Ug[:, h, :], U, AF.Identity, scale=g4[:, h:h + 1], r=[c32, b], w=[Ug])
                yield
                fw.cp('dve', G[:].rearrange("p (a b) -> p a b", a=4), ps3(ia)[:, :, 0:4], r=[pa], w=[G])
                fw.tt('dve', G[:, 4:8], G[:, 4:8], G[:, 0:4], ALU.subtract, r=[G], w=[G])
                fw.act(e_[:], G[:], AF.Exp, r=[G], w=[e_])
                fw.ts('dve', ng[:], G[:, 0:4], -1.0, None, ALU.mult, r=[G], w=[ng])
                for h in range(4):
                    fw.mm(pa[:, h * 128:(h + 1) * 128], ones32, Ug[:, h, :], r=[Ug, c32], w=[pa])
                yield
                fw.tt('dve', sk[:], b4, e_[:, 0:4], ALU.mult, r=[b, e_], w=[sk])
                for h in range(4):
                    fw.act(dec[:, h, :], ps3(ia)[:, h, :], AF.Relu, bias=ng[:, h:h + 1], scale=1.0, r=[pa, ng], w=[dec])
                for h in range(4):
                    fw.act(decT[:, h, :], ps3(ia)[:, h, :], AF.Relu, bias=G[:, h:h + 1], scale=-1.0, r=[pa, G], w=[decT])
                fw.act(eGB[:], ps3(ia), AF.Exp, bias=LNSCALE, r=[pa], w=[eGB])
                for h in range(4):
                    fw.mm(pb_[:, h * 128:(h + 1) * 128], bl[:, 1, h, :], bl[:, 1, h, :], r=[bl], w=[pb_])
                yield
                fw.act(dec[:], dec[:], AF.Exp, scale=-1.0, r=[dec], w=[dec])
                fw.act(decT[:], decT[:], AF.Exp, scale=-1.0, r=[decT], w=[decT])
                ktok = bl[:, 2, :, :]; vtok = bl[:, 3, :, :]
                fw.tt('pool', vb16[:], vtok, b4.unsqueeze(2).to_broadcast([128, 4, 128]), ALU.mult, r=[bl, b], w=[vb16])
                fw.tt('pool', kbg16[:], ktok, sk[:].unsqueeze(2).to_broadcast([128, 4, 128]), ALU.mult, r=[bl, sk], w=[kbg16])
                fw.tt('pool', kd[:], ktok, e_[:, 4:8].unsqueeze(2).to_broadcast([128, 4, 128]), ALU.mult, r=[bl, e_], w=[kd])
                yield
                fw.tt('dve', nm32[:], ps3(ib), dec[:], ALU.mult, r=[pb_, dec], w=[nm32])
                fw.tt('pool', nm32[:], nm32[:], c32[:, C_NSTRF + d, :].unsqueeze(1).to_broadcast([128, 4, 128]), ALU.mult, r=[nm32, c32], w=[nm32])
                yield
                fw.tt('dve', P[:], nm32[:], b4.unsqueeze(2).to_broadcast([128, 4, 128]), ALU.mult, r=[nm32, b], w=[P])
                pqv = psbf(ia)
                for h in range(4):
                    fw.tr(pqv[:, h * 128:(h + 1) * 128], P[:, h, :], ident16, r=[P, c16], w=[pa])
                for h in range(4):
                    fw.mm(pb_[:, h * 128:(h + 1) * 128], bl[:, 1, h, :], bl[:, 0, h, :], r=[bl], w=[pb_])
                yield
                fw.cp('act', Q[:], pqv[:, 0:512].rearrange("p (h c) -> p h c", h=4), r=[pa], w=[Q])
                fw.tt('dve', nm32[:], ps3(ib), decT[:], ALU.mult, r=[pb_, decT], w=[nm32])
                fw.tt('dve', qd[:], eGB[:], bl[:, 0, :, :], ALU.mult, r=[eGB, bl], w=[qd])
                yield
                fw.tt('pool', qk[:], nm32[:], c32[:, C_INTF + d, :].unsqueeze(1).to_broadcast([128, 4, 128]), ALU.mult, r=[nm32, c32], w=[qk])
                for li in range(1, 6):
                    fw.tt('pool', QM[:, li - 1, :, :], Q[:], c16[:, C_MB + li, :].unsqueeze(1).to_broadcast([128, 4, 128]),
                          ALU.mult, r=[Q, c16], w=[QM])
                mb1 = c32[:, C_MB, :].unsqueeze(1).to_broadcast([128, 4, 128])
                idb = c32[:, C_ID, :].unsqueeze(1).to_broadcast([128, 4, 128])
                fw.tt('dve', eGB[:], P[:], mb1, ALU.mult, r=[P, c32], w=[eGB])
                fw.tt('pool', nm32[:], Q[:], mb1, ALU.mult, r=[Q, c32], w=[nm32])
                yield
                fw.tt('dve', T16[:], eGB[:], idb, ALU.add, r=[eGB, c32], w=[T16])
                fw.tt('pool', TT16[:], nm32[:], idb, ALU.add, r=[nm32, c32], w=[TT16])
                yield
                for li in range(1, 6):
                    for h in range(4):
                        fw.mm(pa[:, h * 128:(h + 1) * 128], QM[:, li - 1, h, :], T16[:, h, :], r=[QM, T16], w=[pa])
                    yield
                    fw.cp('act', X16[:], ps3(ia), r=[pa], w=[X16])
                    yield
                    if li < 5:
                        for h in range(4):
                            fw.mm(pb_[:, h * 128:(h + 1) * 128], TT16[:, h, :], X16[:, h, :], r=[TT16, X16], w=[pb_])
                    for h in range(4):
                        fw.mm(pa[:, h * 128:(h + 1) * 128], X16[:, h, :], TT16[:, h, :], r=[TT16, X16], w=[pa])
                    yield
                    if li < 5:
                        fw.tt('dve', T16[:], T16[:], ps3(ib), ALU.add, r=[T16, pb_], w=[T16])
                    fw.tt('dve', TT16[:], TT16[:], ps3(ia), ALU.add, r=[TT16, pa], w=[TT16])
                    yield
                for h in range(4):
                    fw.mm(pa[:, h * 128:(h + 1) * 128], kbg16[:, h, :], TT16[:, h, :], r=[kbg16, TT16], w=[pa])
                for h in range(4):
                    fw.mm(pb_[:, h * 128:(h + 1) * 128], TT16[:, h, :], vb16[:, h, :], r=[TT16, vb16], w=[pb_])
                yield
                fw.cp('act', wT[:], ps3(ia), r=[pa], w=[wT])
                fw.cp('act', u[:], ps3(ib), r=[pb_], w=[u])
                yield
                while turn[0] != my_idx:
                    yield
                Sf = S32c[ci]; Sb = S16c[ci]
                p6 = pa; p7 = pb_
                chs = (0, 1) if d == 0 else (1, 0)
                for ch in chs:
                    R = slice(ch * 64, (ch + 1) * 64)
                    for h in range(4):
                        fw.mm(p6[:, h * 128:(h + 1) * 128], wT[:, h, :], Sb[:, h, :], r=[wT, Sb], w=[p6])
                    yield
                    fw.tt('dve', vn[R, :, :], u[R, :, :], ps3(ia)[R, :, :], ALU.subtract, r=[u, p6], w=[vn])
                    yield
                    for h in range(4):
                        fw.mm(p7[:, h * 64:(h + 1) * 64], Sb[:, h, :], qd[:, h, R], start=True, stop=False, r=[Sb, qd], w=[p7])
                        fw.mm(p7[:, h * 64:(h + 1) * 64], vn[R, h, :], qk[R, h, R], start=False, stop=True, r=[vn, qk], w=[p7])
                    for h in range(4):
                        fw.mm(p6[:, h * 128:(h + 1) * 128], kd[R, h, :], vn[R, h, :], r=[kd, vn], w=[p6])
                    yield
                    fw.cp('act', oT[:, :, R], p7.t[:, 0:256].rearrange("p (h c) -> p h c", h=4), r=[p7], w=[oT])
                    for h in range(4):
                        fw.stt('dve', Sf[:, h, :], Sf[:, h, :], e_[:, 8 + ch * 4 + h:9 + ch * 4 + h], ps3(ia)[:, h, :],
                               ALU.mult, ALU.add, r=[Sf, e_, p6], w=[Sf])
                    yield
                    fw.cp('act', Sb[:], Sf[:], r=[Sf], w=[Sb])
                    yield
                turn[0] += 1
                fw.dma('act', (ofs if d == 0 else ofs2).ap()[t].rearrange("p (h c) -> p h c", h=4), oT[:], r=[oT],
                       w=[DB('ofs%d' % d, t)])
                if is_last and cidx == 0:
                    fw.dma('act', ns.ap()[seq_i, l, d].rearrange("h k v -> k h v"), Sf[:], r=[Sf])

            def run_chains(chain_defs):
                state = []
                for (ci, seq_i, sstart, slen, cidx, d) in chain_defs:
                    Sf = S32c[ci]; Sb = S16c[ci]
                    if cidx == 0:
                        fw.memset('pool', Sf[:], 0.0, w=[Sf])
                    else:
                        fw.dma('sp', Sf[:], s0.ap()[l, d].rearrange("h k v -> k h v"), w=[Sf])
                    fw.cp('act', Sb[:], Sf[:], r=[Sf], w=[Sb])
                    tiles = list(range(sstart // 128, (sstart + slen) // 128))
                    if d == 1:
                        tiles = tiles[::-1]
                    state.append(dict(ci=ci, seq_i=seq_i, cidx=cidx, d=d, tiles=tiles, nxt=0, turn=[0], inflight=0))
                active = [None] * NJ
                delay = [sl * 11 for sl in range(NJ)]
                while True:
                    progressed = False
                    for slot in range(NJ):
                        if delay[slot] > 0:
                            delay[slot] -= 1
                            progressed = True
                            continue
                        if active[slot] is None:
                            cands = [c for c in state if c['nxt'] < len(c['tiles']) and c['inflight'] < 2]
                            if cands:
                                c = min(cands, key=lambda c: (c['inflight'], c['nxt']))
                                k = c['nxt']; c['nxt'] += 1; c['inflight'] += 1
                                g = job(slot, c['ci'], c['tiles'][k], c['d'], c['cidx'], k == len(c['tiles']) - 1, c['seq_i'], c['turn'], k)
                                active[slot] = (g, c)
                        if active[slot] is not None:
                            progressed = True
                            g, c = active[slot]
                            try:
                                next(g)
                            except StopIteration:
                                c['inflight'] -= 1
                                active[slot] = None
                    if not progressed and all(c['nxt'] >= len(c['tiles']) for c in state):
                        break

            pr = []
            for seq_i, (sstart, slen, cidx) in enumerate(SEQS[:4]):
                for d in range(2):
                    pr.append((seq_i * 2 + d, seq_i, sstart, slen, cidx, d))
            run_chains(pr)
            sstart, slen, cidx = SEQS[4]
            run_chains([(0, 4, sstart, slen, cidx, 0), (1, 4, sstart, slen, cidx, 1)])
            fw.emit()

        if stop == 'b12':
            break
        with contextlib.ExitStack() as st:
            wo = alloc(st, "wo", [128, 8, 1024], BF16)
            fw.dma('pool', wo[:], w_out.ap()[l].rearrange("(kc p) n -> p kc n", p=128), w=[wo])
            NE = 3
            def per3(name, shape, dt=F32):
                return [alloc(st, "%s%d" % (name, i), shape, dt) for i in range(NE)]
            ofl = per3("e_ofl", [128, 4, 128]); ofl2 = per3("e_ofl2", [128, 4, 128]); osq = per3("e_osq", [128, 4, 128], BF16)
            ors = per3("e_ors", [128, 4, 128]); gTl = per3("e_gT", [128, 4, 128]); catA = per3("e_catA", [128, 4, 128], BF16)
            catB = per3("e_catB", [128, 4, 128], BF16); xres = per3("e_x", [128, 8, 128])

            def b3job(t):
                def gen(sl):
                    cidx = 0 if t < 8 else 1
                    of_ = ofl[sl]; of2 = ofl2[sl]; gt = gTl[sl]; cb = catB[sl]; xr = xres[sl]; ca = catA[sl]; oq = osq[sl]; orr = ors[sl]
                    ia = 2 * sl; ib = 2 * sl + 1
                    fw.dma('sp', of_[:], ofs.ap()[t].rearrange("p (h c) -> p h c", h=4), r=[DB('ofs0', t)], w=[of_])
                    fw.dma('sp', of2[:], ofs2.ap()[t].rearrange("p (h c) -> p h c", h=4), r=[DB('ofs1', t)], w=[of2])
                    fw.dma('sp', gt[:], gates_v[:, :, t * 128:(t + 1) * 128], r=[DB('gates', t)], w=[gt])
                    fw.dma('sp', cb[:], obs_v[:, :, t * 128:(t + 1) * 128], r=[DB('obs', t)], w=[cb])
                    fw.dma('sp', xr[:], xs_v[:, :, t * 128:(t + 1) * 128], r=[DB('xs', t)], w=[xr])
                    yield
                    fw.tt('pool', of_[:], of_[:], of2[:], ALU.add, r=[of_, of2], w=[of_])
                    yield
                    fw.act(oq[:], of_[:], AF.Square, r=[of_], w=[oq])
                    yield
                    fw.mm(ps[ia][:, :], ones16, oq[:], r=[c16, oq], w=[ps[ia]])
                    yield
                    fw.act(orr[:], ps3(ia), AF.Ln, bias=epsT[:, 0:1], scale=1.0 / 128, r=[ps[ia], epsT], w=[orr])
                    yield
                    fw.act(orr[:], orr[:], AF.Exp, scale=-0.5, r=[orr], w=[orr])
                    yield
                    fw.tt('dve', of_[:], of_[:], orr[:], ALU.mult, r=[of_, orr], w=[of_])
                    yield
                    fw.stt('dve', ca[:], of_[:], onT[:, l:l + 1], gt[:], ALU.mult, ALU.mult, r=[of_, onT, gt], w=[ca])
                    yield
                    for half in range(2):
                        pbn = ia if half == 0 else ib
                        pb2 = ps[pbn]
                        for jj in range(4):
                            dc = half * 4 + jj
                            for kc in range(8):
                                rhs = ca[:, kc, :] if kc < 4 else cb[:, kc - 4, :]
                                fw.mm(pb2[:, jj * 128:(jj + 1) * 128], wo[:, kc, dc * 128:(dc + 1) * 128], rhs,
                                      start=(kc == 0), stop=(kc == 7), r=[wo, ca, cb], w=[pb2])
                        yield
                    for half in range(2):
                        pbn = ia if half == 0 else ib
                        for jj in range(4):
                            dc = half * 4 + jj
                            fw.stt('dve', xr[:, dc, :], ps3(pbn)[:, jj, :], modT[:, 16 + dc, cidx:cidx + 1], xr[:, dc, :],
                                   ALU.mult, ALU.add, r=[ps[pbn], modT, xr], w=[xr])
                        yield
                    fw.dma('act', xs_v[:, :, t * 128:(t + 1) * 128], xr[:], r=[xr], w=[DB('xs', t)])
                return gen
            run_jobs([b3job(t) for t in range(NTL)], NE, stagger=4)
            fw.emit()

        if stop == 'b':
            break
        with contextlib.ExitStack() as st:
            x32 = alloc(st, "c_x", [128, 8, 1024])
            sqc = [alloc(st, "c_sq%d" % i, [128, 1024], BF16) for i in range(2)]
            rs = alloc(st, "c_rs", [128, 1024])
            tmp = [alloc(st, "c_tmp%d" % i, [128, 1024]) for i in range(2)]
            h16 = alloc(st, "c_h", [128, 8, 1024], BF16)
            a16 = alloc(st, "c_a", [128, 32, 1024], BF16)
            w1s = [alloc(st, "c_w1s%d" % i, [128, 2, 8, 128], BF16) for i in range(2)]
            w2s = [alloc(st, "c_w2s%d" % i, [128, 32, 128], BF16) for i in range(2)]
            rl = [alloc(st, "c_rl%d" % i, [128, 512]) for i in range(3)]
            ri = 0
            pi = 0
            for gi in range(NT // 1024):
                g0 = gi * 1024
                c = 0 if g0 < 1024 else 1
                tls = DBs('xs', gi * 8, gi * 8 + 8)
                fw.dma('sp', x32[:], xs_v[:, :, g0:g0 + 1024], r=tls, w=[x32])
                for kc in range(8):
                    s_ = sqc[kc % 2]
                    fw.act(s_[:], x32[:, kc, :], AF.Square, r=[x32], w=[s_])
                    for half in range(2):
                        fw.mm(ps[6 + half][:, :], ones16, s_[:, half * 512:(half + 1) * 512], start=(kc == 0), stop=(kc == 7),
                              r=[c16, s_], w=[ps[6 + half]])
                for half in range(2):
                    rstd(rs[:, half * 512:(half + 1) * 512], ps[6 + half][:, :], 1.0 / 1024, r=[ps[6 + half]], w=[rs])
                for kc in range(8):
                    tm = tmp[kc % 2]
                    fw.stt('dve', tm[:], x32[:, kc, :], w2T[:, kc, c:c + 1], rs[:], ALU.mult, ALU.mult, r=[x32, w2T, rs], w=[tm])
                    fw.act(h16[:, kc, :], tm[:], AF.Identity, bias=modT[:, 24 + kc, c:c + 1], r=[tm, modT], w=[h16])
                for j in range(32):
                    wv2 = w1s[(j // 2) % 2]
                    if j % 2 == 0:
                        fw.dma('sp', wv2[:].rearrange("p j k c -> p j (k c)"), w1c.ap()[j:j + 2].rearrange("j p n -> p j n"),
                               r=[DB('w1c', j // 4)], w=[wv2])
                    wv = T(wv2.t[:, j % 2, :, :]); wv.b = wv2.b
                    for half in range(2):
                        pb = ps[pi % 4]; pi += 1
                        for kc in range(8):
                            fw.mm(pb[:, :], wv[:, kc, :], h16[:, kc, half * 512:(half + 1) * 512], start=(kc == 0), stop=(kc == 7),
                                  r=[wv, h16], w=[pb])
                        r_ = rl[ri % 3]; ri += 1
                        fw.act(r_[:], pb[:, :], AF.Relu, r=[pb], w=[r_])
                        fw.tt('dve', a16[:, j, half * 512:(half + 1) * 512], r_[:], r_[:], ALU.mult, r=[r_], w=[a16])
                for dc in range(8):
                    wv = w2s[dc % 2]
                    fw.dma('sp', wv[:].rearrange("p j c -> p (j c)"), w2c.ap()[dc], r=[DB('w2c', dc)], w=[wv])
                    for half in range(2):
                        pb = ps[4 + (pi % 2)]; pi += 1
                        for j in range(32):
                            fw.mm(pb[:, :], wv[:, j, :], a16[:, j, half * 512:(half + 1) * 512], start=(j == 0), stop=(j == 31),
                                  r=[wv, a16], w=[pb])
                        xv = x32[:, dc, half * 512:(half + 1) * 512]
                        fw.stt('dve', xv, pb[:, :], modT[:, 40 + dc, c:c + 1], xv, ALU.mult, ALU.add, r=[pb, modT, x32], w=[x32])
                fw.dma('act', xs_v[:, :, g0:g0 + 1024], x32[:], r=[x32], w=tls)
            fw.emit()

    with contextlib.ExitStack() as st:
        xf = [alloc(st, "f_x%d" % i, [128, 8, 128]) for i in range(2)]
        sqf = alloc(st, "f_sq", [128, 8, 128], BF16)
        rsf = alloc(st, "f_rs", [128, 128])
        yo = [alloc(st, "f_y%d" % i, [128, 1024]) for i in range(2)]
        for t in range(0 if SKIPA else NTL):
            x = xf[t % 2]; yy = yo[t % 2]
            fw.dma('sp', x[:], xs_v[:, :, t * 128:(t + 1) * 128], r=[DB('xs', t)], w=[x])
            fw.act(sqf[:], x[:], AF.Square, r=[x], w=[sqf])
            pst = ps[4]
            for kc in range(8):
                fw.mm(pst[:, 0:128], ones16, sqf[:, kc, :], start=(kc == 0), stop=(kc == 7), r=[c16, sqf], w=[pst])
            rstd(rsf[:], pst[:, 0:128], 1.0 / 1024, r=[pst], w=[rsf])
            for kc in range(8):
                fw.stt('dve', x[:, kc, :], x[:, kc, :], nfT[:, kc:kc + 1], rsf[:], ALU.mult, ALU.mult, r=[x, nfT, rsf], w=[x])
            for half in range(2):
                pb = ps[(t % 2) * 2 + half]
                for j in range(4):
                    kc = half * 4 + j
                    fw.tr(pb[:, j * 128:(j + 1) * 128], x[:, kc, :], ident32, r=[x, c32], w=[pb])
                fw.cp('dve' if half == 0 else 'act', yy[:, half * 512:(half + 1) * 512], pb[:, :], r=[pb], w=[yy])
            fw.dma('act', y.ap()[t * 128:(t + 1) * 128, :], yy[:], r=[yy])
        fw.emit()
    stack0.close()
    return nc


def make_consts():
    idx = np.arange(128)
    same = (idx[:, None] // 64) == (idx[None, :] // 64)
    c = np.zeros((NCONST, 128, 128), np.float32)
    c[C_ID] = np.eye(128)
    c[C_ONES] = 1.0
    c[C_NEGONES] = -1.0
    c[C_UF] = same & (idx[:, None] <= idx[None, :])
    c[C_UB] = same & (idx[:, None] >= idx[None, :])
    c[C_SAME] = same
    c[C_CH0] = (idx[:, None] < 64) & np.ones((1, 128), bool)
    c[C_CH1] = (idx[:, None] >= 64) & np.ones((1, 128), bool)
    strict_f = same & (idx[None, :] < idx[:, None])
    strict_b = same & (idx[None, :] > idx[:, None])
    c[C_NSTRF] = -strict_f.astype(np.float32)
    c[C_NSTRB] = -strict_b.astype(np.float32)
    incl_f = same & (idx[None, :] <= idx[:, None])
    incl_b = same & (idx[None, :] >= idx[:, None])
    c[C_INTF] = incl_f.T.astype(np.float32) * (128.0 ** -0.5)
    c[C_INTB] = incl_b.T.astype(np.float32) * (128.0 ** -0.5)
    for i in range(6):
        b = 2 ** i
        c[C_MB + i] = ((idx[:, None] // (2 * b)) == (idx[None, :] // (2 * b))) & ((idx[:, None] // b) != (idx[None, :] // b))
    return c


_NC_CACHE = {}


def prep_inputs(inp, core):
    b = core // 4
    f = lambda a: np.ascontiguousarray(np.asarray(a, dtype=np.float32))
    xin = np.concatenate([f(inp['x_prompt'])[4 * core:4 * core + 4].reshape(1024, 1024), f(inp['x_sample'])[b]], axis=0)
    w1 = f(inp['w_ff1']).reshape(4, 8, 128, 32, 128).transpose(0, 3, 2, 1, 4).reshape(4, 32, 128, 1024)
    w2 = f(inp['w_ff2']).reshape(4, 32, 128, 8, 128).transpose(0, 3, 2, 1, 4).reshape(4, 8, 128, 4096)
    return {
        'xin': f(xin), 'cond': f(np.stack([f(inp['c_ctx']), f(inp['c'])[b]])), 's0': f(f(inp['state_delta'])[b]),
        'w_mod': f(inp['w_mod']), 'b_mod': f(inp['b_mod']), 'norm_mix': f(inp['norm_mix']), 'w_in': f(inp['w_in']),
        'conv_qkv': f(inp['conv_qkv']), 'a_log': f(inp['a_log']).reshape(4, 8), 'dt_bias': f(inp['dt_bias']).reshape(4, 8),
        'o_norm': f(inp['o_norm']), 'wsT': f(np.transpose(f(inp['w_spatial']), (0, 1, 3, 2))), 'b_sp': f(inp['b_spatial']),
        'w_out': f(inp['w_out']), 'norm_ffn': f(inp['norm_ffn']), 'w_ff1b': f(w1), 'w_ff2b': f(w2),
        'norm_final': f(inp['norm_final']), 'consts': make_consts(),
    }


def kernel(**inputs):
    if 'nc' not in _NC_CACHE:
        _NC_CACHE['nc'] = build()
    nc = _NC_CACHE['nc']
    shared = None
    in_maps = []
    for core in range(8):
        m = prep_inputs(inputs, core) if shared is None else None
        if shared is None:
            shared = m
        else:
            m = dict(shared)
            b = core // 4
            f = lambda a: np.ascontiguousarray(np.asarray(a, dtype=np.float32))
            m['xin'] = np.concatenate([f(inputs['x_prompt'])[4 * core:4 * core + 4].reshape(1024, 1024), f(inputs['x_sample'])[b]], axis=0)
            m['cond'] = f(np.stack([f(inputs['c_ctx']), f(inputs['c'])[b]]))
            m['s0'] = f(f(inputs['state_delta'])[b])
        in_maps.append(m)
    res = run_bass_kernel_spmd(nc, in_maps, core_ids=list(range(8)))
    outs = res.results
    y_prompt = np.concatenate([np.asarray(outs[c]['y'])[:1024].reshape(4, 256, 1024) for c in range(8)], axis=0)
    y_sample = np.stack([np.asarray(outs[0]['y'])[1024:], np.asarray(outs[4]['y'])[1024:]], axis=0)
    nsd = np.concatenate([np.asarray(outs[c]['ns']) for c in range(8)], axis=0)
    return (y_prompt.astype(np.float32), y_sample.astype(np.float32), nsd.astype(np.float32))
```

```python
import contextlib
import numpy as np
import concourse.bass as bass
import concourse.mybir as mybir
from concourse.bass_utils import run_bass_kernel_spmd

F32 = mybir.dt.float32
BF16 = mybir.dt.bfloat16
AF = mybir.ActivationFunctionType
ALU = mybir.AluOpType
AX = mybir.AxisListType

NT = 5120
NTL = NT // 128
EPS = 1e-6
LNSCALE = -0.5 * 4.852030263919617
SEQS = [(0, 256, 0), (256, 256, 0), (512, 256, 0), (768, 256, 0), (1024, 4096, 1)]
C_ID, C_ONES, C_UF, C_UB, C_SAME, C_CH0, C_CH1, C_NSTRF, C_NSTRB, C_INTF, C_INTB = range(11)
C_MB = 11
C_NEGONES = 17
NCONST = 18


class Buf:
    __slots__ = ('name', 'w', 'r', 'excl')

    def __init__(s, name=''):
        s.name = name
        s.w = None
        s.r = {}
        s.excl = False


class T:
    def __init__(s, t, name=''):
        s.t = t
        s.b = Buf(name)

    def __getitem__(s, k):
        return s.t[k]


def _b(x):
    return x.b if isinstance(x, T) else x


class FW:
    EPOCH = 8000
    NSW = 8
    NHW = 16
    MAXEP = 10
    SAME = ('act', 'dve', 'pool')

    def __init__(s, nc):
        s.nc = nc
        s.eng = {'pe': nc.tensor, 'act': nc.scalar, 'dve': nc.vector, 'pool': nc.gpsimd, 'sp': nc.sync}
        s.ops = {e: [] for e in s.eng}
        s.cnt = {e: 0 for e in s.eng}
        s.epoch = {e: 0 for e in s.eng}
        s.seen = {e: {} for e in s.eng}
        s.sems = {}
        for e in s.eng:
            for ep in range(s.MAXEP):
                k = (e, ep)
                s.sems[k] = nc.alloc_semaphore(name='s_%s_%d' % k)
        for cls, n in (('sw', s.NSW), ('hw', s.NHW)):
            for i in range(n):
                k = ('dma' + cls, i)
                s.sems[k] = nc.alloc_semaphore(name='s_%s_%d' % k)
        s.dma_i = {'sw': 0, 'hw': 0}
        s.dma_use = {'sw': [0] * s.NSW, 'hw': [0] * s.NHW}
        s.nops = 0

    def _deps(s, eng, reads, writes):
        need = {}

        def add(ev):
            if ev is None:
                return
            k, v = ev
            if need.get(k, 0) < v:
                need[k] = v
        for b in reads:
            b = _b(b)
            add(b.w)
            if b.excl:
                for k, v in b.r.items():
                    if k[0] != eng:
                        add((k, v))
        for b in writes:
            b = _b(b)
            add(b.w)
            for k, v in b.r.items():
                add((k, v))
        waits = []
        for k, v in need.items():
            if k[0] == eng and eng not in s.SAME:
                continue
            if s.seen[eng].get(k, 0) < v:
                s.seen[eng][k] = v
                waits.append((k, v))
        return waits

    def _commit(s, ev, reads, writes):
        k, v = ev
        for b in writes:
            b = _b(b)
            b.w = ev
            b.r = {}
        wset = set(id(_b(b)) for b in writes)
        for b in reads:
            b = _b(b)
            if id(b) in wset:
                continue
            if b.r.get(k, 0) < v:
                b.r[k] = v

    def op(s, eng, fn, reads=(), writes=()):
        waits = s._deps(eng, reads, writes)
        if s.cnt[eng] >= s.EPOCH:
            s.epoch[eng] += 1
            s.cnt[eng] = 0
        s.cnt[eng] += 1
        key = (eng, s.epoch[eng])
        val = s.cnt[eng]
        s.ops[eng].append((waits, fn, key, 1))
        s.nops += 1
        s._commit((key, val), reads, writes)

    def dma(s, q, out, in_, r=(), w=(), **kw):
        cls = 'sw' if q == 'pool' else 'hw'
        n = s.NSW if cls == 'sw' else s.NHW
        slot = s.dma_i[cls] % n
        s.dma_i[cls] += 1
        prev = s.dma_use[cls][slot]
        key = ('dma' + cls, slot)
        waits = s._deps(q, r, w)
        if prev > 0 and s.seen[q].get(key, 0) < 16 * prev:
            s.seen[q][key] = 16 * prev
            waits.append((key, 16 * prev))
        s.dma_use[cls][slot] = prev + 1
        s.ops[q].append((waits, lambda e: e.dma_start(out=out, in_=in_, **kw), key, 16))
        s.nops += 1
        s._commit((key, 16 * (prev + 1)), r, w)

    def barrier(s):
        evs = []
        for e in s.eng:
            for ep in range(s.epoch[e] + 1):
                c = s.cnt[e] if ep == s.epoch[e] else s.EPOCH
                if c > 0:
                    evs.append(((e, ep), c))
        for cls in ('sw', 'hw'):
            for slot in range(len(s.dma_use[cls])):
                if s.dma_use[cls][slot] > 0:
                    evs.append((('dma' + cls, slot), 16 * s.dma_use[cls][slot]))
        for e in s.eng:
            waits = []
            for k, v in evs:
                if s.seen[e].get(k, 0) < v:
                    s.seen[e][k] = v
                    waits.append((k, v))
            if waits:
                s.ops[e].append((waits, None, None, 0))

    def emit(s):
        s.barrier()
        nc = s.nc
        keys = set()
        for e in s.ops:
            for (w, f, k, i) in s.ops[e]:
                if k is not None:
                    keys.add(k)
                for (kk, v) in w:
                    keys.add(kk)
        for k in sorted(keys, key=str):
            if k not in s.sems:
                s.sems[k] = nc.alloc_semaphore(name='s_' + '_'.join(str(x) for x in k))
        ops = s.ops
        s.ops = {e: [] for e in s.eng}
        with nc.Block() as block:
            def mk(ename):
                def body(e):
                    for waits, fn, key, inc in ops[ename]:
                        for k, v in waits:
                            e.wait_ge(s.sems[k], v)
                        if fn is not None:
                            fn(e).then_inc(s.sems[key], inc)
                return body
            block.tensor(mk('pe'))
            block.scalar(mk('act'))
            block.vector(mk('dve'))
            block.gpsimd(mk('pool'))
            block.sync(mk('sp'))

    def mm(s, out, lhsT, rhs, start=True, stop=True, r=(), w=()):
        s.op('pe', lambda e: e.matmul(out, lhsT, rhs, start=start, stop=stop), r, w)

    def tr(s, out, in_, ident, r=(), w=()):
        s.op('pe', lambda e: e.transpose(out, in_, ident), r, w)

    def act(s, out, in_, func, bias=0.0, scale=1.0, r=(), w=()):
        s.op('act', lambda e: e.activation(out, in_, func, bias=bias, scale=scale), r, w)

    def tt(s, eng, out, in0, in1, op, r=(), w=()):
        s.op(eng, lambda e: e.tensor_tensor(out, in0, in1, op), r, w)

    def ts(s, eng, out, in0, s1, s2, op0, op1=None, r=(), w=()):
        if op1 is None:
            s.op(eng, lambda e: e.tensor_scalar(out, in0, s1, None, op0), r, w)
        else:
            s.op(eng, lambda e: e.tensor_scalar(out, in0, s1, s2, op0, op1), r, w)

    def stt(s, eng, out, in0, scalar, in1, op0, op1, r=(), w=()):
        s.op(eng, lambda e: e.scalar_tensor_tensor(out, in0, scalar, in1, op0, op1), r, w)

    def cp(s, eng, out, in_, r=(), w=()):
        if eng == 'act':
            s.op(eng, lambda e: e.copy(out, in_), r, w)
        else:
            s.op(eng, lambda e: e.tensor_copy(out, in_), r, w)

    def memset(s, eng, ap, val, w=()):
        s.op(eng, lambda e: e.memset(ap, val), (), w)

    def recip(s, out, in_, r=(), w=()):
        s.op('dve', lambda e: e.reciprocal(out, in_), r, w)


def build(L=4, dbg=False, stop=None):
    nc = bass.Bass("TRN2", target_bir_lowering=False)
    fw = FW(nc)

    def din(name, shape, dt=F32):
        return nc.dram_tensor(name, list(shape), dt, kind="ExternalInput")

    def dscr(name, shape, dt=F32):
        return nc.dram_tensor(name, list(shape), dt, kind=("ExternalOutput" if dbg else "Internal"))

    xin = din("xin", [NT, 1024]); cond = din("cond", [2, 1024]); s0 = din("s0", [4, 2, 4, 128, 128])
    w_mod = din("w_mod", [4, 1024, 6144]); b_mod = din("b_mod", [4, 6144]); norm_mix = din("norm_mix", [4, 1024])
    w_in = din("w_in", [4, 1024, 3088]); conv_qkv = din("conv_qkv", [4, 3, 1536]); a_log = din("a_log", [4, 8])
    dt_bias = din("dt_bias", [4, 8]); o_norm = din("o_norm", [4, 128]); wsT = din("wsT", [4, 4, 128, 128])
    b_sp = din("b_sp", [4, 4, 128]); w_out = din("w_out", [4, 1024, 1024]); norm_ffn = din("norm_ffn", [4, 1024])
    w_ff1b = din("w_ff1b", [4, 32, 128, 1024]); w_ff2b = din("w_ff2b", [4, 8, 128, 4096])
    norm_final = din("norm_final", [1024]); consts = din("consts", [NCONST, 128, 128])
    y = nc.dram_tensor("y", [NT, 1024], F32, kind="ExternalOutput")
    ns = nc.dram_tensor("ns", [4, 4, 2, 4, 128, 128], F32, kind="ExternalOutput")
    xs = dscr("xs", [1024, NT]); qkvs = dscr("qkvs", [1536, NT]); gates = dscr("gates", [512, NT])
    obs = dscr("obs", [512, NT], BF16); bgs = dscr("bgs", [NT, 16]); blobs = dscr("blobs", [NTL, 128, 2048], BF16)
    ofs = dscr("ofs", [NTL, 128, 512]); ofs2 = dscr("ofs2", [NTL, 128, 512])
    w1c = nc.dram_tensor("w1c", [32, 128, 1024], BF16, kind="Internal"); w2c = nc.dram_tensor("w2c", [8, 128, 4096], BF16, kind="Internal")

    dbufs = {}

    def DB(name, i):
        k = (name, i)
        if k not in dbufs:
            dbufs[k] = Buf(name + str(i))
        return dbufs[k]

    def DBs(name, t0, t1):
        return [DB(name, t) for t in range(max(t0, 0), min(t1, NTL))]

    xs_v = xs.ap().rearrange("(kc p) t -> p kc t", p=128)
    qkvs_v = qkvs.ap().rearrange("(n p) t -> p n t", p=128)
    gates_v = gates.ap().rearrange("(n p) t -> p n t", p=128)
    obs_v = obs.ap().rearrange("(n p) t -> p n t", p=128)

    stack0 = contextlib.ExitStack()

    uid = [0]

    def alloc(stack, name, shape, dt=F32):
        uid[0] += 1
        name = "%s_u%d" % (name, uid[0])
        return T(stack.enter_context(nc.sbuf_tensor(name, list(shape), dt)), name)

    ps = [T(stack0.enter_context(nc.psum_tensor("psb%d" % i, [128, 512], F32)), "ps%d" % i) for i in range(8)]
    for p_ in ps:
        p_.b.excl = True

    def ps3(i, h=4):
        return ps[i].t[:].rearrange("p (h c) -> p h c", h=h)

    def psbf(i):
        return ps[i].t[:].bitcast(BF16)

    c32 = alloc(stack0, "c32", [128, NCONST, 128], F32)
    c16 = alloc(stack0, "c16", [128, NCONST, 128], BF16)
    fw.dma('sp', c32[:], consts.ap().rearrange("n p c -> p n c"), w=[c32])
    fw.dma('pool', c16[:], consts.ap().rearrange("n p c -> p n c"), w=[c16])
    epsT = alloc(stack0, "epsT", [128, 1], F32)
    fw.memset('pool', epsT[:], EPS, w=[epsT])
    condT = alloc(stack0, "condT", [128, 8, 2], F32)
    scT = alloc(stack0, "scT", [128, 8, 2], BF16)
    for c in range(2):
        fw.dma('sp', condT[:, :, c], cond.ap()[c].rearrange("(kc p) -> p kc", p=128), w=[condT],
               allow_slow_non_contiguous=True)
    fw.act(scT[:], condT[:], AF.Silu, r=[condT], w=[scT])
    nfT = alloc(stack0, "nfT", [128, 8], F32)
    fw.dma('sp', nfT[:], norm_final.ap().rearrange("(kc p) -> p kc", p=128), w=[nfT], allow_slow_non_contiguous=True)
    nmT = alloc(stack0, "nmT", [128, 4, 8], F32); nffT = alloc(stack0, "nffT", [128, 4, 8], F32)
    bmT = alloc(stack0, "bmT", [128, 4, 48], F32); cwT = alloc(stack0, "cwT", [128, 4, 12, 3], F32)
    onT = alloc(stack0, "onT", [128, 4], F32)
    dtb = alloc(stack0, "dtb", [128, 4, 8], F32); nA = alloc(stack0, "nA", [128, 4, 8], F32)
    ws16 = alloc(stack0, "ws16", [128, 4, 4, 128], BF16); bs16 = alloc(stack0, "bs16", [1, 4, 4, 128], BF16)
    for l in range(L):
        fw.dma('sp', nmT[:, l, :], norm_mix.ap()[l].rearrange("(kc p) -> p kc", p=128), w=[nmT], allow_slow_non_contiguous=True)
        fw.dma('sp', nffT[:, l, :], norm_ffn.ap()[l].rearrange("(kc p) -> p kc", p=128), w=[nffT], allow_slow_non_contiguous=True)
        fw.dma('sp', bmT[:, l, :], b_mod.ap()[l].rearrange("(n p) -> p n", p=128), w=[bmT], allow_slow_non_contiguous=True)
        for wi in range(3):
            fw.dma('sp', cwT[:, l, :, wi], conv_qkv.ap()[l, wi].rearrange("(n p) -> p n", p=128), w=[cwT], allow_slow_non_contiguous=True)
        fw.dma('sp', onT[:, l:l + 1], o_norm.ap()[l].rearrange("(p o) -> p o", o=1), w=[onT], allow_slow_non_contiguous=True)
        fw.dma('sp', dtb[:, l, :], dt_bias.ap()[l].partition_broadcast(128), w=[dtb])
        fw.dma('sp', nA[:, l, :], a_log.ap()[l].partition_broadcast(128), w=[nA])
        fw.dma('pool', ws16[:, l, :, :], wsT.ap()[l].rearrange("g s t -> s g t"), w=[ws16])
        fw.dma('pool', bs16[:, l, :, :], b_sp.ap()[l:l + 1], w=[bs16])
    fw.act(nA[:, 0:L, :], nA[:, 0:L, :], AF.Exp, r=[nA], w=[nA])
    fw.ts('dve', nA[:, 0:L, :], nA[:, 0:L, :], -1.0, None, ALU.mult, r=[nA], w=[nA])

    ident32 = c32[:, C_ID, :]
    ident16 = c16[:, C_ID, :]
    ones16 = c16[:, C_ONES, :]
    ones32 = c32[:, C_ONES, :]

    def rstd(out, in_, scale, r, w):
        fw.act(out, in_, AF.Ln, bias=epsT[:, 0:1], scale=scale, r=list(r) + [epsT], w=w)
        fw.act(out, out, AF.Exp, scale=-0.5, r=w, w=w)

    import os
    SKIPA = os.environ.get('SKIPA') == '1'
    def run_jobs(factories, NJ, stagger=0):
        pending = list(factories)
        active = [None] * NJ
        delay = [sl * stagger for sl in range(NJ)]
        while pending or any(a is not None for a in active):
            for sl in range(NJ):
                if delay[sl] > 0:
                    delay[sl] -= 1
                    continue
                if active[sl] is None and pending:
                    active[sl] = pending.pop(0)(sl)
                if active[sl] is not None:
                    try:
                        next(active[sl])
                    except StopIteration:
                        active[sl] = None

    with contextlib.ExitStack() as st:
        xt = [alloc(st, "x0t%d" % i, [128, 1024]) for i in range(2)]
        xo = [alloc(st, "x0o%d" % i, [128, 8, 128]) for i in range(2)]
        for t in range(0 if SKIPA else NTL):
            a = xt[t % 2]; o = xo[t % 2]
            fw.dma('sp', a[:], xin.ap()[t * 128:(t + 1) * 128, :], w=[a])
            for half in range(2):
                pb = ps[(t % 2) * 2 + half]
                for j in range(4):
                    kc = half * 4 + j
                    fw.tr(pb[:, j * 128:(j + 1) * 128], a[:, kc * 128:(kc + 1) * 128], ident32, r=[a, c32], w=[pb])
                fw.cp('dve' if half == 0 else 'act', o[:, half * 4:(half + 1) * 4, :], ps3((t % 2) * 2 + half), r=[pb], w=[o])
            fw.dma('act', xs_v[:, :, t * 128:(t + 1) * 128], o[:], r=[o], w=[DB('xs', t)])
        fw.emit()

    modT = alloc(stack0, "modT", [128, 48, 2], F32)
    w1T = alloc(stack0, "w1T", [128, 8, 2], F32)
    w2T = alloc(stack0, "w2T", [128, 8, 2], F32)

    for l in range(L):
        if stop == 'x0':
            break
        with contextlib.ExitStack() as st:
            wm = [alloc(st, "wm%d" % i, [128, 8, 512], BF16) for i in range(2)]
            pm = ps[0]
            for nb in range(0 if SKIPA else 12):
                wb = wm[nb % 2]
                fw.dma('pool', wb[:], w_mod.ap()[l].rearrange("(kc p) n -> p kc n", p=128)[:, :, nb * 512:(nb + 1) * 512], w=[wb])
                for cc in range(4):
                    n = nb * 4 + cc
                    for kc in range(8):
                        fw.mm(pm[:, n * 2:n * 2 + 2], wb[:, kc, cc * 128:(cc + 1) * 128], scT[:, kc, :],
                              start=(kc == 0), stop=(kc == 7), r=[wb, scT], w=[pm])
            pmv = pm.t[:, 0:96].rearrange("p (n c) -> p n c", c=2)
            for c in range(0 if SKIPA else 2):
                fw.tt('dve', modT[:, :, c], pmv[:, :, c], bmT[:, l, :], ALU.add, r=[pm, bmT], w=[modT])
                fw.stt('dve', w1T[:, :, c], modT[:, 8:16, c], 1.0, nmT[:, l, :], ALU.add, ALU.mult, r=[modT, nmT], w=[w1T])
                fw.stt('dve', w2T[:, :, c], modT[:, 32:40, c], 1.0, nffT[:, l, :], ALU.add, ALU.mult, r=[modT, nffT], w=[w2T])
            fw.emit()

        if stop == 'm':
            break
        with contextlib.ExitStack() as st:
            win = alloc(st, "win", [128, 8, 3088], BF16)
            w_in_v = w_in.ap()[l].rearrange("(kc p) n -> p kc n", p=128)
            for kc in range(8):
                fw.dma('pool', win[:, kc, :], w_in_v[:, kc, :], w=[win])
            NA = 2
            def perA(name, shape, dt=F32, k=1):
                return [[alloc(st, "%s%d_%d" % (name, i, j), shape, dt) for j in range(k)] for i in range(NA)]
            xT = perA("a_x", [128, 8, 512]); sqA = perA("a_sq", [128, 8, 512], BF16); rsA = perA("a_rs", [128, 512])
            tmpA = perA("a_tmp", [128, 512], F32, 2); hA = perA("a_h", [128, 8, 512], BF16); stgA = perA("a_stg", [128, 512], F32, 4)
            uA = perA("a_u", [128, 4, 512]); bgA = perA("a_bg", [128, 16], F32, 2); vgfA = perA("a_vgf", [128, 4, 128], F32, 2)
            vgnA = perA("a_vgn", [128, 4, 128], BF16, 2); st4A = perA("a_st4", [128, 4], F32, 2); obA = perA("a_ob", [128, 4, 128], BF16, 2)
            COLS = [n * 128 for n in range(12)] + [1536 + n * 128 for n in range(4)] + [2064 + n * 128 for n in range(4)]

            def ajob(gi):
                def gen(sl):
                    base = 4 * sl
                    g0 = gi * 512
                    c = 0 if g0 < 1024 else 1
                    x = xT[sl][0]; sq = sqA[sl][0]; rs = rsA[sl][0]; h16 = hA[sl][0]; uT = uA[sl][0]
                    tls = DBs('xs', gi * 4, gi * 4 + 4)
                    fw.dma('sp', x[:], xs_v[:, :, g0:g0 + 512], r=tls, w=[x])
                    yield
                    fw.act(sq[:], x[:], AF.Square, r=[x], w=[sq])
                    yield
                    pst = ps[base + 2]
                    for kc in range(8):
                        fw.mm(pst[:, :], ones16, sq[:, kc, :], start=(kc == 0), stop=(kc == 7), r=[c16, sq], w=[pst])
                    yield
                    fw.act(rs[:], pst[:, :], AF.Ln, bias=epsT[:, 0:1], scale=1.0 / 1024, r=[pst, epsT], w=[rs])
                    yield
                    fw.act(rs[:], rs[:], AF.Exp, scale=-0.5, r=[rs], w=[rs])
                    yield
                    for kc in range(8):
                        tm = tmpA[sl][kc % 2]
                        fw.stt('dve', tm[:], x[:, kc, :], w1T[:, kc, c:c + 1], rs[:], ALU.mult, ALU.mult, r=[x, w1T, rs], w=[tm])
                        fw.act(h16[:, kc, :], tm[:], AF.Identity, bias=modT[:, kc, c:c + 1], r=[tm, modT], w=[h16])
                        if kc % 2 == 1:
                            yield
                    si = 0
                    for n in range(20):
                        pb = ps[base + (n % 2)]
                        col = COLS[n]
                        for kc in range(8):
                            fw.mm(pb[:, :], win[:, kc, col:col + 128], h16[:, kc, :], start=(kc == 0), stop=(kc == 7),
                                  r=[win, h16], w=[pb])
                        if n < 12:
                            sg = stgA[sl][si % 4]; si += 1
                            fw.cp('dve' if n % 2 == 0 else 'act', sg[:], pb[:, :], r=[pb], w=[sg])
                            fw.dma('sp', qkvs_v[:, n, g0:g0 + 512], sg[:], r=[sg], w=DBs('qkvs', gi * 4, gi * 4 + 4))
                        elif n < 16:
                            sg = stgA[sl][si % 4]; si += 1
                            fw.act(sg[:], pb[:, :], AF.Silu, r=[pb], w=[sg])
                            fw.dma('sp', gates_v[:, n - 12, g0:g0 + 512], sg[:], r=[sg], w=DBs('gates', gi * 4, gi * 4 + 4))
                        else:
                            fw.act(uT[:, n - 16, :], pb[:, :], AF.Gelu, r=[pb], w=[uT])
                        yield
                    for tt_ in range(4):
                        t = gi * 4 + tt_
                        tok = slice(tt_ * 128, (tt_ + 1) * 128)
                        pab = ps[base + 2]; pvg = ps[base + 3]
                        b = bgA[sl][tt_ % 2]; vf = vgfA[sl][tt_ % 2]; vn = vgnA[sl][tt_ % 2]; s4 = st4A[sl][tt_ % 2]; ob = obA[sl][tt_ % 2]
                        for kc in range(8):
                            fw.mm(pab[:, 0:16], h16[:, kc, tok], win[:, kc, 2048:2064], start=(kc == 0), stop=(kc == 7),
                                  r=[h16, win], w=[pab])
                        for kc in range(8):
                            fw.mm(pvg[:, :], h16[:, kc, tok], win[:, kc, 2576:3088], start=(kc == 0), stop=(kc == 7),
                                  r=[h16, win], w=[pvg])
                        yield
                        fw.act(b[:, 0:8], pab[:, 0:8], AF.Sigmoid, r=[pab], w=[b])
                        fw.act(vf[:], ps3(base + 3), AF.Gelu, r=[pvg], w=[vf])
                        yield
                        fw.tt('dve', b[:, 8:16], pab[:, 8:16], dtb[:, l, :], ALU.add, r=[pab, dtb], w=[b])
                        fw.op('dve', lambda e, s4=s4, vf=vf: e.reduce_sum(s4[:], vf[:], AX.X), [vf], [s4])
                        yield
                        fw.act(b[:, 8:16], b[:, 8:16], AF.Exp, r=[b], w=[b])
                        fw.ts('dve', s4[:], s4[:], -1.0 / 128, None, ALU.mult, r=[s4], w=[s4])
                        yield
                        fw.act(b[:, 8:16], b[:, 8:16], AF.Ln, bias=1.0, r=[b], w=[b])
                        fw.tt('dve', vf[:], vf[:], s4[:].unsqueeze(2).to_broadcast([128, 4, 128]), ALU.add, r=[vf, s4], w=[vf])
                        yield
                        fw.tt('dve', b[:, 8:16], b[:, 8:16], nA[:, l, :], ALU.mult, r=[b, nA], w=[b])
                        sg = stgA[sl][si % 4]; si += 1
                        sg3 = sg.t[:].rearrange("p (h c) -> p h c", h=4)
                        fw.tt('pool', sg3, vf[:], vf[:], ALU.mult, r=[vf], w=[sg])
                        yield
                        fw.dma('act', bgs.ap()[t * 128:(t + 1) * 128, :], b[:], r=[b], w=[DB('bgs', t)])
                        fw.op('dve', lambda e, s4=s4, sg3=sg3: e.reduce_sum(s4[:], sg3, AX.X), [sg], [s4])
                        yield
                        fw.act(s4[:], s4[:], AF.Ln, bias=epsT[:, 0:1], scale=1.0 / 128, r=[s4, epsT], w=[s4])
                        yield
                        fw.act(s4[:], s4[:], AF.Exp, scale=-0.5, r=[s4], w=[s4])
                        yield
                        fw.tt('dve', vn[:], vf[:], s4[:].unsqueeze(2).to_broadcast([128, 4, 128]), ALU.mult, r=[vf, s4], w=[vn])
                        yield
                        pmx = ps[base + 2]
                        for g in range(4):
                            fw.mm(pmx[:, g * 128:(g + 1) * 128], vn[:, g, :], ws16[:, l, g, :], start=True, stop=False, r=[vn, ws16], w=[pmx])
                            fw.mm(pmx[:, g * 128:(g + 1) * 128], c16[0:1, C_ONES, :], bs16[0:1, l, g, :], start=False, stop=True,
                                  r=[c16, bs16], w=[pmx])
                        yield
                        fw.tt('dve', ob[:], uT[:, :, tok], ps3(base + 2), ALU.mult, r=[uT, pmx], w=[ob])
                        yield
                        fw.dma('act', obs_v[:, :, t * 128:(t + 1) * 128], ob[:], r=[ob], w=[DB('obs', t)])
                return gen
            run_jobs([ajob(gi) for gi in range(0 if SKIPA else NT // 512)], NA, stagger=40)
            fw.emit()

        if stop == 'a':
            break
        import os
        with contextlib.ExitStack() as st:
            NB0 = 4
            def per0(name, shape, dt=F32):
                return [alloc(st, "%s%d" % (name, i), shape, dt) for i in range(NB0)]
            pre = per0("b_pre", [128, 12, 130]); cv = per0("b_cv", [128, 12, 128]); cv2 = per0("b_cv2", [128, 12, 128])
            cv3 = per0("b_cv3", [128, 12, 128]); sq16 = per0("b_sq16", [128, 8, 128], BF16); rn = per0("b_rn", [128, 8, 128])
            vT16 = per0("b_vT16", [128, 4, 128], BF16); blob0 = per0("b_blob", [128, 4, 4, 128], BF16)
            cw = cwT.t[:, l, :, :]

            def b0job(t, sstart, slen):
                def gen(sl):
                    p = pre[sl]; c_ = cv[sl]; c2 = cv2[sl]; c3 = cv3[sl]; bl = blob0[sl]; sq_ = sq16[sl]; rn_ = rn[sl]; vt_ = vT16[sl]
                    ia = 2 * sl; ib = 2 * sl + 1
                    t0 = t * 128
                    lo = max(t0 - 1, sstart); hi = min(t0 + 129, sstart + slen)
                    if lo > t0 - 1:
                        fw.memset('pool', p[:, :, 0:1], 0.0, w=[p])
                    if hi < t0 + 129:
                        fw.memset('pool', p[:, :, 129:130], 0.0, w=[p])
                    for n3 in range(3):
                        fw.dma('sp', p[:, n3 * 4:(n3 + 1) * 4, lo - (t0 - 1):hi - (t0 - 1)], qkvs_v[:, n3 * 4:(n3 + 1) * 4, lo:hi],
                               r=DBs('qkvs', t - 1, t + 2), w=[p])
                    yield
                    fw.tt('dve', c_[:], p[:, :, 0:128], cw[:, :, 0:1].to_broadcast([128, 12, 128]), ALU.mult, r=[p, cwT], w=[c_])
                    fw.tt('pool', c2[:], p[:, :, 1:129], cw[:, :, 1:2].to_broadcast([128, 12, 128]), ALU.mult, r=[p, cwT], w=[c2])
                    fw.tt('pool', c3[:], p[:, :, 2:130], cw[:, :, 2:3].to_broadcast([128, 12, 128]), ALU.mult, r=[p, cwT], w=[c3])
                    yield
                    fw.tt('dve', c_[:], c_[:], c2[:], ALU.add, r=[c_, c2], w=[c_])
                    yield
                    fw.tt('dve', c_[:], c_[:], c3[:], ALU.add, r=[c_, c3], w=[c_])
                    yield
                    fw.act(c_[:], c_[:], AF.Silu, r=[c_], w=[c_])
                    yield
                    fw.act(sq_[:], c_[:, 0:8, :], AF.Square, r=[c_], w=[sq_])
                    fw.cp('act', vt_[:], c_[:, 8:12, :], r=[c_], w=[vt_])
                    yield
                    fw.mm(ps[ia][:, :], ones16, sq_[:, 0:4, :], r=[c16, sq_], w=[ps[ia]])
                    fw.mm(ps[ib][:, :], ones16, sq_[:, 4:8, :], r=[c16, sq_], w=[ps[ib]])
                    yield
                    fw.act(rn_[:, 0:4, :], ps3(ia), AF.Ln, bias=epsT[:, 0:1], scale=1.0, r=[ps[ia], epsT], w=[rn_])
                    fw.act(rn_[:, 4:8, :], ps3(ib), AF.Ln, bias=epsT[:, 0:1], scale=1.0, r=[ps[ib], epsT], w=[rn_])
                    yield
                    fw.act(rn_[:], rn_[:], AF.Exp, scale=-0.5, r=[rn_], w=[rn_])
                    yield
                    fw.tt('dve', bl[:, 0:2, :, :], c_[:, 0:8, :].rearrange("p (a h) c -> p a h c", a=2),
                          rn_[:].rearrange("p (a h) c -> p a h c", a=2), ALU.mult, r=[c_, rn_], w=[bl])
                    yield
                    pv = psbf(ia)
                    for h in range(4):
                        fw.tr(pv[:, h * 128:(h + 1) * 128], bl[:, 1, h, :], ident16, r=[bl, c16], w=[ps[ia]])
                        fw.tr(pv[:, 512 + h * 128:512 + (h + 1) * 128], vt_[:, h, :], ident16, r=[vt_, c16], w=[ps[ia]])
                    yield
                    fw.cp('dve', bl[:, 2:4, :, :], pv.rearrange("p (a h c) -> p a h c", a=2, h=4), r=[ps[ia]], w=[bl])
                    yield
                    fw.dma('act', blobs.ap()[t], bl[:].rearrange("p a h c -> p (a h c)"), r=[bl], w=[DB('blobs', t)])
                return gen
            jobs0 = []
            for (sstart, slen, cidx) in SEQS:
                for t in range(sstart // 128, (sstart + slen) // 128):
                    jobs0.append(b0job(t, sstart, slen))
            run_jobs(jobs0, NB0, stagger=3)
            fw.emit()

        if stop == 'b0':
            break
        with contextlib.ExitStack() as st:
            NJ = 4
            def per(name, shape, dt=F32):
                return [alloc(st, "%s%d" % (name, i), shape, dt) for i in range(NJ)]
            blobj = per("j_blob", [128, 4, 4, 128], BF16)
            bgt = per("j_bg", [128, 16]); Gs = per("j_Gs", [128, 16]); eG = per("j_eG", [128, 16]); sck = per("j_sck", [128, 4])
            gpad = per("j_gpad", [128, 128])
            for i in range(NJ):
                fw.memset('pool', gpad[i][:], 0.0, w=[gpad[i]])
            for j0 in range(0, 32, 4):
                fw.dma('pool', w1c.ap()[j0:j0 + 4], w_ff1b.ap()[l, j0:j0 + 4], w=[DB('w1c', j0 // 4)])
            for dc in range(8):
                fw.dma('pool', w2c.ap()[dc], w_ff2b.ap()[l, dc], w=[DB('w2c', dc)])
            Ugj = per("j_Ug", [128, 4, 128]); decj = per("j_dec", [128, 4, 128]); decTj = per("j_decT", [128, 4, 128])
            eGBj = per("j_eGB", [128, 4, 128]); nm32j = per("j_nm32", [128, 4, 128])
            Pj = per("j_P", [128, 4, 128], BF16); Qj = per("j_Q", [128, 4, 128], BF16); QMj = per("j_QM", [128, 5, 4, 128], BF16)
            TT16j = per("j_TT16", [128, 4, 128], BF16)
            T16j = per("j_T16", [128, 4, 128], BF16); X16j = per("j_X16", [128, 4, 128], BF16)
            vbj = per("j_vb", [128, 4, 128], BF16); kbgj = per("j_kbg", [128, 4, 128], BF16)
            kdj = per("j_kd", [128, 4, 128], BF16); wTj = per("j_wT", [128, 4, 128], BF16)
            qkj = per("j_qk", [128, 4, 128], BF16); qdj = per("j_qd", [128, 4, 128], BF16)
            u32j = per("j_u", [128, 4, 128]); vnj = per("j_vn", [128, 4, 128], BF16); oTj = per("j_oT", [128, 4, 128])
            NCH = 8
            S32c = [alloc(st, "j_S32_%d" % i, [128, 4, 128]) for i in range(NCH)]
            S16c = [alloc(st, "j_S16_%d" % i, [128, 4, 128], BF16) for i in range(NCH)]
            negones32 = c32[:, C_NEGONES, :]

            def job(slot, ci, t, d, cidx, is_last, seq_i, turn, my_idx):
                bl = blobj[slot]; b = bgt[slot]; G = Gs[slot]; e_ = eG[slot]; sk = sck[slot]; gp = gpad[slot]
                Ug = Ugj[slot]; dec = decj[slot]; decT = decTj[slot]; eGB = eGBj[slot]; nm32 = nm32j[slot]
                P = Pj[slot]; Q = Qj[slot]; QM = QMj[slot]; TT16 = TT16j[slot]; T16 = T16j[slot]; X16 = X16j[slot]
                vb16 = vbj[slot]; kbg16 = kbgj[slot]; kd = kdj[slot]; wT = wTj[slot]; qk = qkj[slot]; qd = qdj[slot]
                u = u32j[slot]; vn = vnj[slot]; oT = oTj[slot]
                pa = ps[2 * slot]; pb_ = ps[2 * slot + 1]; ia = 2 * slot; ib = 2 * slot + 1
                fw.dma('sp', bl[:].rearrange("p a h c -> p (a h c)"), blobs.ap()[t], r=[DB('blobs', t)], w=[bl])
                fw.dma('sp', b[:], bgs.ap()[t * 128:(t + 1) * 128, :], r=[DB('bgs', t)], w=[b])
                g4 = b[:, 8 + d * 4:12 + d * 4]; b4 = b[:, d * 4:d * 4 + 4]
                U = c32[:, C_UF + d, :]
                fw.cp('dve', gp[:, 0:4], g4, r=[b], w=[gp])
                fw.mm(pa[:, 0:128], U, gp[:], r=[c32, gp], w=[pa])
                fw.mm(pa[:, 128:256], c32[:, C_SAME, :], gp[:], r=[c32, gp], w=[pa])
                fw.mm(pa[:, 256:384], c32[:, C_CH0, :], gp[:], r=[c32, gp], w=[pa])
                fw.mm(pa[:, 384:512], c32[:, C_CH1, :], gp[:], r=[c32, gp], w=[pa])
                for h in range(4):
                    fw.act(Ug[:, h, :], U, AF.Identity, scale=g4[:, h:h + 1], r=[c32, b], w=[Ug])
                yield
                fw.cp('dve', G[:].rearrange("p (a b) -> p a b", a=4), ps3(ia)[:, :, 0:4], r=[pa], w=[G])
                fw.tt('dve', G[:, 4:8], G[:, 4:8], G[:, 0:4], ALU.subtract, r=[G], w=[G])
                fw.act(e_[:], G[:], AF.Exp, r=[G], w=[e_])
                for h in range(4):
                    fw.mm(pb_[:, h * 128:(h + 1) * 128], Ug[:, h, :], ones32, start=True, stop=False, r=[Ug, c32], w=[pb_])
                    fw.mm(pb_[:, h * 128:(h + 1) * 128], negones32, Ug[:, h, :], start=False, stop=True, r=[Ug, c32], w=[pb_])
                for h in range(4):
                    fw.mm(pa[:, h * 128:(h + 1) * 128], ones32, Ug[:, h, :], r=[Ug, c32], w=[pa])
                yield
                fw.tt('dve', sk[:], b4, e_[:, 0:4], ALU.mult, r=[b, e_], w=[sk])
                fw.act(dec[:], ps3(ib), AF.Relu, scale=-1.0, r=[pb_], w=[dec])
                fw.act(decT[:], ps3(ib), AF.Relu, scale=1.0, r=[pb_], w=[decT])
                fw.act(eGB[:], ps3(ia), AF.Exp, bias=LNSCALE, r=[pa], w=[eGB])
                for h in range(4):
                    fw.mm(pb_[:, h * 128:(h + 1) * 128], bl[:, 1, h, :], bl[:, 1, h, :], r=[bl], w=[pb_])
                yield
                fw.act(dec[:], dec[:], AF.Exp, scale=-1.0, r=[dec], w=[dec])
                fw.act(decT[:], decT[:], AF.Exp, scale=-1.0, r=[decT], w=[decT])
                ktok = bl[:, 2, :, :]; vtok = bl[:, 3, :, :]
                fw.tt('pool', vb16[:], vtok, b4.unsqueeze(2).to_broadcast([128, 4, 128]), ALU.mult, r=[bl, b], w=[vb16])
                fw.tt('pool', kbg16[:], ktok, sk[:].unsqueeze(2).to_broadcast([128, 4, 128]), ALU.mult, r=[bl, sk], w=[kbg16])
                fw.tt('pool', kd[:], ktok, e_[:, 4:8].unsqueeze(2).to_broadcast([128, 4, 128]), ALU.mult, r=[bl, e_], w=[kd])
                yield
                fw.tt('dve', nm32[:], ps3(ib), dec[:], ALU.mult, r=[pb_, dec], w=[nm32])
                fw.tt('pool', nm32[:], nm32[:], c32[:, C_NSTRF + d, :].unsqueeze(1).to_broadcast([128, 4, 128]), ALU.mult, r=[nm32, c32], w=[nm32])
                yield
                fw.tt('dve', P[:], nm32[:], b4.unsqueeze(2).to_broadcast([128, 4, 128]), ALU.mult, r=[nm32, b], w=[P])
                pqv = psbf(ia)
                for h in range(4):
                    fw.tr(pqv[:, h * 128:(h + 1) * 128], P[:, h, :], ident16, r=[P, c16], w=[pa])
                for h in range(4):
                    fw.mm(pb_[:, h * 128:(h + 1) * 128], bl[:, 1, h, :], bl[:, 0, h, :], r=[bl], w=[pb_])
                yield
                fw.cp('act', Q[:], pqv[:, 0:512].rearrange("p (h c) -> p h c", h=4), r=[pa], w=[Q])
                fw.tt('dve', nm32[:], ps3(ib), decT[:], ALU.mult, r=[pb_, decT], w=[nm32])
                fw.tt('dve', qd[:], eGB[:], bl[:, 0, :, :], ALU.mult, r=[eGB, bl], w=[qd])
                yield
                fw.tt('pool', qk[:], nm32[:], c32[:, C_INTF + d, :].unsqueeze(1).to_broadcast([128, 4, 128]), ALU.mult, r=[nm32, c32], w=[qk])
                for li in range(1, 6):
                    fw.tt('pool', QM[:, li - 1, :, :], Q[:], c16[:, C_MB + li, :].unsqueeze(1).to_broadcast([128, 4, 128]),
                          ALU.mult, r=[Q, c16], w=[QM])
                mb1 = c32[:, C_MB, :].unsqueeze(1).to_broadcast([128, 4, 128])
                idb = c32[:, C_ID, :].unsqueeze(1).to_broadcast([128, 4, 128])
                fw.tt('dve', eGB[:], P[:], mb1, ALU.mult, r=[P, c32], w=[eGB])
                fw.tt('pool', nm32[:], Q[:], mb1, ALU.mult, r=[Q, c32], w=[nm32])
                yield
                fw.tt('dve', T16[:], eGB[:], idb, ALU.add, r=[eGB, c32], w=[T16])
                fw.tt('pool', TT16[:], nm32[:], idb, ALU.add, r=[nm32, c32], w=[TT16])
                yield
                for li in range(1, 6):
                    for h in range(4):
                        fw.mm(pa[:, h * 128:(h + 1) * 128], QM[:, li - 1, h, :], T16[:, h, :], r=[QM, T16], w=[pa])
                    yield
                    fw.cp('act', X16[:], ps3(ia), r=[pa], w=[X16])
                    yield
                    if li < 5:
                        for h in range(4):
                            fw.mm(pb_[:, h * 128:(h + 1) * 128], TT16[:, h, :], X16[:, h, :], r=[TT16, X16], w=[pb_])
                    for h in range(4):
                        fw.mm(pa[:, h * 128:(h + 1) * 128], X16[:, h, :], TT16[:, h, :], r=[TT16, X16], w=[pa])
                    yield
                    if li < 5:
                        fw.tt('dve', T16[:], T16[:], ps3(ib), ALU.add, r=[T16, pb_], w=[T16])
                    fw.tt('dve', TT16[:], TT16[:], ps3(ia), ALU.add, r=[TT16, pa], w=[TT16])
                    yield
                for h in range(4):
                    fw.mm(pa[:, h * 128:(h + 1) * 128], kbg16[:, h, :], TT16[:, h, :], r=[kbg16, TT16], w=[pa])
                for h in range(4):
                    fw.mm(pb_[:, h * 128:(h + 1) * 128], TT16[:, h, :], vb16[:, h, :], r=[TT16, vb16], w=[pb_])
                yield
                fw.cp('act', wT[:], ps3(ia), r=[pa], w=[wT])
                fw.cp('act', u[:], ps3(ib), r=[pb_], w=[u])
                yield
                while turn[0] != my_idx:
                    yield
                Sf = S32c[ci]; Sb = S16c[ci]
                p6 = pa; p7 = pb_
                chs = (0, 1) if d == 0 else (1, 0)
                for ch in chs:
                    R = slice(ch * 64, (ch + 1) * 64)
                    for h in range(4):
                        fw.mm(p6[:, h * 128:(h + 1) * 128], wT[:, h, :], Sb[:, h, :], r=[wT, Sb], w=[p6])
                    yield
                    fw.tt('dve', vn[R, :, :], u[R, :, :], ps3(ia)[R, :, :], ALU.subtract, r=[u, p6], w=[vn])
                    yield
                    for h in range(4):
                        fw.mm(p7[:, h * 64:(h + 1) * 64], Sb[:, h, :], qd[:, h, R], start=True, stop=False, r=[Sb, qd], w=[p7])
                        fw.mm(p7[:, h * 64:(h + 1) * 64], vn[R, h, :], qk[R, h, R], start=False, stop=True, r=[vn, qk], w=[p7])
                    for h in range(4):
                        fw.mm(p6[:, h * 128:(h + 1) * 128], kd[R, h, :], vn[R, h, :], r=[kd, vn], w=[p6])
                    yield
                    fw.cp('act', oT[:, :, R], p7.t[:, 0:256].rearrange("p (h c) -> p h c", h=4), r=[p7], w=[oT])
                    for h in range(4):
                        fw.stt('dve', Sf[:, h, :], Sf[:, h, :], e_[:, 8 + ch * 4 + h:9 + ch * 4 + h], ps3(ia)[:, h, :],
                               ALU.mult, ALU.add, r=[Sf, e_, p6], w=[Sf])
                    yield
                    fw.cp('act', Sb[:], Sf[:], r=[Sf], w=[Sb])
                    yield
                turn[0] += 1
                fw.dma('act', (ofs if d == 0 else ofs2).ap()[t].rearrange("p (h c) -> p h c", h=4), oT[:], r=[oT],
                       w=[DB('ofs%d' % d, t)])
                if is_last and cidx == 0:
                    fw.dma('act', ns.ap()[seq_i, l, d].rearrange("h k v -> k h v"), Sf[:], r=[Sf])

            def run_chains(chain_defs):
                state = []
                for (ci, seq_i, sstart, slen, cidx, d) in chain_defs:
                    Sf = S32c[ci]; Sb = S16c[ci]
                    if cidx == 0:
                        fw.memset('pool', Sf[:], 0.0, w=[Sf])
                    else:
                        fw.dma('sp', Sf[:], s0.ap()[l, d].rearrange("h k v -> k h v"), w=[Sf])
                    fw.cp('act', Sb[:], Sf[:], r=[Sf], w=[Sb])
                    tiles = list(range(sstart // 128, (sstart + slen) // 128))
                    if d == 1:
                        tiles = tiles[::-1]
                    state.append(dict(ci=ci, seq_i=seq_i, cidx=cidx, d=d, tiles=tiles, nxt=0, turn=[0], inflight=0))
                active = [None] * NJ
                delay = [sl * 11 for sl in range(NJ)]
                while True:
                    progressed = False
                    for slot in range(NJ):
                        if delay[slot] > 0:
                            delay[slot] -= 1
                            progressed = True
                            continue
                        if active[slot] is None:
                            cands = [c for c in state if c['nxt'] < len(c['tiles']) and c['inflight'] < 2]
                            if cands:
                                c = min(cands, key=lambda c: (c['inflight'], c['nxt']))
                                k = c['nxt']; c['nxt'] += 1; c['inflight'] += 1
                                g = job(slot, c['ci'], c['tiles'][k], c['d'], c['cidx'], k == len(c['tiles']) - 1, c['seq_i'], c['turn'], k)
                                active[slot] = (g, c)
                        if active[slot] is not None:
                            progressed = True
                            g, c = active[slot]
                            try:
                                next(g)
                            except StopIteration:
                                c['inflight'] -= 1
                                active[slot] = None
                    if not progressed and all(c['nxt'] >= len(c['tiles']) for c in state):
                        break

            pr = []
            for seq_i, (sstart, slen, cidx) in enumerate(SEQS[:4]):
                for d in range(2):
                    pr.append((seq_i * 2 + d, seq_i, sstart, slen, cidx, d))
            run_chains(pr)
            sstart, slen, cidx = SEQS[4]
            run_chains([(0, 4, sstart, slen, cidx, 0), (1, 4, sstart, slen, cidx, 1)])
            fw.emit()

        if stop == 'b12':
            break
        with contextlib.ExitStack() as st:
            wo = alloc(st, "wo", [128, 8, 1024], BF16)
            fw.dma('pool', wo[:], w_out.ap()[l].rearrange("(kc p) n -> p kc n", p=128), w=[wo])
            NE = 4
            def per3(name, shape, dt=F32):
                return [alloc(st, "%s%d" % (name, i), shape, dt) for i in range(NE)]
            ofl = per3("e_ofl", [128, 4, 128]); ofl2 = per3("e_ofl2", [128, 4, 128]); osq = per3("e_osq", [128, 4, 128], BF16)
            ors = per3("e_ors", [128, 4, 128]); gTl = per3("e_gT", [128, 4, 128]); catA = per3("e_catA", [128, 4, 128], BF16)
            catB = per3("e_catB", [128, 4, 128], BF16); xres = per3("e_x", [128, 8, 128])

            def b3job(t):
                def gen(sl):
                    cidx = 0 if t < 8 else 1
                    of_ = ofl[sl]; of2 = ofl2[sl]; gt = gTl[sl]; cb = catB[sl]; xr = xres[sl]; ca = catA[sl]; oq = osq[sl]; orr = ors[sl]
                    ia = 2 * sl; ib = 2 * sl + 1
                    fw.dma('sp', of_[:], ofs.ap()[t].rearrange("p (h c) -> p h c", h=4), r=[DB('ofs0', t)], w=[of_])
                    fw.dma('sp', of2[:], ofs2.ap()[t].rearrange("p (h c) -> p h c", h=4), r=[DB('ofs1', t)], w=[of2])
                    fw.dma('sp', gt[:], gates_v[:, :, t * 128:(t + 1) * 128], r=[DB('gates', t)], w=[gt])
                    fw.dma('sp', cb[:], obs_v[:, :, t * 128:(t + 1) * 128], r=[DB('obs', t)], w=[cb])
                    fw.dma('sp', xr[:], xs_v[:, :, t * 128:(t + 1) * 128], r=[DB('xs', t)], w=[xr])
                    yield
                    fw.tt('pool', of_[:], of_[:], of2[:], ALU.add, r=[of_, of2], w=[of_])
                    yield
                    fw.act(oq[:], of_[:], AF.Square, r=[of_], w=[oq])
                    yield
                    fw.mm(ps[ia][:, :], ones16, oq[:], r=[c16, oq], w=[ps[ia]])
                    yield
                    fw.act(orr[:], ps3(ia), AF.Ln, bias=epsT[:, 0:1], scale=1.0 / 128, r=[ps[ia], epsT], w=[orr])
                    yield
                    fw.act(orr[:], orr[:], AF.Exp, scale=-0.5, r=[orr], w=[orr])
                    yield
                    fw.tt('dve', of_[:], of_[:], orr[:], ALU.mult, r=[of_, orr], w=[of_])
                    yield
                    fw.stt('dve', ca[:], of_[:], onT[:, l:l + 1], gt[:], ALU.mult, ALU.mult, r=[of_, onT, gt], w=[ca])
                    yield
                    for half in range(2):
                        pbn = ia if half == 0 else ib
                        pb2 = ps[pbn]
                        for jj in range(4):
                            dc = half * 4 + jj
                            for kc in range(8):
                                rhs = ca[:, kc, :] if kc < 4 else cb[:, kc - 4, :]
                                fw.mm(pb2[:, jj * 128:(jj + 1) * 128], wo[:, kc, dc * 128:(dc + 1) * 128], rhs,
                                      start=(kc == 0), stop=(kc == 7), r=[wo, ca, cb], w=[pb2])
                        yield
                    for half in range(2):
                        pbn = ia if half == 0 else ib
                        for jj in range(4):
                            dc = half * 4 + jj
                            fw.stt('dve', xr[:, dc, :], ps3(pbn)[:, jj, :], modT[:, 16 + dc, cidx:cidx + 1], xr[:, dc, :],
                                   ALU.mult, ALU.add, r=[ps[pbn], modT, xr], w=[xr])
                        yield
                    fw.dma('act', xs_v[:, :, t * 128:(t + 1) * 128], xr[:], r=[xr], w=[DB('xs', t)])
                return gen
            run_jobs([b3job(t) for t in range(NTL)], NE, stagger=3)
            fw.emit()

        if stop == 'b':
            break
        with contextlib.ExitStack() as st:
            x32 = alloc(st, "c_x", [128, 8, 1024])
            sqc = [alloc(st, "c_sq%d" % i, [128, 1024], BF16) for i in range(2)]
            rs = alloc(st, "c_rs", [128, 1024])
            tmp = [alloc(st, "c_tmp%d" % i, [128, 1024]) for i in range(2)]
            h16 = alloc(st, "c_h", [128, 8, 1024], BF16)
            a16 = alloc(st, "c_a", [128, 32, 1024], BF16)
            w1s = [alloc(st, "c_w1s%d" % i, [128, 2, 8, 128], BF16) for i in range(2)]
            w2s = [alloc(st, "c_w2s%d" % i, [128, 32, 128], BF16) for i in range(2)]
            rl = [alloc(st, "c_rl%d" % i, [128, 512]) for i in range(3)]
            ri = 0
            pi = 0
            for gi in range(NT // 1024):
                g0 = gi * 1024
                c = 0 if g0 < 1024 else 1
                tls = DBs('xs', gi * 8, gi * 8 + 8)
                fw.dma('sp', x32[:], xs_v[:, :, g0:g0 + 1024], r=tls, w=[x32])
                for kc in range(8):
                    s_ = sqc[kc % 2]
                    fw.act(s_[:], x32[:, kc, :], AF.Square, r=[x32], w=[s_])
                    for half in range(2):
                        fw.mm(ps[6 + half][:, :], ones16, s_[:, half * 512:(half + 1) * 512], start=(kc == 0), stop=(kc == 7),
                              r=[c16, s_], w=[ps[6 + half]])
                for half in range(2):
                    rstd(rs[:, half * 512:(half + 1) * 512], ps[6 + half][:, :], 1.0 / 1024, r=[ps[6 + half]], w=[rs])
                for kc in range(8):
                    tm = tmp[kc % 2]
                    fw.stt('dve', tm[:], x32[:, kc, :], w2T[:, kc, c:c + 1], rs[:], ALU.mult, ALU.mult, r=[x32, w2T, rs], w=[tm])
                    fw.act(h16[:, kc, :], tm[:], AF.Identity, bias=modT[:, 24 + kc, c:c + 1], r=[tm, modT], w=[h16])
                for j in range(32):
                    wv2 = w1s[(j // 2) % 2]
                    if j % 2 == 0:
                        fw.dma('sp', wv2[:].rearrange("p j k c -> p j (k c)"), w1c.ap()[j:j + 2].rearrange("j p n -> p j n"),
                               r=[DB('w1c', j // 4)], w=[wv2])
                    wv = T(wv2.t[:, j % 2, :, :]); wv.b = wv2.b
                    for half in range(2):
                        pb = ps[pi % 4]; pi += 1
                        for kc in range(8):
                            fw.mm(pb[:, :], wv[:, kc, :], h16[:, kc, half * 512:(half + 1) * 512], start=(kc == 0), stop=(kc == 7),
                                  r=[wv, h16], w=[pb])
                        r_ = rl[ri % 3]; ri += 1
                        fw.act(r_[:], pb[:, :], AF.Relu, r=[pb], w=[r_])
                        fw.tt('dve', a16[:, j, half * 512:(half + 1) * 512], r_[:], r_[:], ALU.mult, r=[r_], w=[a16])
                for dc in range(8):
                    wv = w2s[dc % 2]
                    fw.dma('sp', wv[:].rearrange("p j c -> p (j c)"), w2c.ap()[dc], r=[DB('w2c', dc)], w=[wv])
                    for half in range(2):
                        pb = ps[4 + (pi % 2)]; pi += 1
                        for j in range(32):
                            fw.mm(pb[:, :], wv[:, j, :], a16[:, j, half * 512:(half + 1) * 512], start=(j == 0), stop=(j == 31),
                                  r=[wv, a16], w=[pb])
                        xv = x32[:, dc, half * 512:(half + 1) * 512]
                        fw.stt('dve', xv, pb[:, :], modT[:, 40 + dc, c:c + 1], xv, ALU.mult, ALU.add, r=[pb, modT, x32], w=[x32])
                fw.dma('act', xs_v[:, :, g0:g0 + 1024], x32[:], r=[x32], w=tls)
            fw.emit()

    with contextlib.ExitStack() as st:
        xf = [alloc(st, "f_x%d" % i, [128, 8, 128]) for i in range(2)]
        sqf = alloc(st, "f_sq", [128, 8, 128], BF16)
        rsf = alloc(st, "f_rs", [128, 128])
        yo = [alloc(st, "f_y%d" % i, [128, 1024]) for i in range(2)]
        for t in range(0 if SKIPA else NTL):
            x = xf[t % 2]; yy = yo[t % 2]
            fw.dma('sp', x[:], xs_v[:, :, t * 128:(t + 1) * 128], r=[DB('xs', t)], w=[x])
            fw.act(sqf[:], x[:], AF.Square, r=[x], w=[sqf])
            pst = ps[4]
            for kc in range(8):
                fw.mm(pst[:, 0:128], ones16, sqf[:, kc, :], start=(kc == 0), stop=(kc == 7), r=[c16, sqf], w=[pst])
            rstd(rsf[:], pst[:, 0:128], 1.0 / 1024, r=[pst], w=[rsf])
            for kc in range(8):
                fw.stt('dve', x[:, kc, :], x[:, kc, :], nfT[:, kc:kc + 1], rsf[:], ALU.mult, ALU.mult, r=[x, nfT, rsf], w=[x])
            for half in range(2):
                pb = ps[(t % 2) * 2 + half]
                for j in range(4):
                    kc = half * 4 + j
                    fw.tr(pb[:, j * 128:(j + 1) * 128], x[:, kc, :], ident32, r=[x, c32], w=[pb])
                fw.cp('dve' if half == 0 else 'act', yy[:, half * 512:(half + 1) * 512], pb[:, :], r=[pb], w=[yy])
            fw.dma('act', y.ap()[t * 128:(t + 1) * 128, :], yy[:], r=[yy])
        fw.emit()
    stack0.close()
    return nc


def make_consts():
    idx = np.arange(128)
    same = (idx[:, None] // 64) == (idx[None, :] // 64)
    c = np.zeros((NCONST, 128, 128), np.float32)
    c[C_ID] = np.eye(128)
    c[C_ONES] = 1.0
    c[C_NEGONES] = -1.0
    c[C_UF] = same & (idx[:, None] <= idx[None, :])
    c[C_UB] = same & (idx[:, None] >= idx[None, :])
    c[C_SAME] = same
    c[C_CH0] = (idx[:, None] < 64) & np.ones((1, 128), bool)
    c[C_CH1] = (idx[:, None] >= 64) & np.ones((1, 128), bool)
    strict_f = same & (idx[None, :] < idx[:, None])
    strict_b = same & (idx[None, :] > idx[:, None])
    c[C_NSTRF] = -strict_f.astype(np.float32)
    c[C_NSTRB] = -strict_b.astype(np.float32)
    incl_f = same & (idx[None, :] <= idx[:, None])
    incl_b = same & (idx[None, :] >= idx[:, None])
    c[C_INTF] = incl_f.T.astype(np.float32) * (128.0 ** -0.5)
    c[C_INTB] = incl_b.T.astype(np.float32) * (128.0 ** -0.5)
    for i in range(6):
        b = 2 ** i
        c[C_MB + i] = ((idx[:, None] // (2 * b)) == (idx[None, :] // (2 * b))) & ((idx[:, None] // b) != (idx[None, :] // b))
    return c


_NC_CACHE = {}


def prep_inputs(inp, core):
    b = core // 4
    f = lambda a: np.ascontiguousarray(np.asarray(a, dtype=np.float32))
    xin = np.concatenate([f(inp['x_prompt'])[4 * core:4 * core + 4].reshape(1024, 1024), f(inp['x_sample'])[b]], axis=0)
    w1 = f(inp['w_ff1']).reshape(4, 8, 128, 32, 128).transpose(0, 3, 2, 1, 4).reshape(4, 32, 128, 1024)
    w2 = f(inp['w_ff2']).reshape(4, 32, 128, 8, 128).transpose(0, 3, 2, 1, 4).reshape(4, 8, 128, 4096)
    return {
        'xin': f(xin), 'cond': f(np.stack([f(inp['c_ctx']), f(inp['c'])[b]])), 's0': f(f(inp['state_delta'])[b]),
        'w_mod': f(inp['w_mod']), 'b_mod': f(inp['b_mod']), 'norm_mix': f(inp['norm_mix']), 'w_in': f(inp['w_in']),
        'conv_qkv': f(inp['conv_qkv']), 'a_log': f(inp['a_log']).reshape(4, 8), 'dt_bias': f(inp['dt_bias']).reshape(4, 8),
        'o_norm': f(inp['o_norm']), 'wsT': f(np.transpose(f(inp['w_spatial']), (0, 1, 3, 2))), 'b_sp': f(inp['b_spatial']),
        'w_out': f(inp['w_out']), 'norm_ffn': f(inp['norm_ffn']), 'w_ff1b': f(w1), 'w_ff2b': f(w2),
        'norm_final': f(inp['norm_final']), 'consts': make_consts(),
    }


def kernel(**inputs):
    if 'nc' not in _NC_CACHE:
        _NC_CACHE['nc'] = build()
    nc = _NC_CACHE['nc']
    shared = None
    in_maps = []
    for core in range(8):
        m = prep_inputs(inputs, core) if shared is None else None
        if shared is None:
            shared = m
        else:
            m = dict(shared)
            b = core // 4
            f = lambda a: np.ascontiguousarray(np.asarray(a, dtype=np.float32))
            m['xin'] = np.concatenate([f(inputs['x_prompt'])[4 * core:4 * core + 4].reshape(1024, 1024), f(inputs['x_sample'])[b]], axis=0)
            m['cond'] = f(np.stack([f(inputs['c_ctx']), f(inputs['c'])[b]]))
            m['s0'] = f(f(inputs['state_delta'])[b])
        in_maps.append(m)
    res = run_bass_kernel_spmd(nc, in_maps, core_ids=list(range(8)))
    outs = res.results
    y_prompt = np.concatenate([np.asarray(outs[c]['y'])[:1024].reshape(4, 256, 1024) for c in range(8)], axis=0)
    y_sample = np.stack([np.asarray(outs[0]['y'])[1024:], np.asarray(outs[4]['y'])[1024:]], axis=0)
    nsd = np.concatenate([np.asarray(outs[c]['ns']) for c in range(8)], axis=0)
    return (y_prompt.astype(np.float32), y_sample.astype(np.float32), nsd.astype(np.float32))
```

```python
import contextlib
import numpy as np
import concourse.bass as bass
import concourse.mybir as mybir
from concourse.bass_utils import run_bass_kernel_spmd

F32 = mybir.dt.float32
BF16 = mybir.dt.bfloat16
AF = mybir.ActivationFunctionType
ALU = mybir.AluOpType
AX = mybir.AxisListType

NT = 5120
NTL = NT // 128
EPS = 1e-6
LNSCALE = -0.5 * 4.852030263919617
SEQS = [(0, 256, 0), (256, 256, 0), (512, 256, 0), (768, 256, 0), (1024, 4096, 1)]
C_ID, C_ONES, C_UF, C_UB, C_SAME, C_CH0, C_CH1, C_NSTRF, C_NSTRB, C_INTF, C_INTB = range(11)
C_MB = 11
C_NEGONES = 17
NCONST = 18


class Buf:
    __slots__ = ('name', 'w', 'r', 'excl')

    def __init__(s, name=''):
        s.name = name
        s.w = None
        s.r = {}
        s.excl = False


class T:
    def __init__(s, t, name=''):
        s.t = t
        s.b = Buf(name)

    def __getitem__(s, k):
        return s.t[k]


def _b(x):
    return x.b if isinstance(x, T) else x


class FW:
    EPOCH = 8000
    NSW = 8
    NHW = 16
    MAXEP = 10
    SAME = ('act', 'dve', 'pool')

    def __init__(s, nc):
        s.nc = nc
        s.eng = {'pe': nc.tensor, 'act': nc.scalar, 'dve': nc.vector, 'pool': nc.gpsimd, 'sp': nc.sync}
        s.ops = {e: [] for e in s.eng}
        s.cnt = {e: 0 for e in s.eng}
        s.epoch = {e: 0 for e in s.eng}
        s.seen = {e: {} for e in s.eng}
        s.sems = {}
        for e in s.eng:
            for ep in range(s.MAXEP):
                k = (e, ep)
                s.sems[k] = nc.alloc_semaphore(name='s_%s_%d' % k)
        for cls, n in (('sw', s.NSW), ('hw', s.NHW)):
            for i in range(n):
                k = ('dma' + cls, i)
                s.sems[k] = nc.alloc_semaphore(name='s_%s_%d' % k)
        s.dma_i = {'sw': 0, 'hw': 0}
        s.dma_use = {'sw': [0] * s.NSW, 'hw': [0] * s.NHW}
        s.nops = 0

    def _deps(s, eng, reads, writes):
        need = {}

        def add(ev):
            if ev is None:
                return
            k, v = ev
            if need.get(k, 0) < v:
                need[k] = v
        for b in reads:
            b = _b(b)
            add(b.w)
            if b.excl:
                for k, v in b.r.items():
                    if k[0] != eng:
                        add((k, v))
        for b in writes:
            b = _b(b)
            add(b.w)
            for k, v in b.r.items():
                add((k, v))
        waits = []
        for k, v in need.items():
            if k[0] == eng and eng not in s.SAME:
                continue
            if s.seen[eng].get(k, 0) < v:
                s.seen[eng][k] = v
                waits.append((k, v))
        return waits

    def _commit(s, ev, reads, writes):
        k, v = ev
        for b in writes:
            b = _b(b)
            b.w = ev
            b.r = {}
        wset = set(id(_b(b)) for b in writes)
        for b in reads:
            b = _b(b)
            if id(b) in wset:
                continue
            if b.r.get(k, 0) < v:
                b.r[k] = v

    def op(s, eng, fn, reads=(), writes=()):
        waits = s._deps(eng, reads, writes)
        if s.cnt[eng] >= s.EPOCH:
            s.epoch[eng] += 1
            s.cnt[eng] = 0
        s.cnt[eng] += 1
        key = (eng, s.epoch[eng])
        val = s.cnt[eng]
        s.ops[eng].append((waits, fn, key, 1))
        s.nops += 1
        s._commit((key, val), reads, writes)

    def dma(s, q, out, in_, r=(), w=(), **kw):
        cls = 'sw' if q == 'pool' else 'hw'
        n = s.NSW if cls == 'sw' else s.NHW
        slot = s.dma_i[cls] % n
        s.dma_i[cls] += 1
        prev = s.dma_use[cls][slot]
        key = ('dma' + cls, slot)
        waits = s._deps(q, r, w)
        if prev > 0 and s.seen[q].get(key, 0) < 16 * prev:
            s.seen[q][key] = 16 * prev
            waits.append((key, 16 * prev))
        s.dma_use[cls][slot] = prev + 1
        s.ops[q].append((waits, lambda e: e.dma_start(out=out, in_=in_, **kw), key, 16))
        s.nops += 1
        s._commit((key, 16 * (prev + 1)), r, w)

    def barrier(s):
        evs = []
        for e in s.eng:
            for ep in range(s.epoch[e] + 1):
                c = s.cnt[e] if ep == s.epoch[e] else s.EPOCH
                if c > 0:
                    evs.append(((e, ep), c))
        for cls in ('sw', 'hw'):
            for slot in range(len(s.dma_use[cls])):
                if s.dma_use[cls][slot] > 0:
                    evs.append((('dma' + cls, slot), 16 * s.dma_use[cls][slot]))
        for e in s.eng:
            waits = []
            for k, v in evs:
                if s.seen[e].get(k, 0) < v:
                    s.seen[e][k] = v
                    waits.append((k, v))
            if waits:
                s.ops[e].append((waits, None, None, 0))

    def emit(s):
        s.barrier()
        nc = s.nc
        keys = set()
        for e in s.ops:
            for (w, f, k, i) in s.ops[e]:
                if k is not None:
                    keys.add(k)
                for (kk, v) in w:
                    keys.add(kk)
        for k in sorted(keys, key=str):
            if k not in s.sems:
                s.sems[k] = nc.alloc_semaphore(name='s_' + '_'.join(str(x) for x in k))
        ops = s.ops
        s.ops = {e: [] for e in s.eng}
        with nc.Block() as block:
            def mk(ename):
                def body(e):
                    for waits, fn, key, inc in ops[ename]:
                        for k, v in waits:
                            e.wait_ge(s.sems[k], v)
                        if fn is not None:
                            fn(e).then_inc(s.sems[key], inc)
                return body
            block.tensor(mk('pe'))
            block.scalar(mk('act'))
            block.vector(mk('dve'))
            block.gpsimd(mk('pool'))
            block.sync(mk('sp'))

    def mm(s, out, lhsT, rhs, start=True, stop=True, r=(), w=()):
        s.op('pe', lambda e: e.matmul(out, lhsT, rhs, start=start, stop=stop), r, w)

    def tr(s, out, in_, ident, r=(), w=()):
        s.op('pe', lambda e: e.transpose(out, in_, ident), r, w)

    def act(s, out, in_, func, bias=0.0, scale=1.0, r=(), w=()):
        s.op('act', lambda e: e.activation(out, in_, func, bias=bias, scale=scale), r, w)

    def tt(s, eng, out, in0, in1, op, r=(), w=()):
        s.op(eng, lambda e: e.tensor_tensor(out, in0, in1, op), r, w)

    def ts(s, eng, out, in0, s1, s2, op0, op1=None, r=(), w=()):
        if op1 is None:
            s.op(eng, lambda e: e.tensor_scalar(out, in0, s1, None, op0), r, w)
        else:
            s.op(eng, lambda e: e.tensor_scalar(out, in0, s1, s2, op0, op1), r, w)

    def stt(s, eng, out, in0, scalar, in1, op0, op1, r=(), w=()):
        s.op(eng, lambda e: e.scalar_tensor_tensor(out, in0, scalar, in1, op0, op1), r, w)

    def cp(s, eng, out, in_, r=(), w=()):
        if eng == 'act':
            s.op(eng, lambda e: e.copy(out, in_), r, w)
        else:
            s.op(eng, lambda e: e.tensor_copy(out, in_), r, w)

    def memset(s, eng, ap, val, w=()):
        s.op(eng, lambda e: e.memset(ap, val), (), w)

    def recip(s, out, in_, r=(), w=()):
        s.op('dve', lambda e: e.reciprocal(out, in_), r, w)


def build(L=4, dbg=False, stop=None):
    nc = bass.Bass("TRN2", target_bir_lowering=False)
    fw = FW(nc)

    def din(name, shape, dt=F32):
        return nc.dram_tensor(name, list(shape), dt, kind="ExternalInput")

    def dscr(name, shape, dt=F32):
        return nc.dram_tensor(name, list(shape), dt, kind=("ExternalOutput" if dbg else "Internal"))

    xin = din("xin", [NT, 1024]); cond = din("cond", [2, 1024]); s0 = din("s0", [4, 2, 4, 128, 128])
    w_mod = din("w_mod", [4, 1024, 6144]); b_mod = din("b_mod", [4, 6144]); norm_mix = din("norm_mix", [4, 1024])
    w_in = din("w_in", [4, 1024, 3088]); conv_qkv = din("conv_qkv", [4, 3, 1536]); a_log = din("a_log", [4, 8])
    dt_bias = din("dt_bias", [4, 8]); o_norm = din("o_norm", [4, 128]); wsT = din("wsT", [4, 4, 128, 128])
    b_sp = din("b_sp", [4, 4, 128]); w_out = din("w_out", [4, 1024, 1024]); norm_ffn = din("norm_ffn", [4, 1024])
    w_ff1b = din("w_ff1b", [4, 32, 128, 1024]); w_ff2b = din("w_ff2b", [4, 8, 128, 4096])
    norm_final = din("norm_final", [1024]); consts = din("consts", [NCONST, 128, 128])
    y = nc.dram_tensor("y", [NT, 1024], F32, kind="ExternalOutput")
    ns = nc.dram_tensor("ns", [4, 4, 2, 4, 128, 128], F32, kind="ExternalOutput")
    xs = dscr("xs", [1024, NT]); qkvs = dscr("qkvs", [1536, NT]); gates = dscr("gates", [512, NT])
    obs = dscr("obs", [512, NT], BF16); bgs = dscr("bgs", [NT, 16]); blobs = dscr("blobs", [NTL, 128, 2048], BF16)
    ofs = dscr("ofs", [NTL, 128, 512]); ofs2 = dscr("ofs2", [NTL, 128, 512])
    w1c = nc.dram_tensor("w1c", [32, 128, 1024], BF16, kind="Internal"); w2c = nc.dram_tensor("w2c", [8, 128, 4096], BF16, kind="Internal")

    dbufs = {}

    def DB(name, i):
        k = (name, i)
        if k not in dbufs:
            dbufs[k] = Buf(name + str(i))
        return dbufs[k]

    def DBs(name, t0, t1):
        return [DB(name, t) for t in range(max(t0, 0), min(t1, NTL))]

    xs_v = xs.ap().rearrange("(kc p) t -> p kc t", p=128)
    qkvs_v = qkvs.ap().rearrange("(n p) t -> p n t", p=128)
    gates_v = gates.ap().rearrange("(n p) t -> p n t", p=128)
    obs_v = obs.ap().rearrange("(n p) t -> p n t", p=128)

    stack0 = contextlib.ExitStack()

    uid = [0]

    def alloc(stack, name, shape, dt=F32):
        uid[0] += 1
        name = "%s_u%d" % (name, uid[0])
        return T(stack.enter_context(nc.sbuf_tensor(name, list(shape), dt)), name)

    ps = [T(stack0.enter_context(nc.psum_tensor("psb%d" % i, [128, 512], F32)), "ps%d" % i) for i in range(8)]
    for p_ in ps:
        p_.b.excl = True

    def ps3(i, h=4):
        return ps[i].t[:].rearrange("p (h c) -> p h c", h=h)

    def psbf(i):
        return ps[i].t[:].bitcast(BF16)

    c32 = alloc(stack0, "c32", [128, NCONST, 128], F32)
    c16 = alloc(stack0, "c16", [128, NCONST, 128], BF16)
    fw.dma('sp', c32[:], consts.ap().rearrange("n p c -> p n c"), w=[c32])
    fw.dma('pool', c16[:], consts.ap().rearrange("n p c -> p n c"), w=[c16])
    epsT = alloc(stack0, "epsT", [128, 1], F32)
    fw.memset('pool', epsT[:], EPS, w=[epsT])
    condT = alloc(stack0, "condT", [128, 8, 2], F32)
    scT = alloc(stack0, "scT", [128, 8, 2], BF16)
    for c in range(2):
        fw.dma('sp', condT[:, :, c], cond.ap()[c].rearrange("(kc p) -> p kc", p=128), w=[condT],
               allow_slow_non_contiguous=True)
    fw.act(scT[:], condT[:], AF.Silu, r=[condT], w=[scT])
    nfT = alloc(stack0, "nfT", [128, 8], F32)
    fw.dma('sp', nfT[:], norm_final.ap().rearrange("(kc p) -> p kc", p=128), w=[nfT], allow_slow_non_contiguous=True)
    nmT = alloc(stack0, "nmT", [128, 4, 8], F32); nffT = alloc(stack0, "nffT", [128, 4, 8], F32)
    bmT = alloc(stack0, "bmT", [128, 4, 48], F32); cwT = alloc(stack0, "cwT", [128, 4, 12, 3], F32)
    onT = alloc(stack0, "onT", [128, 4], F32)
    dtb = alloc(stack0, "dtb", [128, 4, 8], F32); nA = alloc(stack0, "nA", [128, 4, 8], F32)
    ws16 = alloc(stack0, "ws16", [128, 4, 4, 128], BF16); bs16 = alloc(stack0, "bs16", [1, 4, 4, 128], BF16)
    for l in range(L):
        fw.dma('sp', nmT[:, l, :], norm_mix.ap()[l].rearrange("(kc p) -> p kc", p=128), w=[nmT], allow_slow_non_contiguous=True)
        fw.dma('sp', nffT[:, l, :], norm_ffn.ap()[l].rearrange("(kc p) -> p kc", p=128), w=[nffT], allow_slow_non_contiguous=True)
        fw.dma('sp', bmT[:, l, :], b_mod.ap()[l].rearrange("(n p) -> p n", p=128), w=[bmT], allow_slow_non_contiguous=True)
        for wi in range(3):
            fw.dma('sp', cwT[:, l, :, wi], conv_qkv.ap()[l, wi].rearrange("(n p) -> p n", p=128), w=[cwT], allow_slow_non_contiguous=True)
        fw.dma('sp', onT[:, l:l + 1], o_norm.ap()[l].rearrange("(p o) -> p o", o=1), w=[onT], allow_slow_non_contiguous=True)
        fw.dma('sp', dtb[:, l, :], dt_bias.ap()[l].partition_broadcast(128), w=[dtb])
        fw.dma('sp', nA[:, l, :], a_log.ap()[l].partition_broadcast(128), w=[nA])
        fw.dma('pool', ws16[:, l, :, :], wsT.ap()[l].rearrange("g s t -> s g t"), w=[ws16])
        fw.dma('pool', bs16[:, l, :, :], b_sp.ap()[l:l + 1], w=[bs16])
    fw.act(nA[:, 0:L, :], nA[:, 0:L, :], AF.Exp, r=[nA], w=[nA])
    fw.ts('dve', nA[:, 0:L, :], nA[:, 0:L, :], -1.0, None, ALU.mult, r=[nA], w=[nA])

    ident32 = c32[:, C_ID, :]
    ident16 = c16[:, C_ID, :]
    ones16 = c16[:, C_ONES, :]
    ones32 = c32[:, C_ONES, :]

    def rstd(out, in_, scale, r, w):
        fw.act(out, in_, AF.Ln, bias=epsT[:, 0:1], scale=scale, r=list(r) + [epsT], w=w)
        fw.act(out, out, AF.Exp, scale=-0.5, r=w, w=w)

    import os
    SKIPA = os.environ.get('SKIPA') == '1'
    def run_jobs(factories, NJ, stagger=0):
        pending = list(factories)
        active = [None] * NJ
        delay = [sl * stagger for sl in range(NJ)]
        while pending or any(a is not None for a in active):
            for sl in range(NJ):
                if delay[sl] > 0:
                    delay[sl] -= 1
                    continue
                if active[sl] is None and pending:
                    active[sl] = pending.pop(0)(sl)
                if active[sl] is not None:
                    try:
                        next(active[sl])
                    except StopIteration:
                        active[sl] = None

    with contextlib.ExitStack() as st:
        xt = [alloc(st, "x0t%d" % i, [128, 1024]) for i in range(2)]
        xo = [alloc(st, "x0o%d" % i, [128, 8, 128]) for i in range(2)]
        for t in range(0 if SKIPA else NTL):
            a = xt[t % 2]; o = xo[t % 2]
            fw.dma('sp', a[:], xin.ap()[t * 128:(t + 1) * 128, :], w=[a])
            for half in range(2):
                pb = ps[(t % 2) * 2 + half]
                for j in range(4):
                    kc = half * 4 + j
                    fw.tr(pb[:, j * 128:(j + 1) * 128], a[:, kc * 128:(kc + 1) * 128], ident32, r=[a, c32], w=[pb])
                fw.cp('dve' if half == 0 else 'act', o[:, half * 4:(half + 1) * 4, :], ps3((t % 2) * 2 + half), r=[pb], w=[o])
            fw.dma('act', xs_v[:, :, t * 128:(t + 1) * 128], o[:], r=[o], w=[DB('xs', t)])
        fw.emit()

    modT = alloc(stack0, "modT", [128, 48, 2], F32)
    w1T = alloc(stack0, "w1T", [128, 8, 2], F32)
    w2T = alloc(stack0, "w2T", [128, 8, 2], F32)

    for l in range(L):
        if stop == 'x0':
            break
        with contextlib.ExitStack() as st:
            wm = [alloc(st, "wm%d" % i, [128, 8, 512], BF16) for i in range(2)]
            pm = ps[0]
            for nb in range(0 if SKIPA else 12):
                wb = wm[nb % 2]
                fw.dma('pool', wb[:], w_mod.ap()[l].rearrange("(kc p) n -> p kc n", p=128)[:, :, nb * 512:(nb + 1) * 512], w=[wb])
                for cc in range(4):
                    n = nb * 4 + cc
                    for kc in range(8):
                        fw.mm(pm[:, n * 2:n * 2 + 2], wb[:, kc, cc * 128:(cc + 1) * 128], scT[:, kc, :],
                              start=(kc == 0), stop=(kc == 7), r=[wb, scT], w=[pm])
            pmv = pm.t[:, 0:96].rearrange("p (n c) -> p n c", c=2)
            for c in range(0 if SKIPA else 2):
                fw.tt('dve', modT[:, :, c], pmv[:, :, c], bmT[:, l, :], ALU.add, r=[pm, bmT], w=[modT])
                fw.stt('dve', w1T[:, :, c], modT[:, 8:16, c], 1.0, nmT[:, l, :], ALU.add, ALU.mult, r=[modT, nmT], w=[w1T])
                fw.stt('dve', w2T[:, :, c], modT[:, 32:40, c], 1.0, nffT[:, l, :], ALU.add, ALU.mult, r=[modT, nffT], w=[w2T])
            fw.emit()

        if stop == 'm':
            break
        with contextlib.ExitStack() as st:
            win = alloc(st, "win", [128, 8, 3088], BF16)
            w_in_v = w_in.ap()[l].rearrange("(kc p) n -> p kc n", p=128)
            for kc in range(8):
                fw.dma('pool', win[:, kc, :], w_in_v[:, kc, :], w=[win])
            NA = 2
            def perA(name, shape, dt=F32, k=1):
                return [[alloc(st, "%s%d_%d" % (name, i, j), shape, dt) for j in range(k)] for i in range(NA)]
            xT = perA("a_x", [128, 8, 512]); sqA = perA("a_sq", [128, 8, 512], BF16); rsA = perA("a_rs", [128, 512])
            tmpA = perA("a_tmp", [128, 512], F32, 2); hA = perA("a_h", [128, 8, 512], BF16); stgA = perA("a_stg", [128, 512], F32, 4)
            uA = perA("a_u", [128, 4, 512]); bgA = perA("a_bg", [128, 16], F32, 2); vgfA = perA("a_vgf", [128, 4, 128], F32, 2)
            vgnA = perA("a_vgn", [128, 4, 128], BF16, 2); st4A = perA("a_st4", [128, 4], F32, 2); obA = perA("a_ob", [128, 4, 128], BF16, 2)
            COLS = [n * 128 for n in range(12)] + [1536 + n * 128 for n in range(4)] + [2064 + n * 128 for n in range(4)]

            def ajob(gi):
                def gen(sl):
                    base = 4 * sl
                    g0 = gi * 512
                    c = 0 if g0 < 1024 else 1
                    x = xT[sl][0]; sq = sqA[sl][0]; rs = rsA[sl][0]; h16 = hA[sl][0]; uT = uA[sl][0]
                    tls = DBs('xs', gi * 4, gi * 4 + 4)
                    fw.dma('sp', x[:], xs_v[:, :, g0:g0 + 512], r=tls, w=[x])
                    yield
                    fw.act(sq[:], x[:], AF.Square, r=[x], w=[sq])
                    yield
                    pst = ps[base + 2]
                    for kc in range(8):
                        fw.mm(pst[:, :], ones16, sq[:, kc, :], start=(kc == 0), stop=(kc == 7), r=[c16, sq], w=[pst])
                    yield
                    fw.act(rs[:], pst[:, :], AF.Ln, bias=epsT[:, 0:1], scale=1.0 / 1024, r=[pst, epsT], w=[rs])
                    yield
                    fw.act(rs[:], rs[:], AF.Exp, scale=-0.5, r=[rs], w=[rs])
                    yield
                    for kc in range(8):
                        tm = tmpA[sl][kc % 2]
                        fw.stt('dve', tm[:], x[:, kc, :], w1T[:, kc, c:c + 1], rs[:], ALU.mult, ALU.mult, r=[x, w1T, rs], w=[tm])
                        fw.act(h16[:, kc, :], tm[:], AF.Identity, bias=modT[:, kc, c:c + 1], r=[tm, modT], w=[h16])
                        if kc % 2 == 1:
                            yield
                    si = 0
                    for n in range(20):
                        pb = ps[base + (n % 2)]
                        col = COLS[n]
                        for kc in range(8):
                            fw.mm(pb[:, :], win[:, kc, col:col + 128], h16[:, kc, :], start=(kc == 0), stop=(kc == 7),
                                  r=[win, h16], w=[pb])
                        if n < 12:
                            sg = stgA[sl][si % 4]; si += 1
                            fw.cp('dve' if n % 2 == 0 else 'act', sg[:], pb[:, :], r=[pb], w=[sg])
                            fw.dma('sp', qkvs_v[:, n, g0:g0 + 512], sg[:], r=[sg], w=DBs('qkvs', gi * 4, gi * 4 + 4))
                        elif n < 16:
                            sg = stgA[sl][si % 4]; si += 1
                            fw.act(sg[:], pb[:, :], AF.Silu, r=[pb], w=[sg])
                            fw.dma('sp', gates_v[:, n - 12, g0:g0 + 512], sg[:], r=[sg], w=DBs('gates', gi * 4, gi * 4 + 4))
                        else:
                            fw.act(uT[:, n - 16, :], pb[:, :], AF.Gelu, r=[pb], w=[uT])
                        yield
                    for tt_ in range(4):
                        t = gi * 4 + tt_
                        tok = slice(tt_ * 128, (tt_ + 1) * 128)
                        pab = ps[base + 2]; pvg = ps[base + 3]
                        b = bgA[sl][tt_ % 2]; vf = vgfA[sl][tt_ % 2]; vn = vgnA[sl][tt_ % 2]; s4 = st4A[sl][tt_ % 2]; ob = obA[sl][tt_ % 2]
                        for kc in range(8):
                            fw.mm(pab[:, 0:16], h16[:, kc, tok], win[:, kc, 2048:2064], start=(kc == 0), stop=(kc == 7),
                                  r=[h16, win], w=[pab])
                        for kc in range(8):
                            fw.mm(pvg[:, :], h16[:, kc, tok], win[:, kc, 2576:3088], start=(kc == 0), stop=(kc == 7),
                                  r=[h16, win], w=[pvg])
                        yield
                        fw.act(b[:, 0:8], pab[:, 0:8], AF.Sigmoid, r=[pab], w=[b])
                        fw.act(vf[:], ps3(base + 3), AF.Gelu, r=[pvg], w=[vf])
                        yield
                        fw.tt('dve', b[:, 8:16], pab[:, 8:16], dtb[:, l, :], ALU.add, r=[pab, dtb], w=[b])
                        fw.op('dve', lambda e, s4=s4, vf=vf: e.reduce_sum(s4[:], vf[:], AX.X), [vf], [s4])
                        yield
                        fw.act(b[:, 8:16], b[:, 8:16], AF.Exp, r=[b], w=[b])
                        fw.ts('dve', s4[:], s4[:], -1.0 / 128, None, ALU.mult, r=[s4], w=[s4])
                        yield
                        fw.act(b[:, 8:16], b[:, 8:16], AF.Ln, bias=1.0, r=[b], w=[b])
                        fw.tt('dve', vf[:], vf[:], s4[:].unsqueeze(2).to_broadcast([128, 4, 128]), ALU.add, r=[vf, s4], w=[vf])
                        yield
                        fw.tt('dve', b[:, 8:16], b[:, 8:16], nA[:, l, :], ALU.mult, r=[b, nA], w=[b])
                        sg = stgA[sl][si % 4]; si += 1
                        sg3 = sg.t[:].rearrange("p (h c) -> p h c", h=4)
                        fw.tt('pool', sg3, vf[:], vf[:], ALU.mult, r=[vf], w=[sg])
                        yield
                        fw.dma('act', bgs.ap()[t * 128:(t + 1) * 128, :], b[:], r=[b], w=[DB('bgs', t)])
                        fw.op('dve', lambda e, s4=s4, sg3=sg3: e.reduce_sum(s4[:], sg3, AX.X), [sg], [s4])
                        yield
                        fw.act(s4[:], s4[:], AF.Ln, bias=epsT[:, 0:1], scale=1.0 / 128, r=[s4, epsT], w=[s4])
                        yield
                        fw.act(s4[:], s4[:], AF.Exp, scale=-0.5, r=[s4], w=[s4])
                        yield
                        fw.tt('dve', vn[:], vf[:], s4[:].unsqueeze(2).to_broadcast([128, 4, 128]), ALU.mult, r=[vf, s4], w=[vn])
                        yield
                        pmx = ps[base + 2]
                        for g in range(4):
                            fw.mm(pmx[:, g * 128:(g + 1) * 128], vn[:, g, :], ws16[:, l, g, :], start=True, stop=False, r=[vn, ws16], w=[pmx])
                            fw.mm(pmx[:, g * 128:(g + 1) * 128], c16[0:1, C_ONES, :], bs16[0:1, l, g, :], start=False, stop=True,
                                  r=[c16, bs16], w=[pmx])
                        yield
                        fw.tt('dve', ob[:], uT[:, :, tok], ps3(base + 2), ALU.mult, r=[uT, pmx], w=[ob])
                        yield
                        fw.dma('act', obs_v[:, :, t * 128:(t + 1) * 128], ob[:], r=[ob], w=[DB('obs', t)])
                return gen
            run_jobs([ajob(gi) for gi in range(0 if SKIPA else NT // 512)], NA, stagger=40)
            fw.emit()

        if stop == 'a':
            break
        import os
        with contextlib.ExitStack() as st:
            NB0 = 4
            def per0(name, shape, dt=F32):
                return [alloc(st, "%s%d" % (name, i), shape, dt) for i in range(NB0)]
            pre = per0("b_pre", [128, 12, 130]); cv = per0("b_cv", [128, 12, 128]); cv2 = per0("b_cv2", [128, 12, 128])
            cv3 = per0("b_cv3", [128, 12, 128]); sq16 = per0("b_sq16", [128, 8, 128], BF16); rn = per0("b_rn", [128, 8, 128])
            vT16 = per0("b_vT16", [128, 4, 128], BF16); blob0 = per0("b_blob", [128, 4, 4, 128], BF16)
            cw = cwT.t[:, l, :, :]

            def b0job(t, sstart, slen):
                def gen(sl):
                    p = pre[sl]; c_ = cv[sl]; c2 = cv2[sl]; c3 = cv3[sl]; bl = blob0[sl]; sq_ = sq16[sl]; rn_ = rn[sl]; vt_ = vT16[sl]
                    ia = 2 * sl; ib = 2 * sl + 1
                    t0 = t * 128
                    lo = max(t0 - 1, sstart); hi = min(t0 + 129, sstart + slen)
                    if lo > t0 - 1:
                        fw.memset('pool', p[:, :, 0:1], 0.0, w=[p])
                    if hi < t0 + 129:
                        fw.memset('pool', p[:, :, 129:130], 0.0, w=[p])
                    for n3 in range(3):
                        fw.dma('sp', p[:, n3 * 4:(n3 + 1) * 4, lo - (t0 - 1):hi - (t0 - 1)], qkvs_v[:, n3 * 4:(n3 + 1) * 4, lo:hi],
                               r=DBs('qkvs', t - 1, t + 2), w=[p])
                    yield
                    fw.tt('dve', c_[:], p[:, :, 0:128], cw[:, :, 0:1].to_broadcast([128, 12, 128]), ALU.mult, r=[p, cwT], w=[c_])
                    fw.tt('pool', c2[:], p[:, :, 1:129], cw[:, :, 1:2].to_broadcast([128, 12, 128]), ALU.mult, r=[p, cwT], w=[c2])
                    fw.tt('pool', c3[:], p[:, :, 2:130], cw[:, :, 2:3].to_broadcast([128, 12, 128]), ALU.mult, r=[p, cwT], w=[c3])
                    yield
                    fw.tt('dve', c_[:], c_[:], c2[:], ALU.add, r=[c_, c2], w=[c_])
                    yield
                    fw.tt('dve', c_[:], c_[:], c3[:], ALU.add, r=[c_, c3], w=[c_])
                    yield
                    fw.act(c_[:], c_[:], AF.Silu, r=[c_], w=[c_])
                    yield
                    fw.act(sq_[:], c_[:, 0:8, :], AF.Square, r=[c_], w=[sq_])
                    fw.cp('act', vt_[:], c_[:, 8:12, :], r=[c_], w=[vt_])
                    yield
                    fw.mm(ps[ia][:, :], ones16, sq_[:, 0:4, :], r=[c16, sq_], w=[ps[ia]])
                    fw.mm(ps[ib][:, :], ones16, sq_[:, 4:8, :], r=[c16, sq_], w=[ps[ib]])
                    yield
                    fw.act(rn_[:, 0:4, :], ps3(ia), AF.Ln, bias=epsT[:, 0:1], scale=1.0, r=[ps[ia], epsT], w=[rn_])
                    fw.act(rn_[:, 4:8, :], ps3(ib), AF.Ln, bias=epsT[:, 0:1], scale=1.0, r=[ps[ib], epsT], w=[rn_])
                    yield
                    fw.act(rn_[:], rn_[:], AF.Exp, scale=-0.5, r=[rn_], w=[rn_])
                    yield
                    fw.tt('dve', bl[:, 0:2, :, :], c_[:, 0:8, :].rearrange("p (a h) c -> p a h c", a=2),
                          rn_[:].rearrange("p (a h) c -> p a h c", a=2), ALU.mult, r=[c_, rn_], w=[bl])
                    yield
                    pv = psbf(ia)
                    for h in range(4):
                        fw.tr(pv[:, h * 128:(h + 1) * 128], bl[:, 1, h, :], ident16, r=[bl, c16], w=[ps[ia]])
                        fw.tr(pv[:, 512 + h * 128:512 + (h + 1) * 128], vt_[:, h, :], ident16, r=[vt_, c16], w=[ps[ia]])
                    yield
                    fw.cp('dve', bl[:, 2:4, :, :], pv.rearrange("p (a h c) -> p a h c", a=2, h=4), r=[ps[ia]], w=[bl])
                    yield
                    fw.dma('act', blobs.ap()[t], bl[:].rearrange("p a h c -> p (a h c)"), r=[bl], w=[DB('blobs', t)])
                return gen
            jobs0 = []
            for (sstart, slen, cidx) in SEQS:
                for t in range(sstart // 128, (sstart + slen) // 128):
                    jobs0.append(b0job(t, sstart, slen))
            run_jobs(jobs0, NB0, stagger=3)
            fw.emit()

        if stop == 'b0':
            break
        with contextlib.ExitStack() as st:
            NJ = 4
            def per(name, shape, dt=F32):
                return [alloc(st, "%s%d" % (name, i), shape, dt) for i in range(NJ)]
            blobj = per("j_blob", [128, 4, 4, 128], BF16)
            bgt = per("j_bg", [128, 16]); Gs = per("j_Gs", [128, 16]); eG = per("j_eG", [128, 16]); sck = per("j_sck", [128, 4])
            gpad = per("j_gpad", [128, 128])
            for i in range(NJ):
                fw.memset('pool', gpad[i][:], 0.0, w=[gpad[i]])
            for j0 in range(0, 32, 4):
                fw.dma('pool', w1c.ap()[j0:j0 + 4], w_ff1b.ap()[l, j0:j0 + 4], w=[DB('w1c', j0 // 4)])
            for dc in range(8):
                fw.dma('pool', w2c.ap()[dc], w_ff2b.ap()[l, dc], w=[DB('w2c', dc)])
            Ugj = per("j_Ug", [128, 4, 128]); decj = per("j_dec", [128, 4, 128]); decTj = per("j_decT", [128, 4, 128])
            eGBj = per("j_eGB", [128, 4, 128]); nm32j = per("j_nm32", [128, 4, 128])
            Pj = per("j_P", [128, 4, 128], BF16); Qj = per("j_Q", [128, 4, 128], BF16); QMj = per("j_QM", [128, 5, 4, 128], BF16)
            TT16j = per("j_TT16", [128, 4, 128], BF16)
            T16j = per("j_T16", [128, 4, 128], BF16); X16j = per("j_X16", [128, 4, 128], BF16)
            vbj = per("j_vb", [128, 4, 128], BF16); kbgj = per("j_kbg", [128, 4, 128], BF16)
            kdj = per("j_kd", [128, 4, 128], BF16); wTj = per("j_wT", [128, 4, 128], BF16)
            qkj = per("j_qk", [128, 4, 128], BF16); qdj = per("j_qd", [128, 4, 128], BF16)
            u32j = per("j_u", [128, 4, 128]); vnj = per("j_vn", [128, 4, 128], BF16); oTj = per("j_oT", [128, 4, 128])
            NCH = 10
            S32c = [alloc(st, "j_S32_%d" % i, [128, 4, 128]) for i in range(NCH)]
            S16c = [alloc(st, "j_S16_%d" % i, [128, 4, 128], BF16) for i in range(NCH)]
            negones32 = c32[:, C_NEGONES, :]

            def job(slot, ci, t, d, cidx, is_last, seq_i, turn, my_idx):
                bl = blobj[slot]; b = bgt[slot]; G = Gs[slot]; e_ = eG[slot]; sk = sck[slot]; gp = gpad[slot]
                Ug = Ugj[slot]; dec = decj[slot]; decT = decTj[slot]; eGB = eGBj[slot]; nm32 = nm32j[slot]
                P = Pj[slot]; Q = Qj[slot]; QM = QMj[slot]; TT16 = TT16j[slot]; T16 = T16j[slot]; X16 = X16j[slot]
                vb16 = vbj[slot]; kbg16 = kbgj[slot]; kd = kdj[slot]; wT = wTj[slot]; qk = qkj[slot]; qd = qdj[slot]
                u = u32j[slot]; vn = vnj[slot]; oT = oTj[slot]
                pa = ps[2 * slot]; pb_ = ps[2 * slot + 1]; ia = 2 * slot; ib = 2 * slot + 1
                fw.dma('sp', bl[:].rearrange("p a h c -> p (a h c)"), blobs.ap()[t], r=[DB('blobs', t)], w=[bl])
                fw.dma('sp', b[:], bgs.ap()[t * 128:(t + 1) * 128, :], r=[DB('bgs', t)], w=[b])
                g4 = b[:, 8 + d * 4:12 + d * 4]; b4 = b[:, d * 4:d * 4 + 4]
                U = c32[:, C_UF + d, :]
                fw.cp('dve', gp[:, 0:4], g4, r=[b], w=[gp])
                fw.mm(pa[:, 0:128], U, gp[:], r=[c32, gp], w=[pa])
                fw.mm(pa[:, 128:256], c32[:, C_SAME, :], gp[:], r=[c32, gp], w=[pa])
                fw.mm(pa[:, 256:384], c32[:, C_CH0, :], gp[:], r=[c32, gp], w=[pa])
                fw.mm(pa[:, 384:512], c32[:, C_CH1, :], gp[:], r=[c32, gp], w=[pa])
                for h in range(4):
                    fw.act(Ug[:, h, :], U, AF.Identity, scale=g4[:, h:h + 1], r=[c32, b], w=[Ug])
                yield
                fw.cp('dve', G[:].rearrange("p (a b) -> p a b", a=4), ps3(ia)[:, :, 0:4], r=[pa], w=[G])
                fw.tt('dve', G[:, 4:8], G[:, 4:8], G[:, 0:4], ALU.subtract, r=[G], w=[G])
                fw.act(e_[:], G[:], AF.Exp, r=[G], w=[e_])
                for h in range(4):
                    fw.mm(pb_[:, h * 128:(h + 1) * 128], Ug[:, h, :], ones32, start=True, stop=False, r=[Ug, c32], w=[pb_])
                    fw.mm(pb_[:, h * 128:(h + 1) * 128], negones32, Ug[:, h, :], start=False, stop=True, r=[Ug, c32], w=[pb_])
                for h in range(4):
                    fw.mm(pa[:, h * 128:(h + 1) * 128], ones32, Ug[:, h, :], r=[Ug, c32], w=[pa])
                yield
                fw.tt('dve', sk[:], b4, e_[:, 0:4], ALU.mult, r=[b, e_], w=[sk])
                fw.act(dec[:], ps3(ib), AF.Relu, scale=-1.0, r=[pb_], w=[dec])
                fw.act(decT[:], ps3(ib), AF.Relu, scale=1.0, r=[pb_], w=[decT])
                fw.act(eGB[:], ps3(ia), AF.Exp, bias=LNSCALE, r=[pa], w=[eGB])
                for h in range(4):
                    fw.mm(pb_[:, h * 128:(h + 1) * 128], bl[:, 1, h, :], bl[:, 1, h, :], r=[bl], w=[pb_])
                yield
                fw.act(dec[:], dec[:], AF.Exp, scale=-1.0, r=[dec], w=[dec])
                fw.act(decT[:], decT[:], AF.Exp, scale=-1.0, r=[decT], w=[decT])
                ktok = bl[:, 2, :, :]; vtok = bl[:, 3, :, :]
                fw.tt('pool', vb16[:], vtok, b4.unsqueeze(2).to_broadcast([128, 4, 128]), ALU.mult, r=[bl, b], w=[vb16])
                fw.tt('pool', kbg16[:], ktok, sk[:].unsqueeze(2).to_broadcast([128, 4, 128]), ALU.mult, r=[bl, sk], w=[kbg16])
                fw.tt('pool', kd[:], ktok, e_[:, 4:8].unsqueeze(2).to_broadcast([128, 4, 128]), ALU.mult, r=[bl, e_], w=[kd])
                yield
                fw.tt('dve', nm32[:], ps3(ib), dec[:], ALU.mult, r=[pb_, dec], w=[nm32])
                fw.tt('pool', nm32[:], nm32[:], c32[:, C_NSTRF + d, :].unsqueeze(1).to_broadcast([128, 4, 128]), ALU.mult, r=[nm32, c32], w=[nm32])
                yield
                fw.tt('dve', P[:], nm32[:], b4.unsqueeze(2).to_broadcast([128, 4, 128]), ALU.mult, r=[nm32, b], w=[P])
                pqv = psbf(ia)
                for h in range(4):
                    fw.tr(pqv[:, h * 128:(h + 1) * 128], P[:, h, :], ident16, r=[P, c16], w=[pa])
                for h in range(4):
                    fw.mm(pb_[:, h * 128:(h + 1) * 128], bl[:, 1, h, :], bl[:, 0, h, :], r=[bl], w=[pb_])
                yield
                fw.cp('act', Q[:], pqv[:, 0:512].rearrange("p (h c) -> p h c", h=4), r=[pa], w=[Q])
                fw.tt('dve', nm32[:], ps3(ib), decT[:], ALU.mult, r=[pb_, decT], w=[nm32])
                fw.tt('dve', qd[:], eGB[:], bl[:, 0, :, :], ALU.mult, r=[eGB, bl], w=[qd])
                yield
                fw.tt('pool', qk[:], nm32[:], c32[:, C_INTF + d, :].unsqueeze(1).to_broadcast([128, 4, 128]), ALU.mult, r=[nm32, c32], w=[qk])
                for li in range(1, 6):
                    fw.tt('pool', QM[:, li - 1, :, :], Q[:], c16[:, C_MB + li, :].unsqueeze(1).to_broadcast([128, 4, 128]),
                          ALU.mult, r=[Q, c16], w=[QM])
                mb1 = c32[:, C_MB, :].unsqueeze(1).to_broadcast([128, 4, 128])
                idb = c32[:, C_ID, :].unsqueeze(1).to_broadcast([128, 4, 128])
                fw.tt('dve', eGB[:], P[:], mb1, ALU.mult, r=[P, c32], w=[eGB])
                fw.tt('pool', nm32[:], Q[:], mb1, ALU.mult, r=[Q, c32], w=[nm32])
                yield
                fw.tt('dve', T16[:], eGB[:], idb, ALU.add, r=[eGB, c32], w=[T16])
                fw.tt('pool', TT16[:], nm32[:], idb, ALU.add, r=[nm32, c32], w=[TT16])
                yield
                for li in range(1, 6):
                    for h in range(4):
                        fw.mm(pa[:, h * 128:(h + 1) * 128], QM[:, li - 1, h, :], T16[:, h, :], r=[QM, T16], w=[pa])
                    yield
                    fw.cp('act', X16[:], ps3(ia), r=[pa], w=[X16])
                    yield
                    if li < 5:
                        for h in range(4):
                            fw.mm(pb_[:, h * 128:(h + 1) * 128], TT16[:, h, :], X16[:, h, :], r=[TT16, X16], w=[pb_])
                    for h in range(4):
                        fw.mm(pa[:, h * 128:(h + 1) * 128], X16[:, h, :], TT16[:, h, :], r=[TT16, X16], w=[pa])
                    yield
                    if li < 5:
                        fw.tt('dve', T16[:], T16[:], ps3(ib), ALU.add, r=[T16, pb_], w=[T16])
                    fw.tt('dve', TT16[:], TT16[:], ps3(ia), ALU.add, r=[TT16, pa], w=[TT16])
                    yield
                for h in range(4):
                    fw.mm(pa[:, h * 128:(h + 1) * 128], kbg16[:, h, :], TT16[:, h, :], r=[kbg16, TT16], w=[pa])
                for h in range(4):
                    fw.mm(pb_[:, h * 128:(h + 1) * 128], TT16[:, h, :], vb16[:, h, :], r=[TT16, vb16], w=[pb_])
                yield
                fw.cp('act', wT[:], ps3(ia), r=[pa], w=[wT])
                fw.cp('act', u[:], ps3(ib), r=[pb_], w=[u])
                yield
                while turn[0] != my_idx:
                    yield
                Sf = S32c[ci]; Sb = S16c[ci]
                p6 = pa; p7 = pb_
                chs = (0, 1) if d == 0 else (1, 0)
                for ch in chs:
                    R = slice(ch * 64, (ch + 1) * 64)
                    for h in range(4):
                        fw.mm(p6[:, h * 128:(h + 1) * 128], wT[:, h, :], Sb[:, h, :], r=[wT, Sb], w=[p6])
                    yield
                    fw.tt('dve', vn[R, :, :], u[R, :, :], ps3(ia)[R, :, :], ALU.subtract, r=[u, p6], w=[vn])
                    yield
                    for h in range(4):
                        fw.mm(p7[:, h * 64:(h + 1) * 64], Sb[:, h, :], qd[:, h, R], start=True, stop=False, r=[Sb, qd], w=[p7])
                        fw.mm(p7[:, h * 64:(h + 1) * 64], vn[R, h, :], qk[R, h, R], start=False, stop=True, r=[vn, qk], w=[p7])
                    for h in range(4):
                        fw.mm(p6[:, h * 128:(h + 1) * 128], kd[R, h, :], vn[R, h, :], r=[kd, vn], w=[p6])
                    yield
                    fw.cp('act', oT[:, :, R], p7.t[:, 0:256].rearrange("p (h c) -> p h c", h=4), r=[p7], w=[oT])
                    for h in range(4):
                        fw.stt('dve', Sf[:, h, :], Sf[:, h, :], e_[:, 8 + ch * 4 + h:9 + ch * 4 + h], ps3(ia)[:, h, :],
                               ALU.mult, ALU.add, r=[Sf, e_, p6], w=[Sf])
                    yield
                    fw.cp('act', Sb[:], Sf[:], r=[Sf], w=[Sb])
                    yield
                turn[0] += 1
                fw.dma('act', (ofs if d == 0 else ofs2).ap()[t].rearrange("p (h c) -> p h c", h=4), oT[:], r=[oT],
                       w=[DB('ofs%d' % d, t)])
                if is_last and cidx == 0:
                    fw.dma('act', ns.ap()[seq_i, l, d].rearrange("h k v -> k h v"), Sf[:], r=[Sf])

            def run_chains(chain_defs):
                state = []
                for (ci, seq_i, sstart, slen, cidx, d) in chain_defs:
                    Sf = S32c[ci]; Sb = S16c[ci]
                    if cidx == 0:
                        fw.memset('pool', Sf[:], 0.0, w=[Sf])
                    else:
                        fw.dma('sp', Sf[:], s0.ap()[l, d].rearrange("h k v -> k h v"), w=[Sf])
                    fw.cp('act', Sb[:], Sf[:], r=[Sf], w=[Sb])
                    tiles = list(range(sstart // 128, (sstart + slen) // 128))
                    if d == 1:
                        tiles = tiles[::-1]
                    state.append(dict(ci=ci, seq_i=seq_i, cidx=cidx, d=d, tiles=tiles, nxt=0, turn=[0], inflight=0))
                active = [None] * NJ
                delay = [sl * 11 for sl in range(NJ)]
                while True:
                    progressed = False
                    for slot in range(NJ):
                        if delay[slot] > 0:
                            delay[slot] -= 1
                            progressed = True
                            continue
                        if active[slot] is None:
                            cands = [c for c in state if c['nxt'] < len(c['tiles']) and c['inflight'] < 2]
                            if cands:
                                c = min(cands, key=lambda c: (c['inflight'], -(len(c['tiles']) - c['nxt'])))
                                k = c['nxt']; c['nxt'] += 1; c['inflight'] += 1
                                g = job(slot, c['ci'], c['tiles'][k], c['d'], c['cidx'], k == len(c['tiles']) - 1, c['seq_i'], c['turn'], k)
                                active[slot] = (g, c)
                        if active[slot] is not None:
                            progressed = True
                            g, c = active[slot]
                            try:
                                next(g)
                            except StopIteration:
                                c['inflight'] -= 1
                                active[slot] = None
                    if not progressed and all(c['nxt'] >= len(c['tiles']) for c in state):
                        break

            pr = []
            for seq_i, (sstart, slen, cidx) in enumerate(SEQS[:4]):
                for d in range(2):
                    pr.append((seq_i * 2 + d, seq_i, sstart, slen, cidx, d))
            sstart, slen, cidx = SEQS[4]
            pr += [(8, 4, sstart, slen, cidx, 0), (9, 4, sstart, slen, cidx, 1)]
            run_chains(pr)
            fw.emit()

        if stop == 'b12':
            break
        with contextlib.ExitStack() as st:
            wo = alloc(st, "wo", [128, 8, 1024], BF16)
            fw.dma('pool', wo[:], w_out.ap()[l].rearrange("(kc p) n -> p kc n", p=128), w=[wo])
            NE = 4
            def per3(name, shape, dt=F32):
                return [alloc(st, "%s%d" % (name, i), shape, dt) for i in range(NE)]
            ofl = per3("e_ofl", [128, 4, 128]); ofl2 = per3("e_ofl2", [128, 4, 128]); osq = per3("e_osq", [128, 4, 128], BF16)
            ors = per3("e_ors", [128, 4, 128]); gTl = per3("e_gT", [128, 4, 128]); catA = per3("e_catA", [128, 4, 128], BF16)
            catB = per3("e_catB", [128, 4, 128], BF16); xres = per3("e_x", [128, 8, 128])

            def b3job(t):
                def gen(sl):
                    cidx = 0 if t < 8 else 1
                    of_ = ofl[sl]; of2 = ofl2[sl]; gt = gTl[sl]; cb = catB[sl]; xr = xres[sl]; ca = catA[sl]; oq = osq[sl]; orr = ors[sl]
                    ia = 2 * sl; ib = 2 * sl + 1
                    fw.dma('sp', of_[:], ofs.ap()[t].rearrange("p (h c) -> p h c", h=4), r=[DB('ofs0', t)], w=[of_])
                    fw.dma('sp', of2[:], ofs2.ap()[t].rearrange("p (h c) -> p h c", h=4), r=[DB('ofs1', t)], w=[of2])
                    fw.dma('sp', gt[:], gates_v[:, :, t * 128:(t + 1) * 128], r=[DB('gates', t)], w=[gt])
                    fw.dma('sp', cb[:], obs_v[:, :, t * 128:(t + 1) * 128], r=[DB('obs', t)], w=[cb])
                    fw.dma('sp', xr[:], xs_v[:, :, t * 128:(t + 1) * 128], r=[DB('xs', t)], w=[xr])
                    yield
                    fw.tt('pool', of_[:], of_[:], of2[:], ALU.add, r=[of_, of2], w=[of_])
                    yield
                    fw.act(oq[:], of_[:], AF.Square, r=[of_], w=[oq])
                    yield
                    fw.mm(ps[ia][:, :], ones16, oq[:], r=[c16, oq], w=[ps[ia]])
                    yield
                    fw.act(orr[:], ps3(ia), AF.Ln, bias=epsT[:, 0:1], scale=1.0 / 128, r=[ps[ia], epsT], w=[orr])
                    yield
                    fw.act(orr[:], orr[:], AF.Exp, scale=-0.5, r=[orr], w=[orr])
                    yield
                    fw.tt('dve', of_[:], of_[:], orr[:], ALU.mult, r=[of_, orr], w=[of_])
                    yield
                    fw.stt('dve', ca[:], of_[:], onT[:, l:l + 1], gt[:], ALU.mult, ALU.mult, r=[of_, onT, gt], w=[ca])
                    yield
                    for half in range(2):
                        pbn = ia if half == 0 else ib
                        pb2 = ps[pbn]
                        for jj in range(4):
                            dc = half * 4 + jj
                            for kc in range(8):
                                rhs = ca[:, kc, :] if kc < 4 else cb[:, kc - 4, :]
                                fw.mm(pb2[:, jj * 128:(jj + 1) * 128], wo[:, kc, dc * 128:(dc + 1) * 128], rhs,
                                      start=(kc == 0), stop=(kc == 7), r=[wo, ca, cb], w=[pb2])
                        yield
                    for half in range(2):
                        pbn = ia if half == 0 else ib
                        for jj in range(4):
                            dc = half * 4 + jj
                            fw.stt('dve', xr[:, dc, :], ps3(pbn)[:, jj, :], modT[:, 16 + dc, cidx:cidx + 1], xr[:, dc, :],
                                   ALU.mult, ALU.add, r=[ps[pbn], modT, xr], w=[xr])
                        yield
                    fw.dma('act', xs_v[:, :, t * 128:(t + 1) * 128], xr[:], r=[xr], w=[DB('xs', t)])
                return gen
            run_jobs([b3job(t) for t in range(NTL)], NE, stagger=3)
            fw.emit()

        if stop == 'b':
            break
        with contextlib.ExitStack() as st:
            x32 = alloc(st, "c_x", [128, 8, 1024])
            sqc = [alloc(st, "c_sq%d" % i, [128, 1024], BF16) for i in range(2)]
            rs = alloc(st, "c_rs", [128, 1024])
            tmp = [alloc(st, "c_tmp%d" % i, [128, 1024]) for i in range(2)]
            h16 = alloc(st, "c_h", [128, 8, 1024], BF16)
            a16 = alloc(st, "c_a", [128, 32, 1024], BF16)
            w1s = [alloc(st, "c_w1s%d" % i, [128, 2, 8, 128], BF16) for i in range(2)]
            w2s = [alloc(st, "c_w2s%d" % i, [128, 32, 128], BF16) for i in range(2)]
            rl = [alloc(st, "c_rl%d" % i, [128, 512]) for i in range(3)]
            ri = 0
            pi = 0
            for gi in range(NT // 1024):
                g0 = gi * 1024
                c = 0 if g0 < 1024 else 1
                tls = DBs('xs', gi * 8, gi * 8 + 8)
                fw.dma('sp', x32[:], xs_v[:, :, g0:g0 + 1024], r=tls, w=[x32])
                for kc in range(8):
                    s_ = sqc[kc % 2]
                    fw.act(s_[:], x32[:, kc, :], AF.Square, r=[x32], w=[s_])
                    for half in range(2):
                        fw.mm(ps[6 + half][:, :], ones16, s_[:, half * 512:(half + 1) * 512], start=(kc == 0), stop=(kc == 7),
                              r=[c16, s_], w=[ps[6 + half]])
                for half in range(2):
                    rstd(rs[:, half * 512:(half + 1) * 512], ps[6 + half][:, :], 1.0 / 1024, r=[ps[6 + half]], w=[rs])
                for kc in range(8):
                    tm = tmp[kc % 2]
                    fw.stt('dve', tm[:], x32[:, kc, :], w2T[:, kc, c:c + 1], rs[:], ALU.mult, ALU.mult, r=[x32, w2T, rs], w=[tm])
                    fw.act(h16[:, kc, :], tm[:], AF.Identity, bias=modT[:, 24 + kc, c:c + 1], r=[tm, modT], w=[h16])
                for j in range(32):
                    wv2 = w1s[(j // 2) % 2]
                    if j % 2 == 0:
                        fw.dma('sp', wv2[:].rearrange("p j k c -> p j (k c)"), w1c.ap()[j:j + 2].rearrange("j p n -> p j n"),
                               r=[DB('w1c', j // 4)], w=[wv2])
                    wv = T(wv2.t[:, j % 2, :, :]); wv.b = wv2.b
                    for half in range(2):
                        pb = ps[pi % 4]; pi += 1
                        for kc in range(8):
                            fw.mm(pb[:, :], wv[:, kc, :], h16[:, kc, half * 512:(half + 1) * 512], start=(kc == 0), stop=(kc == 7),
                                  r=[wv, h16], w=[pb])
                        r_ = rl[ri % 3]; ri += 1
                        fw.act(r_[:], pb[:, :], AF.Relu, r=[pb], w=[r_])
                        fw.tt('dve', a16[:, j, half * 512:(half + 1) * 512], r_[:], r_[:], ALU.mult, r=[r_], w=[a16])
                for dc in range(8):
                    wv = w2s[dc % 2]
                    fw.dma('sp', wv[:].rearrange("p j c -> p (j c)"), w2c.ap()[dc], r=[DB('w2c', dc)], w=[wv])
                    for half in range(2):
                        pb = ps[4 + (pi % 2)]; pi += 1
                        for j in range(32):
                            fw.mm(pb[:, :], wv[:, j, :], a16[:, j, half * 512:(half + 1) * 512], start=(j == 0), stop=(j == 31),
                                  r=[wv, a16], w=[pb])
                        xv = x32[:, dc, half * 512:(half + 1) * 512]
                        fw.stt('dve', xv, pb[:, :], modT[:, 40 + dc, c:c + 1], xv, ALU.mult, ALU.add, r=[pb, modT, x32], w=[x32])
                fw.dma('act', xs_v[:, :, g0:g0 + 1024], x32[:], r=[x32], w=tls)
            fw.emit()

    with contextlib.ExitStack() as st:
        xf = [alloc(st, "f_x%d" % i, [128, 8, 128]) for i in range(2)]
        sqf = alloc(st, "f_sq", [128, 8, 128], BF16)
        rsf = alloc(st, "f_rs", [128, 128])
        yo = [alloc(st, "f_y%d" % i, [128, 1024]) for i in range(2)]
        for t in range(0 if SKIPA else NTL):
            x = xf[t % 2]; yy = yo[t % 2]
            fw.dma('sp', x[:], xs_v[:, :, t * 128:(t + 1) * 128], r=[DB('xs', t)], w=[x])
            fw.act(sqf[:], x[:], AF.Square, r=[x], w=[sqf])
            pst = ps[4]
            for kc in range(8):
                fw.mm(pst[:, 0:128], ones16, sqf[:, kc, :], start=(kc == 0), stop=(kc == 7), r=[c16, sqf], w=[pst])
            rstd(rsf[:], pst[:, 0:128], 1.0 / 1024, r=[pst], w=[rsf])
            for kc in range(8):
                fw.stt('dve', x[:, kc, :], x[:, kc, :], nfT[:, kc:kc + 1], rsf[:], ALU.mult, ALU.mult, r=[x, nfT, rsf], w=[x])
            for half in range(2):
                pb = ps[(t % 2) * 2 + half]
                for j in range(4):
                    kc = half * 4 + j
                    fw.tr(pb[:, j * 128:(j + 1) * 128], x[:, kc, :], ident32, r=[x, c32], w=[pb])
                fw.cp('dve' if half == 0 else 'act', yy[:, half * 512:(half + 1) * 512], pb[:, :], r=[pb], w=[yy])
            fw.dma('act', y.ap()[t * 128:(t + 1) * 128, :], yy[:], r=[yy])
        fw.emit()
    stack0.close()
    return nc


def make_consts():
    idx = np.arange(128)
    same = (idx[:, None] // 64) == (idx[None, :] // 64)
    c = np.zeros((NCONST, 128, 128), np.float32)
    c[C_ID] = np.eye(128)
    c[C_ONES] = 1.0
    c[C_NEGONES] = -1.0
    c[C_UF] = same & (idx[:, None] <= idx[None, :])
    c[C_UB] = same & (idx[:, None] >= idx[None, :])
    c[C_SAME] = same
    c[C_CH0] = (idx[:, None] < 64) & np.ones((1, 128), bool)
    c[C_CH1] = (idx[:, None] >= 64) & np.ones((1, 128), bool)
    strict_f = same & (idx[None, :] < idx[:, None])
    strict_b = same & (idx[None, :] > idx[:, None])
    c[C_NSTRF] = -strict_f.astype(np.float32)
    c[C_NSTRB] = -strict_b.astype(np.float32)
    incl_f = same & (idx[None, :] <= idx[:, None])
    incl_b = same & (idx[None, :] >= idx[:, None])
    c[C_INTF] = incl_f.T.astype(np.float32) * (128.0 ** -0.5)
    c[C_INTB] = incl_b.T.astype(np.float32) * (128.0 ** -0.5)
    for i in range(6):
        b = 2 ** i
        c[C_MB + i] = ((idx[:, None] // (2 * b)) == (idx[None, :] // (2 * b))) & ((idx[:, None] // b) != (idx[None, :] // b))
    return c


_NC_CACHE = {}


def prep_inputs(inp, core):
    b = core // 4
    f = lambda a: np.ascontiguousarray(np.asarray(a, dtype=np.float32))
    xin = np.concatenate([f(inp['x_prompt'])[4 * core:4 * core + 4].reshape(1024, 1024), f(inp['x_sample'])[b]], axis=0)
    w1 = f(inp['w_ff1']).reshape(4, 8, 128, 32, 128).transpose(0, 3, 2, 1, 4).reshape(4, 32, 128, 1024)
    w2 = f(inp['w_ff2']).reshape(4, 32, 128, 8, 128).transpose(0, 3, 2, 1, 4).reshape(4, 8, 128, 4096)
    return {
        'xin': f(xin), 'cond': f(np.stack([f(inp['c_ctx']), f(inp['c'])[b]])), 's0': f(f(inp['state_delta'])[b]),
        'w_mod': f(inp['w_mod']), 'b_mod': f(inp['b_mod']), 'norm_mix': f(inp['norm_mix']), 'w_in': f(inp['w_in']),
        'conv_qkv': f(inp['conv_qkv']), 'a_log': f(inp['a_log']).reshape(4, 8), 'dt_bias': f(inp['dt_bias']).reshape(4, 8),
        'o_norm': f(inp['o_norm']), 'wsT': f(np.transpose(f(inp['w_spatial']), (0, 1, 3, 2))), 'b_sp': f(inp['b_spatial']),
        'w_out': f(inp['w_out']), 'norm_ffn': f(inp['norm_ffn']), 'w_ff1b': f(w1), 'w_ff2b': f(w2),
        'norm_final': f(inp['norm_final']), 'consts': make_consts(),
    }


def kernel(**inputs):
    if 'nc' not in _NC_CACHE:
        _NC_CACHE['nc'] = build()
    nc = _NC_CACHE['nc']
    shared = None
    in_maps = []
    for core in range(8):
        m = prep_inputs(inputs, core) if shared is None else None
        if shared is None:
            shared = m
        else:
            m = dict(shared)
            b = core // 4
            f = lambda a: np.ascontiguousarray(np.asarray(a, dtype=np.float32))
            m['xin'] = np.concatenate([f(inputs['x_prompt'])[4 * core:4 * core + 4].reshape(1024, 1024), f(inputs['x_sample'])[b]], axis=0)
            m['cond'] = f(np.stack([f(inputs['c_ctx']), f(inputs['c'])[b]]))
            m['s0'] = f(f(inputs['state_delta'])[b])
        in_maps.append(m)
    res = run_bass_kernel_spmd(nc, in_maps, core_ids=list(range(8)))
    outs = res.results
    y_prompt = np.concatenate([np.asarray(outs[c]['y'])[:1024].reshape(4, 256, 1024) for c in range(8)], axis=0)
    y_sample = np.stack([np.asarray(outs[0]['y'])[1024:], np.asarray(outs[4]['y'])[1024:]], axis=0)
    nsd = np.concatenate([np.asarray(outs[c]['ns']) for c in range(8)], axis=0)
    return (y_prompt.astype(np.float32), y_sample.astype(np.float32), nsd.astype(np.float32))
```
